# Optimizing a Trainium2 kernel written in Bass

```python
import math
import jax
import jax.numpy as jnp
from jax import lax
import numpy as np


D_MODEL = 2048
BATCH = 4
SEQ = 4096
DEPTH = 4

MEM_LEN = 256
NORM_EPS = 1e-5
D_FF = 5632

D_MIX = D_MODEL
D_SSD = D_MIX // 2
D_SB = D_MIX // 4
D_RW = D_MIX - D_SSD - D_SB

SSD_HEAD_DIM = 64
SSD_HEADS = D_SSD // SSD_HEAD_DIM
SSD_GROUPS = 2
SSD_STATE = 128
SSD_CONV = 4
SSD_CHUNK = 128
SSD_BC = SSD_GROUPS * SSD_STATE
SSD_CONV_DIM = D_SSD + 2 * SSD_BC
SSD_IN = D_SSD + SSD_CONV_DIM + SSD_HEADS

SB_HEAD_DIM = 64
SB_HEADS = D_SB // SB_HEAD_DIM
SB_BLOCK = 128
SB_IN = 3 * D_SB

RW_HEAD_DIM = 64
RW_HEADS = D_RW // RW_HEAD_DIM
RW_DECAY_LORA = 64
RW_A_LORA = 64
RW_V_LORA = 32
RW_GATE_LORA = 128
RW_GN_EPS = 64e-5
RW_IN = 3 * D_RW + RW_DECAY_LORA + RW_A_LORA + RW_GATE_LORA

N_IN = SSD_IN + SB_IN + RW_IN

CROSS_HEADS = 4
CROSS_HEAD_DIM = 128
CROSS_DIM = CROSS_HEADS * CROSS_HEAD_DIM

kernel_name = 'hybrid_ssd_stickbreak_rwkv7_macaron'


def rmsnorm(x, g):
    x32 = x.astype(jnp.float32)
    y = x32 * lax.rsqrt(jnp.mean(x32 * x32, axis=-1, keepdims=True) + NORM_EPS)
    return (y * g.astype(jnp.float32)).astype(x.dtype)


def grouped_rmsnorm(x, g, n_groups):
    lead = x.shape[:-1]
    d = x.shape[-1]
    xg = x.astype(jnp.float32).reshape(lead + (n_groups, d // n_groups))
    y = xg * lax.rsqrt(jnp.mean(xg * xg, axis=-1, keepdims=True) + NORM_EPS)
    return (y.reshape(lead + (d,)) * g.astype(jnp.float32)).astype(x.dtype)


def swiglu(h, w_gate, w_up, w_down):
    return (jax.nn.silu(h @ w_gate) * (h @ w_up)) @ w_down


def causal_depthwise_conv(u, w, b):
    n_ch = w.shape[1]
    y = lax.conv_general_dilated(u, w[:, None, :].astype(u.dtype), window_strides=(1,),
                                 padding=[(w.shape[0] - 1, 0)],
                                 dimension_numbers=('NWC', 'WIO', 'NWC'),
                                 feature_group_count=n_ch)
    return y + b.astype(u.dtype)


def segsum_exp(a):
    n = a.shape[-1]
    cs = jnp.cumsum(a, axis=-1)
    diff = cs[..., :, None] - cs[..., None, :]
    mask = jnp.tril(jnp.ones((n, n), dtype=bool))
    return jnp.exp(jnp.where(mask, diff, -jnp.inf))


def ssd_chunked(x, dt, a, bm, cm):
    Bsz, T, H, P = x.shape
    G, N = bm.shape[-2], bm.shape[-1]
    E = H // G
    L = SSD_CHUNK
    nc = T // L
    xd = (x * dt[..., None]).reshape(Bsz, nc, L, G, E, P)
    da = jnp.transpose((dt * a).reshape(Bsz, nc, L, G, E), (0, 3, 4, 1, 2))
    bc = bm.reshape(Bsz, nc, L, G, N)
    cc = cm.reshape(Bsz, nc, L, G, N)
    a_cs = jnp.cumsum(da, axis=-1)
    cb = jnp.einsum('bclgn,bcsgn->bgcls', cc, bc)
    w_in = cb[:, :, None] * segsum_exp(da)
    y_diag = jnp.einsum('bgecls,bcsgep->bclgep', w_in, xd)
    decay_to_end = jnp.transpose(jnp.exp(a_cs[..., -1:] - a_cs), (0, 3, 4, 1, 2))
    states = jnp.einsum('bclgn,bclgep->bcgepn', bc, xd * decay_to_end[..., None])
    chunk_a = jnp.pad(a_cs[..., -1], ((0, 0), (0, 0), (0, 0), (1, 0)))
    decay_chunk = segsum_exp(chunk_a)
    states = jnp.concatenate([jnp.zeros_like(states[:, :1]), states], axis=1)
    prev = jnp.einsum('bgezc,bcgepn->bzgepn', decay_chunk[..., :-1, :], states)
    decay_from_start = jnp.transpose(jnp.exp(a_cs), (0, 3, 4, 1, 2))
    y_off = jnp.einsum('bclgn,bcgepn->bclgep', cc, prev) * decay_from_start[..., None]
    return (y_diag + y_off).reshape(Bsz, T, H, P)


def ssd_mixer(u, conv_w, conv_b, dt_bias, a_log, d_skip, norm_g):
    Bsz, T, _ = u.shape
    f32 = jnp.float32
    z = u[..., :D_SSD]
    xbc = u[..., D_SSD:D_SSD + SSD_CONV_DIM]
    dt_raw = u[..., D_SSD + SSD_CONV_DIM:]
    xbc = jax.nn.silu(causal_depthwise_conv(xbc, conv_w, conv_b))
    xs = xbc[..., :D_SSD].reshape(Bsz, T, SSD_HEADS, SSD_HEAD_DIM).astype(f32)
    bm = xbc[..., D_SSD:D_SSD + SSD_BC].reshape(Bsz, T, SSD_GROUPS, SSD_STATE).astype(f32)
    cm = xbc[..., D_SSD + SSD_BC:].reshape(Bsz, T, SSD_GROUPS, SSD_STATE).astype(f32)
    dt = jax.nn.softplus(dt_raw.astype(f32) + dt_bias.astype(f32))
    a = -jnp.exp(a_log.astype(f32))
    y = ssd_chunked(xs, dt, a, bm, cm) + xs * d_skip.astype(f32)[:, None]
    y = y.reshape(Bsz, T, D_SSD).astype(u.dtype)
    return grouped_rmsnorm(y * jax.nn.silu(z), norm_g, SSD_GROUPS)


def stick_breaking_mixer(u, out_g):
    Bsz, T, _ = u.shape
    q = u[..., :D_SB].reshape(Bsz, T, SB_HEADS, SB_HEAD_DIM)
    k = u[..., D_SB:2 * D_SB].reshape(Bsz, T, SB_HEADS, SB_HEAD_DIM)
    v = u[..., 2 * D_SB:].reshape(Bsz, T, SB_HEADS, SB_HEAD_DIM)
    nb = T // SB_BLOCK
    qb = jnp.transpose(q.reshape(Bsz, nb, SB_BLOCK, SB_HEADS, SB_HEAD_DIM), (1, 0, 2, 3, 4))
    kpos = jnp.arange(T)
    scale = SB_HEAD_DIM ** -0.5

    def one_block(args):
        q_blk, blk = args
        tpos = blk * SB_BLOCK + jnp.arange(SB_BLOCK)
        z = jnp.einsum('bqhd,bkhd->bhqk', q_blk, k).astype(jnp.float32) * scale
        strict = kpos[None, :] < tpos[:, None]
        log_1m = jnp.where(strict, jax.nn.log_sigmoid(-z), 0.0)
        after = lax.cumsum(log_1m, axis=3, reverse=True) - log_1m
        w = jnp.where(strict, jnp.exp(jax.nn.log_sigmoid(z) + after), 0.0)
        return jnp.einsum('bhqk,bkhd->bqhd', w.astype(v.dtype), v)

    o = lax.map(one_block, (qb, jnp.arange(nb)))
    o = jnp.transpose(o, (1, 0, 2, 3, 4)).reshape(Bsz, T, D_SB)
    return grouped_rmsnorm(o, out_g, SB_HEADS)


def token_shift(f):
    return jnp.pad(f, ((0, 0), (1, 0), (0, 0)))[:, :-1]


def rwkv7_scan(r, decay, k, v, a_vec, b_vec):
    Bsz, T, H, N = r.shape
    seq = (jnp.moveaxis(r, 1, 0), jnp.moveaxis(decay, 1, 0), jnp.moveaxis(k, 1, 0),
           jnp.moveaxis(v, 1, 0), jnp.moveaxis(a_vec, 1, 0), jnp.moveaxis(b_vec, 1, 0))

    def step(S, inp):
        r_t, w_t, k_t, v_t, a_t, b_t = inp
        sa = jnp.einsum('bhvk,bhk->bhv', S, a_t)
        S = S * w_t[:, :, None, :] + sa[..., None] * b_t[:, :, None, :] + v_t[..., None] * k_t[:, :, None, :]
        return S, jnp.einsum('bhvk,bhk->bhv', S, r_t)

    S0 = jnp.zeros((Bsz, H, N, N), jnp.float32)
    _, y = lax.scan(step, S0, seq)
    return jnp.moveaxis(y, 0, 1)


def rwkv7_mixer(u, v_first, v_mix, mu, w0, w_up, a0, a_up, g_up, k_k, k_a, r_k, ln_w, ln_b):
    f32 = jnp.float32
    Bsz, T, _ = u.shape
    u = u.astype(f32)
    u = u + (token_shift(u) - u) * mu
    o1, o2, o3 = D_RW, 2 * D_RW, 3 * D_RW
    o4 = o3 + RW_DECAY_LORA
    o5 = o4 + RW_A_LORA
    r, k, v = u[..., :o1], u[..., o1:o2], u[..., o2:o3]
    wd, ad, gd = u[..., o3:o4], u[..., o4:o5], u[..., o5:]
    w_log = -jax.nn.softplus(-(w0 + jnp.tanh(wd) @ w_up)) - 0.5
    decay = jnp.exp(-jnp.exp(w_log))
    if v_mix is None:
        v_first = v
    else:
        v0, v_down, v_up = v_mix
        v = v + (v_first - v) * jax.nn.sigmoid(v0 + (v @ v_down) @ v_up)
    a = jax.nn.sigmoid(a0 + ad @ a_up)
    g = jax.nn.sigmoid(gd) @ g_up
    hs = (Bsz, T, RW_HEADS, RW_HEAD_DIM)
    kk = (k * k_k).reshape(hs)
    kk = kk * lax.rsqrt(jnp.maximum(jnp.sum(kk * kk, axis=-1, keepdims=True), 1e-24))
    k = k * (1.0 + (a - 1.0) * k_a)
    r_h, k_h, v_h, a_h = r.reshape(hs), k.reshape(hs), v.reshape(hs), a.reshape(hs)
    y = rwkv7_scan(r_h, decay.reshape(hs), k_h, v_h, -kk, kk * a_h)
    mean = jnp.mean(y, axis=-1, keepdims=True)
    var = jnp.mean(jnp.square(y - mean), axis=-1, keepdims=True)
    y = ((y - mean) * lax.rsqrt(var + RW_GN_EPS)).reshape(Bsz, T, D_RW) * ln_w + ln_b
    bonus = jnp.sum(r_h * k_h * r_k, axis=-1, keepdims=True) * v_h
    y = (y + bonus.reshape(Bsz, T, D_RW)) * g
    return y, v_first


def cross_attention(h, memn, wq, wk, wv, wo):
    Bsz, T, _ = h.shape
    M = memn.shape[1]
    q = (h @ wq).reshape(Bsz, T, CROSS_HEADS, CROSS_HEAD_DIM)
    k = (memn @ wk).reshape(Bsz, M, CROSS_HEADS, CROSS_HEAD_DIM)
    v = (memn @ wv).reshape(Bsz, M, CROSS_HEADS, CROSS_HEAD_DIM)
    s = jnp.einsum('bthd,bmhd->bhtm', q, k).astype(jnp.float32) * (CROSS_HEAD_DIM ** -0.5)
    p = jax.nn.softmax(s, axis=-1).astype(v.dtype)
    o = jnp.einsum('bhtm,bmhd->bthd', p, v).reshape(Bsz, T, CROSS_DIM)
    return o @ wo


def setup_inputs(seed: int = 0) -> dict:
    key = jax.random.key(seed)
    ks = iter(jax.random.split(key, 64))
    f32 = jnp.float32
    L = DEPTH

    def nrm(shape, scale):
        return jax.random.normal(next(ks), shape, f32) * scale

    def gain(shape):
        return 1.0 + nrm(shape, 0.02)

    d_in = D_MODEL ** -0.5
    inp = {}
    inp['x'] = nrm((BATCH, SEQ, D_MODEL), 1.0)
    inp['mem'] = nrm((BATCH, MEM_LEN, D_MODEL), 1.0)
    inp['ffn1_norm'] = gain((L, D_MODEL))
    inp['ffn1_w_gate'] = nrm((L, D_MODEL, D_FF), d_in)
    inp['ffn1_w_up'] = nrm((L, D_MODEL, D_FF), d_in)
    inp['ffn1_w_down'] = nrm((L, D_FF, D_MODEL), D_FF ** -0.5)
    inp['mix_norm'] = gain((L, D_MODEL))
    inp['w_in'] = nrm((L, D_MODEL, N_IN), d_in)
    inp['w_out'] = nrm((L, D_MIX, D_MODEL), D_MIX ** -0.5)
    inp['ssd_conv_w'] = nrm((L, SSD_CONV, SSD_CONV_DIM), SSD_CONV ** -0.5)
    inp['ssd_conv_b'] = nrm((L, SSD_CONV_DIM), 0.02)
    dt0 = jnp.exp(jax.random.uniform(next(ks), (L, SSD_HEADS), f32, math.log(1e-3), math.log(1e-1)))
    inp['ssd_dt_bias'] = dt0 + jnp.log(-jnp.expm1(-dt0))
    inp['ssd_a_log'] = jnp.log(jax.random.uniform(next(ks), (L, SSD_HEADS), f32, 1.0, 16.0))
    inp['ssd_d'] = 1.0 + nrm((L, SSD_HEADS), 0.1)
    inp['ssd_norm'] = gain((L, D_SSD))
    inp['sb_norm'] = gain((L, D_SB))
    inp['rw_mu'] = jax.random.uniform(next(ks), (L, RW_IN), f32)
    inp['rw_w0'] = jnp.linspace(-6.0, -1.0, D_RW, dtype=f32)[None, :] + nrm((L, D_RW), 0.1)
    inp['rw_w_up'] = nrm((L, RW_DECAY_LORA, D_RW), 0.5 * RW_DECAY_LORA ** -0.5)
    inp['rw_a0'] = nrm((L, D_RW), 0.1)
    inp['rw_a_up'] = nrm((L, RW_A_LORA, D_RW), RW_A_LORA ** -0.5)
    inp['rw_g_up'] = nrm((L, RW_GATE_LORA, D_RW), RW_GATE_LORA ** -0.5)
    inp['rw_v0'] = 1.0 + nrm((L - 1, D_RW), 0.1)
    inp['rw_v_down'] = nrm((L - 1, D_RW, RW_V_LORA), D_RW ** -0.5)
    inp['rw_v_up'] = nrm((L - 1, RW_V_LORA, D_RW), RW_V_LORA ** -0.5)
    inp['rw_k_k'] = 0.85 + nrm((L, D_RW), 0.05)
    inp['rw_k_a'] = 1.0 + nrm((L, D_RW), 0.05)
    inp['rw_r_k'] = nrm((L, RW_HEADS, RW_HEAD_DIM), 0.1)
    inp['rw_ln_w'] = gain((L, D_RW))
    inp['rw_ln_b'] = nrm((L, D_RW), 0.02)
    inp['cross_norm'] = gain((L, D_MODEL))
    inp['cross_wq'] = nrm((L, D_MODEL, CROSS_DIM), d_in)
    inp['cross_wk'] = nrm((L, D_MODEL, CROSS_DIM), d_in)
    inp['cross_wv'] = nrm((L, D_MODEL, CROSS_DIM), d_in)
    inp['cross_wo'] = nrm((L, CROSS_DIM, D_MODEL), CROSS_DIM ** -0.5)
    inp['ffn2_norm'] = gain((L, D_MODEL))
    inp['ffn2_w_gate'] = nrm((L, D_MODEL, D_FF), d_in)
    inp['ffn2_w_up'] = nrm((L, D_MODEL, D_FF), d_in)
    inp['ffn2_w_down'] = nrm((L, D_FF, D_MODEL), D_FF ** -0.5)
    inp['mem_norm'] = gain((D_MODEL,))
    inp['final_norm'] = gain((D_MODEL,))
    return inp


def reference(x, mem, ffn1_norm, ffn1_w_gate, ffn1_w_up, ffn1_w_down, mix_norm, w_in, w_out,
              ssd_conv_w, ssd_conv_b, ssd_dt_bias, ssd_a_log, ssd_d, ssd_norm, sb_norm,
              rw_mu, rw_w0, rw_w_up, rw_a0, rw_a_up, rw_g_up, rw_v0, rw_v_down, rw_v_up,
              rw_k_k, rw_k_a, rw_r_k, rw_ln_w, rw_ln_b,
              cross_norm, cross_wq, cross_wk, cross_wv, cross_wo,
              ffn2_norm, ffn2_w_gate, ffn2_w_up, ffn2_w_down, mem_norm, final_norm):
    memn = rmsnorm(mem, mem_norm)
    v_first = None
    for l in range(DEPTH):
        x = x + 0.5 * swiglu(rmsnorm(x, ffn1_norm[l]), ffn1_w_gate[l], ffn1_w_up[l], ffn1_w_down[l])
        u = rmsnorm(x, mix_norm[l]) @ w_in[l]
        u_ssd = u[..., :SSD_IN]
        u_sb = u[..., SSD_IN:SSD_IN + SB_IN]
        u_rw = u[..., SSD_IN + SB_IN:]
        y_ssd = ssd_mixer(u_ssd, ssd_conv_w[l], ssd_conv_b[l], ssd_dt_bias[l], ssd_a_log[l],
                          ssd_d[l], ssd_norm[l])
        y_sb = stick_breaking_mixer(u_sb, sb_norm[l])
        v_mix = None if l == 0 else (rw_v0[l - 1], rw_v_down[l - 1], rw_v_up[l - 1])
        y_rw, v_first = rwkv7_mixer(u_rw, v_first, v_mix, rw_mu[l], rw_w0[l], rw_w_up[l], rw_a0[l],
                                    rw_a_up[l], rw_g_up[l], rw_k_k[l], rw_k_a[l], rw_r_k[l],
                                    rw_ln_w[l], rw_ln_b[l])
        y_mix = jnp.concatenate([y_ssd.astype(x.dtype), y_sb.astype(x.dtype), y_rw.astype(x.dtype)], axis=-1)
        x = x + y_mix @ w_out[l]
        x = x + cross_attention(rmsnorm(x, cross_norm[l]), memn, cross_wq[l], cross_wk[l],
                                cross_wv[l], cross_wo[l])
        x = x + 0.5 * swiglu(rmsnorm(x, ffn2_norm[l]), ffn2_w_gate[l], ffn2_w_up[l], ffn2_w_down[l])
    return rmsnorm(x, final_norm)
```

```python
import contextlib
import bisect
import numpy as np
import concourse.bass as bass
import concourse.mybir as mybir
from concourse.bass_utils import run_bass_kernel_spmd

AF = mybir.ActivationFunctionType
ALU = mybir.AluOpType
AX = mybir.AxisListType
F32 = mybir.dt.float32
BF16 = mybir.dt.bfloat16

D = 2048
KC = 16
DFF = 5632
NFC = 44
NIN = 5904
MEM = 256
EPS = 1e-5
DEPTH_FULL = 4
T_FULL = 4096
O_Z, O_X, O_B, O_C, O_DT = 0, 1024, 2048, 2304, 2560
O_SB = 2576
O_RW = 4112

WEIGHT_NAMES = [
    "ffn1_norm", "ffn1_w_gate", "ffn1_w_up", "ffn1_w_down", "mix_norm", "w_in", "w_out",
    "ssd_conv_w", "ssd_conv_b", "ssd_dt_bias", "ssd_a_log", "ssd_d", "ssd_norm", "sb_norm",
    "rw_mu", "rw_w0", "rw_w_up", "rw_a0", "rw_a_up", "rw_g_up", "rw_v0", "rw_v_down", "rw_v_up",
    "rw_k_k", "rw_k_a", "rw_r_k", "rw_ln_w", "rw_ln_b",
    "cross_norm", "cross_wq", "cross_wk", "cross_wv", "cross_wo",
    "ffn2_norm", "ffn2_w_gate", "ffn2_w_up", "ffn2_w_down", "mem_norm", "final_norm"]


class Buf:
    __slots__ = ("w", "r")

    def __init__(self):
        self.w = None
        self.r = {}


class Sched:
    ENGS = ("pe", "act", "dve", "pool", "sp")
    NDMA = 8

    def __init__(self, nc, stack):
        self.nc = nc
        self.q = {e: [] for e in self.ENGS}
        self.n = {e: 0 for e in self.ENGS}
        self.need = {e: set() for e in self.ENGS}
        self.sems = {}
        self.cnt = {}
        for e in self.ENGS:
            self.sems[e] = stack.enter_context(nc.semaphore("s_" + e))
        self.dma_pool = {}
        self.dma_rr = {}
        for e in ("sp", "pool", "act"):
            keys = []
            for i in range(self.NDMA):
                k = "d_%s%d" % (e, i)
                self.sems[k] = stack.enter_context(nc.semaphore(k))
                self.cnt[k] = 0
                keys.append(k)
            self.dma_pool[e] = keys
            self.dma_rr[e] = 0
        self.seen = {e: {} for e in self.ENGS}
        self.out_tokens = []

    def _waits(self, eng, reads, writes):
        need = {}

        def add(tok):
            if tok is None:
                return
            k, v = tok
            if k == "pe" and eng == "pe":
                return
            if need.get(k, 0) < v:
                need[k] = v
        for b in reads:
            add(b.w)
        for b in writes:
            add(b.w)
            for t in b.r.items():
                add(t)
        out = []
        seen = self.seen[eng]
        for k, v in need.items():
            if seen.get(k, 0) < v:
                seen[k] = v
                out.append((k, v))
                if k in self.need:
                    self.need[k].add(v)
        return out

    def _mark(self, tok, reads, writes):
        k, v = tok
        for b in reads:
            if b.r.get(k, 0) < v:
                b.r[k] = v
        for b in writes:
            b.w = tok
            b.r = {}

    def op(self, eng, fn, reads=(), writes=()):
        waits = self._waits(eng, reads, writes)
        self.n[eng] += 1
        tok = (eng, self.n[eng])
        self.q[eng].append((waits, fn, None, self.n[eng]))
        self._mark(tok, reads, writes)
        return tok

    def dma(self, eng, fn, reads=(), writes=(), is_output=False):
        k = self.dma_pool[eng][self.dma_rr[eng] % self.NDMA]
        self.dma_rr[eng] += 1
        waits = self._waits(eng, reads, writes)
        if self.cnt[k] > 0 and self.seen[eng].get(k, 0) < self.cnt[k]:
            self.seen[eng][k] = self.cnt[k]
            waits.append((k, self.cnt[k]))
        self.cnt[k] += 16
        tok = (k, self.cnt[k])
        self.n[eng] += 1
        self.q[eng].append((waits, fn, k, self.n[eng]))
        self._mark(tok, reads, writes)
        if is_output:
            self.out_tokens.append(tok)
        return tok

    def barrier(self):
        last = []
        for e in self.ENGS:
            if self.n[e] > 0 and self.q[e] and self.q[e][-1][2] is None:
                last.append((e, self.n[e]))
            elif self.n[e] > 0:
                for ent in reversed(self.q[e]):
                    if ent[2] is None:
                        last.append((e, ent[3]))
                        break
        for k, v in self.cnt.items():
            if v > 0:
                last.append((k, v))
        for e in self.ENGS:
            waits = []
            for k, v in last:
                if k == e:
                    continue
                if self.seen[e].get(k, 0) < v:
                    self.seen[e][k] = v
                    waits.append((k, v))
                    if k in self.need:
                        self.need[k].add(v)
            self.n[e] += 1
            self.q[e].append((waits, lambda en: en.nop(), None, self.n[e]))

    def finish(self):
        fin = {}
        for k, v in self.out_tokens:
            fin[k] = max(fin.get(k, 0), v)
        sems = self.sems
        q = self.q
        nc = self.nc
        ranks = {e: sorted(self.need[e]) for e in self.ENGS}

        def val(k, v):
            if k in ranks:
                return bisect.bisect_right(ranks[k], v)
            return v

        def replay(engname, engobj):
            needset = self.need[engname]
            for waits, fn, dk, idx in q[engname]:
                for wk, wv in waits:
                    engobj.wait_ge(sems[wk], val(wk, wv))
                ins = fn(engobj)
                if dk is not None:
                    ins.then_inc(sems[dk], 16)
                elif idx in needset:
                    ins.then_inc(sems[engname], 1)

        with nc.Block() as block:
            @block.tensor
            def _(e):
                replay("pe", e)

            @block.scalar
            def _(e):
                replay("act", e)

            @block.vector
            def _(e):
                replay("dve", e)

            @block.gpsimd
            def _(e):
                replay("pool", e)

            @block.sync
            def _(e):
                replay("sp", e)
                for k, v in fin.items():
                    e.wait_ge(sems[k], v)


class TL:
    __slots__ = ("t", "b")

    def __init__(self, t):
        self.t = t
        self.b = Buf()

    def __getitem__(self, k):
        return self.t[k]


class Ring:
    def __init__(self, tiles):
        self.tiles = tiles
        self.i = 0

    def next(self):
        t = self.tiles[self.i % len(self.tiles)]
        self.i += 1
        return t


class Ctx:
    pass


def build_program(T=T_FULL, depth=DEPTH_FULL, dbg=(), stages=None, LW=4, rwp=99, dbg_in=(), dumps=()):
    nc = bass.Bass("TRN2", target_bir_lowering=False)
    C = Ctx()
    C.nc = nc
    C.T = T
    C.TB = min(1024, T)
    C.NTB = T // C.TB
    C.depth = depth
    C.stages = stages
    C.rwp = rwp
    C.dumps = dumps
    C.dumped = set()
    shapes = {
        "x": [T, D], "mem": [MEM, D],
        "ffn1_norm": [4, D], "ffn1_w_gate": [4, D, DFF], "ffn1_w_up": [4, D, DFF], "ffn1_w_down": [4, DFF, D],
        "mix_norm": [4, D], "w_in": [4, D, NIN], "w_out": [4, D, D],
        "ssd_conv_w": [4, 4, 1536], "ssd_conv_b": [4, 1536], "ssd_dt_bias": [4, 16], "ssd_a_log": [4, 16],
        "ssd_d": [4, 16], "ssd_norm": [4, 1024], "sb_norm": [4, 512],
        "rw_mu": [4, 1792], "rw_w0": [4, 512], "rw_w_up": [4, 64, 512], "rw_a0": [4, 512], "rw_a_up": [4, 64, 512],
        "rw_g_up": [4, 128, 512], "rw_v0": [3, 512], "rw_v_down": [3, 512, 32], "rw_v_up": [3, 32, 512],
        "rw_k_k": [4, 512], "rw_k_a": [4, 512], "rw_r_k": [4, 8, 64], "rw_ln_w": [4, 512], "rw_ln_b": [4, 512],
        "cross_norm": [4, D], "cross_wq": [4, D, 512], "cross_wk": [4, D, 512], "cross_wv": [4, D, 512],
        "cross_wo": [4, 512, D],
        "ffn2_norm": [4, D], "ffn2_w_gate": [4, D, DFF], "ffn2_w_up": [4, D, DFF], "ffn2_w_down": [4, DFF, D],
        "mem_norm": [D], "final_norm": [D],
    }
    I = {}
    for k, shp in shapes.items():
        if len(shp) >= 2 and shp[0] in (3, 4) and k not in ("ssd_conv_w",):
            shp = [LW if shp[0] == 4 else max(LW - 1, 1)] + shp[1:]
        elif k == "ssd_conv_w":
            shp = [LW] + shp[1:]
        I[k] = nc.dram_tensor(k, shp, F32, kind="ExternalInput").ap()
    C.I = I
    C.out = nc.dram_tensor("out", [T, D], F32, kind="ExternalOutput").ap()

    def scratch(name, shape, dt):
        kind = "ExternalOutput" if name in dbg else ("ExternalInput" if name in dbg_in else "Internal")
        return nc.dram_tensor(name, shape, dt, kind=kind).ap()
    C.X = scratch("X", [T, D], F32)
    C.UT = scratch("UT", [NIN, T], F32)
    C.SZ = scratch("SZ", [T, 1024], BF16)
    C.VSB = scratch("VSB", [T, 512], BF16)
    C.YMT = scratch("YMT", [D, T], BF16)
    C.VF = scratch("VF", [512, T], F32)
    C.bX = [Buf() for _ in range(C.NTB)]
    C.bUT = Buf()
    C.bSZ = Buf()
    C.bVSB = Buf()
    C.bYMT = Buf()
    C.bVF = Buf()

    with contextlib.ExitStack() as st:
        S = Sched(nc, st)
        C.S = S
        C.st = st
        sbp = lambda name, shape, dt: TL(st.enter_context(nc.sbuf_tensor(name, shape, dt)))
        C.ident = sbp("ident", [128, 128], BF16)
        C.identf = sbp("identf", [128, 128], F32)
        C.ones_bf = sbp("ones_bf", [128, 128], BF16)
        C.memnT = sbp("memnT", [128, KC, MEM], BF16)
        C.psum = Ring([TL(st.enter_context(nc.psum_tensor("ps%d" % i, [128, 512], F32))) for i in range(8)])
        C.evac_i = 0
        for tl, dt in ((C.ident, BF16), (C.identf, F32)):
            S.op("pool", lambda e, tl=tl: e.memset(tl[:], 0.0), writes=[tl.b])
            S.op("pool", lambda e, tl=tl: e.affine_select(out=tl[:], in_=tl[:], compare_op=ALU.not_equal, fill=1.0,
                                                          base=0, pattern=[[-1, 128]], channel_multiplier=1),
                 reads=[tl.b], writes=[tl.b])
        S.op("pool", lambda e: e.memset(C.ones_bf[:], 1.0), writes=[C.ones_bf.b])

        run = lambda name: (stages is None) or (name in stages)
        src = C.I["x"]
        if run("mem"):
            stage_mem(C)
        for l in range(depth):
            if run("ffn1"):
                stage_ffn(C, l, 1, src)
                src = C.X
            if run("inproj"):
                stage_inproj(C, l, src)
            if run("ssd"):
                stage_ssd(C, l)
            if run("sb"):
                stage_sb(C, l)
            if run("rw"):
                stage_rw(C, l)
            if run("outproj"):
                stage_outproj(C, l)
            if run("cross"):
                stage_cross(C, l)
            if run("ffn2"):
                stage_ffn(C, l, 2, src)
        if run("final"):
            stage_final(C, src)
        S.barrier()
        S.finish()
    return nc


class StageMem:
    def __init__(self, C):
        self.C = C
        self.st = contextlib.ExitStack()
        self.k = 0

    def __enter__(self):
        self.st.__enter__()
        return self

    def __exit__(self, *a):
        self.C.S.barrier()
        self.st.__exit__(None, None, None)
        return False

    def tile(self, shape, dt, name=None):
        self.C.tcount = getattr(self.C, "tcount", 0) + 1
        return TL(self.st.enter_context(self.C.nc.sbuf_tensor("t%d_%s" % (self.C.tcount, name or ""), shape, dt)))

    def ring(self, n, shape, dt, name=None):
        return Ring([self.tile(shape, dt, name) for _ in range(n)])


def evac_eng(C):
    C.evac_i += 1
    return "act" if C.evac_i % 2 else "dve"


def copy_op(C, eng, out_ap, in_ap, reads, writes):
    S = C.S
    if eng == "act":
        S.op("act", lambda e: e.copy(out=out_ap, in_=in_ap), reads=reads, writes=writes)
    else:
        S.op(eng, lambda e: e.tensor_copy(out=out_ap, in_=in_ap), reads=reads, writes=writes)


def rstd_from_sumsq(C, ss, n, eps, tmp=None):
    S = C.S
    S.op("act", lambda e: e.activation(out=ss[:], in_=ss[:], func=AF.Ln, scale=1.0 / n, bias=eps), reads=[ss.b], writes=[ss.b])
    S.op("act", lambda e: e.activation(out=ss[:], in_=ss[:], func=AF.Exp, scale=-0.5), reads=[ss.b], writes=[ss.b])


def norm_transpose(C, M, src_ap, t0, ntok, gb, xnT, xnT_bufs, src_bufs, col0=0):
    S = C.S
    for m in range(ntok // 128):
        xt = M.xring.next()
        S.dma("sp", lambda e, xt=xt, m=m: e.dma_start(out=xt[:], in_=src_ap[t0 + m * 128:t0 + (m + 1) * 128, :]),
              reads=src_bufs, writes=[xt.b])
        ss = M.ssring.next()
        junk = M.junk
        S.op("act", lambda e, xt=xt, ss=ss: e.activation(out=junk[:], in_=xt[:], func=AF.Square, accum_out=ss[:]),
             reads=[xt.b], writes=[junk.b, ss.b])
        rstd_from_sumsq(C, ss, D, EPS)
        xn = M.xnring.next()
        S.op("dve", lambda e, xt=xt, ss=ss, xn=xn: e.scalar_tensor_tensor(out=xn[:], in0=xt[:], scalar=ss[:, 0:1], in1=gb[:],
                                                                        op0=ALU.mult, op1=ALU.mult),
             reads=[xt.b, ss.b, gb.b], writes=[xn.b])
        for half in range(2):
            pt = C.psum.next()
            ptv = pt.t[:].bitcast(BF16)
            for j in range(8):
                kc = half * 8 + j
                S.op("pe", lambda e, ptv=ptv, xn=xn, j=j, kc=kc: e.transpose(ptv[:, j * 128:(j + 1) * 128], xn[:, kc * 128:(kc + 1) * 128], C.ident[:]),
                     reads=[xn.b, C.ident.b], writes=[pt.b])
            dst = xnT[:, half * 8:(half + 1) * 8, col0 + m * 128:col0 + (m + 1) * 128]
            copy_op(C, evac_eng(C), dst, ptv.rearrange("p (k t) -> p k t", k=8), [pt.b], [xnT_bufs[m]])


def load_gb(C, M, row_ap):
    gb = M.gb
    C.S.dma("sp", lambda e: e.dma_start(out=gb[:], in_=row_ap.partition_broadcast(128)), writes=[gb.b])
    return gb


def load_wblk(C, M, w_ap, c0, ncols):
    wt = M.wring.next()
    C.S.dma("pool", lambda e: e.dma_start(out=wt[:, :, 0:ncols], in_=w_ap[:, c0:c0 + ncols].rearrange("(k p) f -> p k f", p=128)),
            writes=[wt.b])
    return wt


def mm_fm(C, wt, cs, ncols, xnT, xnT_bufs, tsl):
    S = C.S
    ps = C.psum.next()
    n = tsl.stop - tsl.start
    for kc in range(KC):
        S.op("pe", lambda e, kc=kc, ps=ps: e.matmul(ps[0:ncols, 0:n], lhsT=wt[:, kc, cs:cs + ncols], rhs=xnT[:, kc, tsl],
                                                   start=(kc == 0), stop=(kc == KC - 1)),
             reads=[wt.b] + list(xnT_bufs), writes=[ps.b])
    return ps


def proj_tok(C, M, actT, act_bufs, nch, w_ap, scale, tb, res_src):
    S = C.S
    TB = C.TB
    t0 = tb * TB
    nm = TB // 128
    G = 4
    for n in range(4):
        accs = [C.psum.next() for _ in range(nm)]
        for g0 in range(0, nch, G):
            gn = min(G, nch - g0)
            w2 = M.w2ring.next()
            S.dma("pool", lambda e, w2=w2, g0=g0, gn=gn, n=n: e.dma_start(
                out=w2[:, 0:gn, :], in_=w_ap[g0 * 128:(g0 + gn) * 128, n * 512:(n + 1) * 512].rearrange("(g p) f -> p g f", p=128)),
                writes=[w2.b])
            for gi in range(gn):
                f = g0 + gi
                for m in range(nm):
                    S.op("pe", lambda e, f=f, gi=gi, m=m, w2=w2, acc=accs[m]: e.matmul(
                        acc[:, :], lhsT=actT[:, f, m * 128:(m + 1) * 128], rhs=w2[:, gi, :], start=(f == 0), stop=(f == nch - 1)),
                        reads=[w2.b] + list(act_bufs), writes=[accs[m].b])
        for m in range(nm):
            xr = M.rring.next()
            S.dma("sp", lambda e, xr=xr, m=m, n=n: e.dma_start(out=xr[:], in_=res_src[t0 + m * 128:t0 + (m + 1) * 128, n * 512:(n + 1) * 512]),
                  reads=[C.bX[tb]] if res_src is C.X else [], writes=[xr.b])
            S.op("dve", lambda e, xr=xr, acc=accs[m]: e.scalar_tensor_tensor(out=xr[:], in0=acc[:, :], scalar=scale, in1=xr[:],
                                                                            op0=ALU.mult, op1=ALU.add),
                 reads=[accs[m].b, xr.b], writes=[xr.b])
            S.dma("sp", lambda e, xr=xr, m=m, n=n: e.dma_start(out=C.X[t0 + m * 128:t0 + (m + 1) * 128, n * 512:(n + 1) * 512], in_=xr[:]),
                  reads=[xr.b], writes=[C.bX[tb]])


def lin_mem(C, M):
    TB = C.TB
    M.xring = M.ring(2, [128, D], F32, "x")
    M.xnring = M.ring(2, [128, D], BF16, "xn")
    M.ssring = M.ring(4, [128, 1], F32, "ss")
    M.junk = M.tile([128, D], BF16, "junk")
    M.gb = M.tile([128, D], F32, "gb")
    M.xnT = M.tile([128, KC, TB], BF16, "xnT")
    M.xnT_bufs = [Buf() for _ in range(TB // 128)]
    M.wring = M.ring(4, [128, KC, 256], BF16, "w")
    M.w2ring = M.ring(4, [128, 4, 512], BF16, "w2")
    M.rring = M.ring(4, [128, 512], F32, "res")


def stage_mem(C):
    with StageMem(C) as M:
        lin_mem(C, M)
        gb = load_gb(C, M, C.I["mem_norm"])
        bufs = [Buf(), Buf()]
        norm_transpose(C, M, C.I["mem"], 0, MEM, gb, C.memnT, bufs, [])
        C.S.op("pool", lambda e: e.nop(), reads=bufs, writes=[C.memnT.b])


def stage_ffn(C, l, which, src):
    S = C.S
    TB = C.TB
    pre = "ffn%d_" % which
    wg, wu, wd = C.I[pre + "w_gate"][l], C.I[pre + "w_up"][l], C.I[pre + "w_down"][l]
    NH = NFC // 2
    with StageMem(C) as M:
        lin_mem(C, M)
        hT = M.tile([128, NH, TB], BF16, "hT")
        hbufs = [Buf() for _ in range(NH)]
        sgring = M.ring(3, [128, 512], F32, "sg")
        gb = load_gb(C, M, C.I[pre + "norm"][l])
        for tb in range(C.NTB):
            t0 = tb * TB
            norm_transpose(C, M, src, t0, TB, gb, M.xnT, M.xnT_bufs, [C.bX[tb]] if src is C.X else [])
            for half in range(2):
                for fb in range(NH // 2):
                    c0 = (half * NH + fb * 2) * 128
                    wgt = load_wblk(C, M, wg, c0, 256)
                    wut = load_wblk(C, M, wu, c0, 256)
                    for fc in range(2):
                        f = fb * 2 + fc
                        for th in range(TB // 512):
                            tsl = slice(th * 512, (th + 1) * 512)
                            pg = mm_fm(C, wgt, fc * 128, 128, M.xnT, M.xnT_bufs, tsl)
                            pu = mm_fm(C, wut, fc * 128, 128, M.xnT, M.xnT_bufs, tsl)
                            sg = sgring.next()
                            S.op("act", lambda e, sg=sg, pg=pg: e.activation(out=sg[:], in_=pg[:, :], func=AF.Silu),
                                 reads=[pg.b], writes=[sg.b])
                            S.op("dve", lambda e, sg=sg, pu=pu, f=f, tsl=tsl: e.tensor_tensor(out=hT[:, f, tsl], in0=pu[:, :], in1=sg[:], op=ALU.mult),
                                 reads=[pu.b, sg.b], writes=[hbufs[f]])
                wrows = wd[half * NH * 128:(half + 1) * NH * 128, :]
                proj_tok(C, M, hT, hbufs, NH, wrows, 0.5, tb, src if half == 0 else C.X)


def stage_inproj(C, l, src):
    S = C.S
    TB = C.TB
    w = C.I["w_in"][l]
    with StageMem(C) as M:
        lin_mem(C, M)
        stg = M.ring(4, [128, 512], F32, "stg")
        stb = M.ring(4, [128, 256], BF16, "stb")
        gb = load_gb(C, M, C.I["mix_norm"][l])
        segs = [(O_Z, 1024, "z"), (O_X, O_SB - O_X, "fm"), (O_SB, 1024, "fm"), (O_SB + 1024, 512, "v"), (O_RW, NIN - O_RW, "fm")]
        for tb in range(C.NTB):
            t0 = tb * TB
            norm_transpose(C, M, src, t0, TB, gb, M.xnT, M.xnT_bufs, [C.bX[tb]] if src is C.X else [])
            for (s0, sn, kind) in segs:
                for c0 in range(s0, s0 + sn, 256):
                    ncol = min(256, s0 + sn - c0)
                    wt = load_wblk(C, M, w, c0, ncol)
                    if kind == "fm":
                        for cs in range(0, ncol, 128):
                            nn = min(128, ncol - cs)
                            for th in range(TB // 512):
                                tsl = slice(th * 512, (th + 1) * 512)
                                ps = mm_fm(C, wt, cs, nn, M.xnT, M.xnT_bufs, tsl)
                                sg = stg.next()
                                copy_op(C, evac_eng(C), sg[0:nn, :], ps[0:nn, :], [ps.b], [sg.b])
                                S.dma("sp", lambda e, sg=sg, r0=c0 + cs, nn=nn, th=th, t0=t0: e.dma_start(
                                    out=C.UT[r0:r0 + nn, t0 + th * 512:t0 + (th + 1) * 512], in_=sg[0:nn, :]),
                                    reads=[sg.b], writes=[C.bUT])
                    else:
                        for m in range(TB // 128):
                            ps = C.psum.next()
                            for kc in range(KC):
                                S.op("pe", lambda e, kc=kc, ps=ps, m=m, wt=wt: e.matmul(
                                    ps[:, 0:256], lhsT=M.xnT[:, kc, m * 128:(m + 1) * 128], rhs=wt[:, kc, 0:256],
                                    start=(kc == 0), stop=(kc == KC - 1)),
                                    reads=[wt.b] + M.xnT_bufs, writes=[ps.b])
                            sb_ = stb.next()
                            if kind == "z":
                                S.op("act", lambda e, sb_=sb_, ps=ps: e.activation(out=sb_[:], in_=ps[:, 0:256], func=AF.Silu),
                                     reads=[ps.b], writes=[sb_.b])
                                dst, db, cc = C.SZ, C.bSZ, c0 - O_Z
                            else:
                                copy_op(C, "dve", sb_[:], ps[:, 0:256], [ps.b], [sb_.b])
                                dst, db, cc = C.VSB, C.bVSB, c0 - (O_SB + 1024)
                            S.dma("sp", lambda e, sb_=sb_, dst=dst, cc=cc, m=m, t0=t0: e.dma_start(
                                out=dst[t0 + m * 128:t0 + (m + 1) * 128, cc:cc + 256], in_=sb_[:]),
                                reads=[sb_.b], writes=[db])


def stage_outproj(C, l):
    S = C.S
    TB = C.TB
    with StageMem(C) as M:
        lin_mem(C, M)
        for tb in range(C.NTB):
            t0 = tb * TB
            S.dma("sp", lambda e, t0=t0: e.dma_start(out=M.xnT[:], in_=C.YMT[:, t0:t0 + TB].rearrange("(k p) t -> p k t", p=128)),
                  reads=[C.bYMT], writes=M.xnT_bufs)
            proj_tok(C, M, M.xnT, M.xnT_bufs, KC, C.I["w_out"][l], 1.0, tb, C.X)


def stage_cross(C, l):
    S = C.S
    TB = C.TB
    scale = 128 ** -0.5
    with StageMem(C) as M:
        lin_mem(C, M)
        KT = M.tile([128, 4, MEM], BF16, "KT")
        V = M.tile([128, 2, 512], BF16, "V")
        qT = M.tile([128, 4, TB], BF16, "qT")
        oT = M.tile([128, 4, TB], BF16, "oT")
        obufs = [Buf() for _ in range(4)]
        pring = M.ring(4, [128, 512], BF16, "pT")
        rdring = M.ring(2, [128, 512], F32, "rden")
        for cb in range(2):
            wt = load_wblk(C, M, C.I["cross_wk"][l], cb * 256, 256)
            for hh in range(2):
                h = cb * 2 + hh
                ps = mm_fm(C, wt, hh * 128, 128, C.memnT, [C.memnT.b], slice(0, MEM))
                copy_op(C, evac_eng(C), KT[:, h, :], ps[:, 0:MEM], [ps.b], [KT.b])
            wt = load_wblk(C, M, C.I["cross_wv"][l], cb * 256, 256)
            for mc in range(2):
                ps = C.psum.next()
                for kc in range(KC):
                    S.op("pe", lambda e, kc=kc, ps=ps, mc=mc, wt=wt: e.matmul(
                        ps[:, 0:256], lhsT=C.memnT[:, kc, mc * 128:(mc + 1) * 128], rhs=wt[:, kc, 0:256],
                        start=(kc == 0), stop=(kc == KC - 1)), reads=[wt.b, C.memnT.b], writes=[ps.b])
                copy_op(C, evac_eng(C), V[:, mc, cb * 256:(cb + 1) * 256], ps[:, 0:256], [ps.b], [V.b])
        gb = load_gb(C, M, C.I["cross_norm"][l])
        for tb in range(C.NTB):
            t0 = tb * TB
            norm_transpose(C, M, C.X, t0, TB, gb, M.xnT, M.xnT_bufs, [C.bX[tb]])
            for cb in range(2):
                wt = load_wblk(C, M, C.I["cross_wq"][l], cb * 256, 256)
                for hh in range(2):
                    h = cb * 2 + hh
                    for th in range(TB // 512):
                        tsl = slice(th * 512, (th + 1) * 512)
                        ps = mm_fm(C, wt, hh * 128, 128, M.xnT, M.xnT_bufs, tsl)
                        S.op("act", lambda e, ps=ps, h=h, tsl=tsl: e.activation(out=qT[:, h, tsl], in_=ps[:, :], func=AF.Copy, scale=scale),
                             reads=[ps.b], writes=[qT.b])
            for h in range(4):
                for th in range(TB // 512):
                    tsl = slice(th * 512, (th + 1) * 512)
                    pts = []
                    for mc in range(2):
                        ps = C.psum.next()
                        S.op("pe", lambda e, ps=ps, h=h, mc=mc, tsl=tsl: e.matmul(ps[:, :], lhsT=KT[:, h, mc * 128:(mc + 1) * 128], rhs=qT[:, h, tsl],
                                                                              start=True, stop=True),
                             reads=[KT.b, qT.b], writes=[ps.b])
                        pT = pring.next()
                        S.op("act", lambda e, ps=ps, pT=pT: e.activation(out=pT[:], in_=ps[:, :], func=AF.Exp), reads=[ps.b], writes=[pT.b])
                        pts.append(pT)
                    po = C.psum.next()
                    pd = C.psum.next()
                    for mc in range(2):
                        S.op("pe", lambda e, po=po, mc=mc, h=h, pT=pts[mc]: e.matmul(po[:, :], lhsT=V[:, mc, h * 128:(h + 1) * 128], rhs=pT[:],
                                                                                  start=(mc == 0), stop=(mc == 1)),
                             reads=[V.b, pts[mc].b], writes=[po.b])
                    for mc in range(2):
                        S.op("pe", lambda e, pd=pd, mc=mc, pT=pts[mc]: e.matmul(pd[:, :], lhsT=C.ones_bf[:], rhs=pT[:], start=(mc == 0), stop=(mc == 1)),
                             reads=[C.ones_bf.b, pts[mc].b], writes=[pd.b])
                    rd = rdring.next()
                    S.op("dve", lambda e, rd=rd, pd=pd: e.reciprocal(out=rd[:], in_=pd[:, :]), reads=[pd.b], writes=[rd.b])
                    S.op("dve", lambda e, rd=rd, po=po, h=h, tsl=tsl: e.tensor_tensor(out=oT[:, h, tsl], in0=po[:, :], in1=rd[:], op=ALU.mult),
                         reads=[po.b, rd.b], writes=[obufs[h]])
            proj_tok(C, M, oT, obufs, 4, C.I["cross_wo"][l], 1.0, tb, C.X)


def stage_final(C, src):
    S = C.S
    with StageMem(C) as M:
        lin_mem(C, M)
        gb = load_gb(C, M, C.I["final_norm"])
        oring = M.ring(2, [128, D], F32, "o")
        for m in range(C.T // 128):
            tb = (m * 128) // C.TB
            xt = M.xring.next()
            S.dma("sp", lambda e, xt=xt, m=m: e.dma_start(out=xt[:], in_=src[m * 128:(m + 1) * 128, :]),
                  reads=[C.bX[tb]] if src is C.X else [], writes=[xt.b])
            ss = M.ssring.next()
            S.op("act", lambda e, xt=xt, ss=ss: e.activation(out=M.junk[:], in_=xt[:], func=AF.Square, accum_out=ss[:]),
                 reads=[xt.b], writes=[M.junk.b, ss.b])
            rstd_from_sumsq(C, ss, D, EPS)
            ot = oring.next()
            S.op("dve", lambda e, xt=xt, ss=ss, ot=ot: e.scalar_tensor_tensor(out=ot[:], in0=xt[:], scalar=ss[:, 0:1], in1=gb[:],
                                                                            op0=ALU.mult, op1=ALU.mult),
                 reads=[xt.b, ss.b, gb.b], writes=[ot.b])
            S.dma("sp", lambda e, ot=ot, m=m: e.dma_start(out=C.out[m * 128:(m + 1) * 128, :], in_=ot[:]),
                  reads=[ot.b], is_output=True)


def dump(C, name, ap, buf):
    if name not in getattr(C, "dumps", ()) or name in C.dumped:
        return
    C.dumped.add(name)
    d = C.nc.dram_tensor("D_" + name, list(ap.shape), ap.dtype, kind="ExternalOutput").ap()
    C.S.dma("sp", lambda e: e.dma_start(out=d, in_=ap), reads=[buf], is_output=True)


def bc_last(ap, shape):
    return ap.unsqueeze(2).broadcast_to(shape)


def load_cols(C, tl, dst_ap, src_1d):
    C.S.dma("sp", lambda e: e.dma_start(out=dst_ap, in_=src_1d.rearrange("(c p) -> p c", p=128), allow_slow_non_contiguous=True),
            writes=[tl.b])


def stage_ssd(C, l):
    S = C.S
    T = C.T
    I = C.I
    W = min(512, T)
    NSB = T // W
    NCH = W // 128
    with StageMem(C) as M:
        cw = M.tile([128, 12, 4], F32, "cw")
        for k in range(4):
            load_cols(C, cw, cw[:, :, k], I["ssd_conv_w"][l, k])
        cb = M.tile([128, 12], F32, "cb")
        load_cols(C, cb, cb[:, :], I["ssd_conv_b"][l])
        ng = M.tile([128, 1024], F32, "ng")
        S.dma("sp", lambda e: e.dma_start(out=ng[:], in_=I["ssd_norm"][l].partition_broadcast(128)), writes=[ng.b])
        dfull = M.tile([128, 16], F32, "dfull")
        S.dma("sp", lambda e: e.dma_start(out=dfull[:], in_=I["ssd_d"][l].partition_broadcast(128)), writes=[dfull.b])
        dtb = M.tile([128, 1], F32, "dtb")
        av = M.tile([128, 1], F32, "av")
        S.op("pool", lambda e: e.memset(dtb[:], 0.0), writes=[dtb.b])
        S.op("pool", lambda e: e.memset(av[:], 0.0), writes=[av.b])
        for r0 in (0, 32, 64):
            S.dma("sp", lambda e, r0=r0: e.dma_start(out=dtb[r0:r0 + 16, :], in_=I["ssd_dt_bias"][l].rearrange("(h o) -> h o", o=1)), writes=[dtb.b])
            S.dma("sp", lambda e, r0=r0: e.dma_start(out=av[r0:r0 + 16, :], in_=I["ssd_a_log"][l].rearrange("(h o) -> h o", o=1)), writes=[av.b])
        S.op("act", lambda e: e.activation(out=av[:], in_=av[:], func=AF.Exp), reads=[av.b], writes=[av.b])
        S.op("dve", lambda e: e.tensor_scalar(out=av[:], in0=av[:], scalar1=-1.0, scalar2=None, op0=ALU.mult), reads=[av.b], writes=[av.b])
        tri = M.tile([128, 2, 128], F32, "tri")
        S.op("pool", lambda e: e.memset(tri[:], 1.0), writes=[tri.b])
        S.op("pool", lambda e: e.affine_select(out=tri[:], in_=tri[:], compare_op=ALU.is_ge, fill=0.0, base=0,
                                               pattern=[[0, 2], [1, 128]], channel_multiplier=-1), reads=[tri.b], writes=[tri.b])
        sel = M.tile([16, 16, 128], F32, "sel")
        S.op("pool", lambda e: e.memset(sel[:], 0.0), writes=[sel.b])
        S.op("pool", lambda e: e.affine_select(out=sel[:], in_=sel[:], compare_op=ALU.not_equal, fill=1.0, base=0,
                                               pattern=[[-1, 16], [0, 128]], channel_multiplier=1), reads=[sel.b], writes=[sel.b])
        ones16 = M.tile([16, 128], F32, "ones16")
        S.op("pool", lambda e: e.memset(ones16[:], 1.0), writes=[ones16.b])
        onesW = M.tile([128, 128], F32, "onesW")
        S.op("pool", lambda e: e.memset(onesW[:], 1.0), writes=[onesW.b])
        ST = M.tile([128, 16, 64], F32, "ST")
        S.op("pool", lambda e: e.memset(ST[:], 0.0), writes=[ST.b])
        stbring = M.ring(2, [128, 16, 64], BF16, "STb")
        stb = stbring.next()
        S.op("pool", lambda e, stb=stb: e.memset(stb[:], 0.0), writes=[stb.b])
        cinring = M.ring(2, [128, 4, W + 3], F32, "cin")
        accring = M.ring(2, [128, W], F32, "acc")
        xbcring = M.ring(2, [128, 12, W], BF16, "xbc")
        rawring = M.ring(2, [128, W], F32, "raw")
        e1t = M.tile([128, W], F32, "e1")
        stkring = M.ring(2, [128, W], F32, "stk")
        dat = M.tile([128, W], F32, "da")
        acsring = M.ring(2, [128, W], F32, "acs")
        for t_ in rawring.tiles + stkring.tiles + acsring.tiles + [e1t, dat]:
            S.op("pool", lambda e, t_=t_: e.memset(t_[:], 0.0), writes=[t_.b])
        tmring = M.ring(2, [128, 128], F32, "tm")
        earing = M.ring(2, [128, 16], F32, "ea")
        xtmring = M.ring(2, [128, 1024], BF16, "xtm")
        btmring = M.ring(2, [128, 256], BF16, "btm")
        cbmring = M.ring(2, [128, 2, 128], F32, "cbm")
        xdring = M.ring(2, [128, 16, 64], BF16, "xd")
        xddring = M.ring(2, [128, 16, 64], BF16, "xdd")
        xDring = M.ring(2, [128, 16, 64], F32, "xD")
        difring = M.ring(2, [128, 4, 128], F32, "dif")
        decring = M.ring(2, [128, 4, 128], F32, "dec")
        wTring = M.ring(8, [128, 4, 128], BF16, "wT")
        t1ring = M.ring(2, [128, 512], F32, "t1")
        t2ring = M.ring(2, [128, 512], F32, "t2")
        yzring = M.ring(2, [128, 1024], F32, "yz")
        szring = M.ring(2, [128, 1024], BF16, "sz")
        ynring = M.ring(2, [128, 1024], BF16, "yn")
        ymtring = M.ring(2, [128, 8, W], BF16, "ymt")
        ss2ring = M.ring(2, [128, 2], F32, "ss2")
        dglring = M.ring(2, [16, 16], F32, "dgl")
        edlring = M.ring(2, [128, 16], F32, "edl")
        junk = M.tile([128, 512], BF16, "junk")

        for sb in range(NSB):
            t0 = sb * W
            xbc = xbcring.next()
            for cg in range(3):
                cin = cinring.next()
                r0 = O_X + cg * 512
                if t0 == 0:
                    S.op("pool", lambda e, cin=cin: e.memset(cin[:, :, 0:3], 0.0), writes=[cin.b])
                    S.dma("sp", lambda e, cin=cin, r0=r0: e.dma_start(out=cin[:, :, 3:3 + W], in_=C.UT[r0:r0 + 512, 0:W].rearrange("(c p) t -> p c t", p=128)),
                          reads=[C.bUT], writes=[cin.b])
                else:
                    S.dma("sp", lambda e, cin=cin, r0=r0, t0=t0: e.dma_start(out=cin[:], in_=C.UT[r0:r0 + 512, t0 - 3:t0 + W].rearrange("(c p) t -> p c t", p=128)),
                          reads=[C.bUT], writes=[cin.b])
                for ci in range(4):
                    c = cg * 4 + ci
                    acc = accring.next()
                    S.op("dve", lambda e, acc=acc, cin=cin, ci=ci, c=c: e.tensor_scalar(out=acc[:], in0=cin[:, ci, 3:3 + W], scalar1=cw[:, c, 3:4], scalar2=None, op0=ALU.mult),
                         reads=[cin.b, cw.b], writes=[acc.b])
                    for k in (2, 1, 0):
                        S.op("dve", lambda e, acc=acc, cin=cin, ci=ci, c=c, k=k: e.scalar_tensor_tensor(out=acc[:], in0=cin[:, ci, k:k + W], scalar=cw[:, c, k:k + 1], in1=acc[:],
                                                                                                  op0=ALU.mult, op1=ALU.add),
                             reads=[cin.b, cw.b, acc.b], writes=[acc.b])
                    S.op("act", lambda e, acc=acc, c=c, xbc=xbc: e.activation(out=xbc[:, c, :], in_=acc[:], func=AF.Silu, bias=cb[:, c:c + 1]),
                         reads=[acc.b, cb.b], writes=[xbc.b])
            raw = rawring.next()
            for r0 in (0, 32, 64):
                S.dma("sp", lambda e, raw=raw, r0=r0, t0=t0: e.dma_start(out=raw[r0:r0 + 16, :], in_=C.UT[O_DT:O_DT + 16, t0:t0 + W]),
                      reads=[C.bUT], writes=[raw.b])
            stk = stkring.next()
            acs = acsring.next()
            S.op("act", lambda e, raw=raw: e.activation(out=e1t[0:80, :], in_=raw[0:80, :], func=AF.Exp, bias=dtb[0:80, :]),
                 reads=[raw.b, dtb.b], writes=[e1t.b])
            S.op("act", lambda e, stk=stk: e.activation(out=stk[0:80, :], in_=e1t[0:80, :], func=AF.Ln, bias=1.0), reads=[e1t.b], writes=[stk.b])
            S.op("dve", lambda e, stk=stk: e.tensor_scalar(out=dat[0:80, :], in0=stk[0:80, :], scalar1=av[0:80, :], scalar2=None, op0=ALU.mult),
                 reads=[stk.b, av.b], writes=[dat.b])
            for ch in range(NCH):
                cs = slice(ch * 128, (ch + 1) * 128)
                S.op("dve", lambda e, acs=acs, cs=cs: e.tensor_tensor_scan(out=acs[0:80, cs], data0=onesW[0:80, :], data1=dat[0:80, cs], initial=0.0,
                                                                        op0=ALU.mult, op1=ALU.add), reads=[dat.b, onesW.b], writes=[acs.b])
            S.op("pool", lambda e, stk=stk, acs=acs: e.tensor_copy(out=stk[32:48, :], in_=acs[32:48, :]), reads=[acs.b], writes=[stk.b])
            for ch in range(NCH):
                cs = slice(ch * 128, (ch + 1) * 128)
                last = (ch + 1) * 128 - 1
                S.op("act", lambda e, stk=stk, acs=acs, cs=cs, last=last: e.activation(out=stk[64:80, cs], in_=acs[64:80, cs], func=AF.Exp, scale=-1.0,
                                                                                   bias=acs[64:80, last:last + 1]), reads=[acs.b], writes=[stk.b])
            ymt = ymtring.next()
            for ch in range(NCH):
                cs = slice(ch * 128, (ch + 1) * 128)
                last = (ch + 1) * 128 - 1
                tc = t0 + ch * 128
                sz = szring.next()
                S.dma("sp", lambda e, sz=sz, tc=tc: e.dma_start(out=sz[:], in_=C.SZ[tc:tc + 128, :]), reads=[C.bSZ], writes=[sz.b])
                ptm = C.psum.next()
                S.op("pe", lambda e, ptm=ptm, stk=stk, cs=cs: e.transpose(ptm[:, 0:128], stk[:, cs], C.identf[:]), reads=[stk.b, C.identf.b], writes=[ptm.b])
                tm = tmring.next()
                S.op("act", lambda e, tm=tm, ptm=ptm: e.copy(out=tm[:], in_=ptm[:, 0:128]), reads=[ptm.b], writes=[tm.b])
                ea = earing.next()
                S.op("act", lambda e, ea=ea, tm=tm: e.activation(out=ea[:], in_=tm[:, 32:48], func=AF.Exp), reads=[tm.b], writes=[ea.b])
                px = C.psum.next()
                pxv = px.t[:].bitcast(BF16)
                for c in range(8):
                    S.op("pe", lambda e, pxv=pxv, xbc=xbc, c=c, cs=cs: e.transpose(pxv[:, c * 128:(c + 1) * 128], xbc[:, c, cs], C.ident[:]),
                         reads=[xbc.b, C.ident.b], writes=[px.b])
                xtm = xtmring.next()
                S.op("dve", lambda e, xtm=xtm, pxv=pxv: e.tensor_copy(out=xtm[:], in_=pxv), reads=[px.b], writes=[xtm.b])
                pb = C.psum.next()
                pbv = pb.t[:].bitcast(BF16)
                for g in range(2):
                    S.op("pe", lambda e, pbv=pbv, xbc=xbc, g=g, cs=cs: e.transpose(pbv[:, g * 128:(g + 1) * 128], xbc[:, 8 + g, cs], C.ident[:]),
                         reads=[xbc.b, C.ident.b], writes=[pb.b])
                btm = btmring.next()
                S.op("act", lambda e, btm=btm, pbv=pbv: e.copy(out=btm[:], in_=pbv[:, 0:256]), reads=[pb.b], writes=[btm.b])
                pcb = C.psum.next()
                for g in range(2):
                    S.op("pe", lambda e, pcb=pcb, xbc=xbc, g=g, cs=cs: e.matmul(pcb[:, g * 128:(g + 1) * 128], lhsT=xbc[:, 8 + g, cs], rhs=xbc[:, 10 + g, cs], start=True, stop=True),
                         reads=[xbc.b], writes=[pcb.b])
                cbm = cbmring.next()
                S.op("dve", lambda e, cbm=cbm, pcb=pcb: e.tensor_tensor(out=cbm[:], in0=pcb[:, 0:256].rearrange("p (g l) -> p g l", g=2), in1=tri[:], op=ALU.mult),
                     reads=[pcb.b, tri.b], writes=[cbm.b])
                xd = xdring.next()
                xdd = xddring.next()
                xD = xDring.next()
                xv = xtm[:].rearrange("p (h d) -> p h d", h=16)
                S.op("dve", lambda e, xd=xd, xv=xv, tm=tm: e.tensor_tensor(out=xd[:], in0=xv, in1=bc_last(tm[:, 0:16], [128, 16, 64]), op=ALU.mult),
                     reads=[xtm.b, tm.b], writes=[xd.b])
                S.op("dve", lambda e, xd=xd, xdd=xdd, tm=tm: e.tensor_tensor(out=xdd[:], in0=xd[:], in1=bc_last(tm[:, 64:80], [128, 16, 64]), op=ALU.mult),
                     reads=[xd.b, tm.b], writes=[xdd.b])
                S.op("pool", lambda e, xD=xD, xv=xv: e.tensor_tensor(out=xD[:], in0=xv, in1=bc_last(dfull[:, 0:16], [128, 16, 64]), op=ALU.mult),
                     reads=[xtm.b, dfull.b], writes=[xD.b])
                wTs = []
                for q in range(4):
                    g = q // 2
                    pbc = C.psum.next()
                    for j in range(4):
                        h = 4 * q + j
                        S.op("pe", lambda e, pbc=pbc, j=j, h=h, acs=acs, cs=cs: e.matmul(pbc[:, j * 128:(j + 1) * 128], lhsT=sel[0:16, h, :], rhs=acs[0:16, cs], start=True, stop=True),
                             reads=[sel.b, acs.b], writes=[pbc.b])
                    dif = difring.next()
                    for j in range(4):
                        h = 4 * q + j
                        S.op("dve", lambda e, dif=dif, pbc=pbc, j=j, h=h, tm=tm: e.tensor_scalar(out=dif[:, j, :], in0=pbc[:, j * 128:(j + 1) * 128], scalar1=tm[:, 32 + h:33 + h], scalar2=0.0,
                                                                                         op0=ALU.subtract, op1=ALU.min), reads=[pbc.b, tm.b], writes=[dif.b])
                    dec = decring.next()
                    S.op("act", lambda e, dec=dec, dif=dif: e.activation(out=dec[:], in_=dif[:], func=AF.Exp), reads=[dif.b], writes=[dec.b])
                    wT = wTring.next()
                    S.op("dve", lambda e, wT=wT, dec=dec, cbm=cbm, g=g: e.tensor_tensor(out=wT[:], in0=dec[:], in1=cbm[:, g, :].unsqueeze(1).broadcast_to([128, 4, 128]), op=ALU.mult),
                         reads=[dec.b, cbm.b], writes=[wT.b])
                    wTs.append(wT)
                yz = yzring.next()
                ss2 = ss2ring.next()
                for g in range(2):
                    pyd = C.psum.next()
                    for hh in range(8):
                        h = g * 8 + hh
                        S.op("pe", lambda e, pyd=pyd, hh=hh, h=h, wT=wTs[h // 4], xd=xd: e.matmul(pyd[:, hh * 64:(hh + 1) * 64], lhsT=wT[:, h % 4, :], rhs=xd[:, h, :], start=True, stop=True),
                             reads=[wTs[h // 4].b, xd.b], writes=[pyd.b])
                    pyo = C.psum.next()
                    S.op("pe", lambda e, pyo=pyo, g=g, xbc=xbc, cs=cs, stb=stb: e.matmul(pyo[:, :], lhsT=xbc[:, 10 + g, cs], rhs=stb[:, g * 8:(g + 1) * 8, :], start=True, stop=True),
                         reads=[xbc.b, stb.b], writes=[pyo.b])
                    t1 = t1ring.next()
                    S.op("dve", lambda e, t1=t1, pyo=pyo, ea=ea, g=g: e.tensor_tensor(out=t1[:].rearrange("p (h d) -> p h d", h=8), in0=pyo[:, :].rearrange("p (h d) -> p h d", h=8),
                                                                                  in1=bc_last(ea[:, g * 8:(g + 1) * 8], [128, 8, 64]), op=ALU.mult),
                         reads=[pyo.b, ea.b], writes=[t1.b])
                    t2 = t2ring.next()
                    S.op("dve", lambda e, t1=t1, t2=t2, pyd=pyd: e.tensor_tensor(out=t2[:], in0=pyd[:, :], in1=t1[:], op=ALU.add), reads=[pyd.b, t1.b], writes=[t2.b])
                    S.op("pool", lambda e, t2=t2, xD=xD, g=g: e.tensor_tensor(out=t2[:].rearrange("p (h d) -> p h d", h=8), in0=t2[:].rearrange("p (h d) -> p h d", h=8),
                                                                           in1=xD[:, g * 8:(g + 1) * 8, :], op=ALU.add), reads=[t2.b, xD.b], writes=[t2.b])
                    S.op("pool", lambda e, t2=t2, yz=yz, sz=sz, g=g: e.tensor_tensor(out=yz[:, g * 512:(g + 1) * 512], in0=t2[:], in1=sz[:, g * 512:(g + 1) * 512], op=ALU.mult),
                         reads=[t2.b, sz.b], writes=[yz.b])
                    S.op("act", lambda e, yz=yz, ss2=ss2, g=g: e.activation(out=junk[:], in_=yz[:, g * 512:(g + 1) * 512], func=AF.Square, accum_out=ss2[:, g:g + 1]),
                         reads=[yz.b], writes=[junk.b, ss2.b])
                psts = []
                for g in range(2):
                    pst = C.psum.next()
                    S.op("pe", lambda e, pst=pst, g=g, btm=btm, xdd=xdd: e.matmul(pst[:, :], lhsT=btm[:, g * 128:(g + 1) * 128], rhs=xdd[:, g * 8:(g + 1) * 8, :], start=True, stop=True),
                         reads=[btm.b, xdd.b], writes=[pst.b])
                    psts.append(pst)
                dgl = dglring.next()
                S.op("dve", lambda e, dgl=dgl, acs=acs, last=last: e.tensor_scalar(out=dgl[:], in0=C.identf[0:16, 0:16], scalar1=acs[0:16, last:last + 1], scalar2=None, op0=ALU.mult),
                     reads=[acs.b, C.identf.b], writes=[dgl.b])
                pedl = C.psum.next()
                S.op("pe", lambda e, pedl=pedl, dgl=dgl: e.matmul(pedl[:, 0:16], lhsT=ones16[:], rhs=dgl[:], start=True, stop=True), reads=[ones16.b, dgl.b], writes=[pedl.b])
                edl = edlring.next()
                S.op("act", lambda e, edl=edl, pedl=pedl: e.activation(out=edl[:], in_=pedl[:, 0:16], func=AF.Exp), reads=[pedl.b], writes=[edl.b])
                S.op("dve", lambda e, edl=edl: e.tensor_tensor(out=ST[:], in0=ST[:], in1=bc_last(edl[:, 0:16], [128, 16, 64]), op=ALU.mult), reads=[ST.b, edl.b], writes=[ST.b])
                for g in range(2):
                    S.op("dve", lambda e, g=g, pst=psts[g]: e.tensor_tensor(out=ST[:, g * 8:(g + 1) * 8, :], in0=ST[:, g * 8:(g + 1) * 8, :],
                                                                          in1=pst[:, :].rearrange("p (h d) -> p h d", h=8), op=ALU.add),
                         reads=[ST.b, psts[g].b], writes=[ST.b])
                stb = stbring.next()
                S.op("act", lambda e, stb=stb: e.copy(out=stb[:], in_=ST[:]), reads=[ST.b], writes=[stb.b])
                rstd_from_sumsq(C, ss2, 512, EPS)
                yn = ynring.next()
                for g in range(2):
                    S.op("dve", lambda e, yn=yn, yz=yz, ss2=ss2, g=g: e.scalar_tensor_tensor(out=yn[:, g * 512:(g + 1) * 512], in0=yz[:, g * 512:(g + 1) * 512], scalar=ss2[:, g:g + 1],
                                                                                        in1=ng[:, g * 512:(g + 1) * 512], op0=ALU.mult, op1=ALU.mult),
                         reads=[yz.b, ss2.b, ng.b], writes=[yn.b])
                pym = C.psum.next()
                pymv = pym.t[:].bitcast(BF16)
                for c in range(8):
                    S.op("pe", lambda e, pymv=pymv, yn=yn, c=c: e.transpose(pymv[:, c * 128:(c + 1) * 128], yn[:, c * 128:(c + 1) * 128], C.ident[:]),
                         reads=[yn.b, C.ident.b], writes=[pym.b])
                S.op("act", lambda e, ymt=ymt, pymv=pymv, cs=cs: e.copy(out=ymt[:, :, cs], in_=pymv.rearrange("p (k t) -> p k t", k=8)), reads=[pym.b], writes=[ymt.b])
            S.dma("sp", lambda e, ymt=ymt, t0=t0: e.dma_start(out=C.YMT[0:1024, t0:t0 + W].rearrange("(k p) t -> p k t", p=128), in_=ymt[:]),
                  reads=[ymt.b], writes=[C.bYMT])


def stage_sb(C, l):
    S = C.S
    T = C.T
    I = C.I
    NG = T // 512
    NB = T // 128
    with StageMem(C) as M:
        ring6 = Ring(C.psum.tiles[:6])
        accr = Ring(C.psum.tiles[6:])
        qT = M.tile([128, 4, T], BF16, "qT")
        kT = M.tile([128, 4, T], BF16, "kT")
        vt = M.tile([128, NB, 512], BF16, "vt")
        for c in range(4):
            S.dma("pool", lambda e, c=c: e.dma_start(out=qT[:, c, :], in_=C.UT[O_SB + c * 128:O_SB + (c + 1) * 128, :]), reads=[C.bUT], writes=[qT.b])
            S.dma("pool", lambda e, c=c: e.dma_start(out=kT[:, c, :], in_=C.UT[O_SB + 512 + c * 128:O_SB + 512 + (c + 1) * 128, :]), reads=[C.bUT], writes=[kT.b])
        for j0 in range(0, NB, 8):
            j1 = min(NB, j0 + 8)
            S.dma("sp", lambda e, j0=j0, j1=j1: e.dma_start(out=vt[:, j0:j1, :], in_=C.VSB[j0 * 128:j1 * 128, :].rearrange("(j p) c -> p j c", p=128)), reads=[C.bVSB], writes=[vt.b])
        sbg = M.tile([64, 8], F32, "sbg")
        S.dma("sp", lambda e: e.dma_start(out=sbg[:], in_=I["sb_norm"][l].rearrange("(h d) -> d h", d=64), allow_slow_non_contiguous=True), writes=[sbg.b])
        masks = []
        for r in range(4):
            mk = M.tile([128, 4, 128], F32, "mask%d" % r)
            S.op("pool", lambda e, mk=mk: e.memset(mk[:], 1.0), writes=[mk.b])
            S.op("pool", lambda e, mk=mk, r=r: e.affine_select(out=mk[:], in_=mk[:], compare_op=ALU.is_gt, fill=0.0, base=-128 * r,
                                                              pattern=[[128, 4], [1, 128]], channel_multiplier=-1), reads=[mk.b], writes=[mk.b])
            masks.append(mk)
        trib = M.tile([128, 128], BF16, "trib")
        S.op("pool", lambda e: e.memset(trib[:], 1.0), writes=[trib.b])
        S.op("pool", lambda e: e.affine_select(out=trib[:], in_=trib[:], compare_op=ALU.is_ge, fill=0.0, base=0,
                                               pattern=[[-1, 128]], channel_multiplier=1), reads=[trib.b], writes=[trib.b])
        ering = M.ring(3, [128, 512], F32, "e")
        spring = M.ring(3, [128, 512], BF16, "sp")
        tring = M.ring(2, [128, 512], F32, "t")
        wring = M.ring(3, [128, 512], BF16, "wT")
        aring = M.ring(3, [128, 512], BF16, "acc")
        sqring = M.ring(2, [64, 512], BF16, "sq")
        rsring = M.ring(2, [64, 512], F32, "rs")
        yoring = M.ring(2, [64, 512], BF16, "yo")
        for g in range(NG):
            qs = slice(g * 512, (g + 1) * 512)
            nj = 4 * g + 4
            for h in range(8):
                hc = h // 2
                hp = slice((h % 2) * 64, (h % 2) * 64 + 64)
                po = accr.next()
                acc = None
                for j in range(nj - 1, -1, -1):
                    ks = slice(j * 128, (j + 1) * 128)
                    pz = ring6.next()
                    S.op("pe", lambda e, pz=pz, ks=ks, hc=hc, hp=hp, qs=qs: e.matmul(pz[:, :], lhsT=kT[hp, hc, ks], rhs=qT[hp, hc, qs], start=True, stop=True),
                         reads=[kT.b, qT.b], writes=[pz.b])
                    et = ering.next()
                    S.op("act", lambda e, et=et, pz=pz: e.activation(out=et[:], in_=pz[:, :], func=AF.Exp, scale=0.125), reads=[pz.b], writes=[et.b])
                    if j >= 4 * g:
                        mk = masks[j - 4 * g]
                        S.op("dve", lambda e, et=et, mk=mk: e.tensor_tensor(out=et[:], in0=et[:], in1=mk[:].rearrange("p a b -> p (a b)"), op=ALU.mult),
                             reads=[et.b, mk.b], writes=[et.b])
                    sp = spring.next()
                    S.op("act", lambda e, sp=sp, et=et: e.activation(out=sp[:], in_=et[:], func=AF.Ln, bias=1.0), reads=[et.b], writes=[sp.b])
                    ps = ring6.next()
                    S.op("pe", lambda e, ps=ps, sp=sp, last=(acc is None): e.matmul(ps[:, :], lhsT=trib[:], rhs=sp[:], start=True, stop=last),
                         reads=[trib.b, sp.b], writes=[ps.b])
                    if acc is not None:
                        S.op("pe", lambda e, ps=ps, acc=acc: e.matmul(ps[:, :], lhsT=C.ones_bf[:], rhs=acc[:], start=False, stop=True),
                             reads=[C.ones_bf.b, acc.b], writes=[ps.b])
                    tt = tring.next()
                    S.op("act", lambda e, tt=tt, ps=ps: e.activation(out=tt[:], in_=ps[:, :], func=AF.Exp, scale=-1.0), reads=[ps.b], writes=[tt.b])
                    wT = wring.next()
                    S.op("dve", lambda e, wT=wT, et=et, tt=tt: e.tensor_tensor(out=wT[:], in0=et[:], in1=tt[:], op=ALU.mult), reads=[et.b, tt.b], writes=[wT.b])
                    S.op("pe", lambda e, po=po, wT=wT, j=j, h=h, nj=nj: e.matmul(po[0:64, :], lhsT=vt[:, j, h * 64:(h + 1) * 64], rhs=wT[:], start=(j == nj - 1), stop=(j == 0)),
                         reads=[vt.b, wT.b], writes=[po.b])
                    if j > 0:
                        nacc = aring.next()
                        if acc is None:
                            S.op("pool", lambda e, nacc=nacc, sp=sp: e.tensor_copy(out=nacc[:], in_=sp[:]), reads=[sp.b], writes=[nacc.b])
                        else:
                            S.op("pool", lambda e, nacc=nacc, sp=sp, acc=acc: e.tensor_tensor(out=nacc[:], in0=acc[:], in1=sp[:], op=ALU.add), reads=[sp.b, acc.b], writes=[nacc.b])
                        acc = nacc
                sq = sqring.next()
                S.op("act", lambda e, sq=sq, po=po: e.activation(out=sq[:], in_=po[0:64, :], func=AF.Square), reads=[po.b], writes=[sq.b])
                pss = ring6.next()
                S.op("pe", lambda e, pss=pss, sq=sq: e.matmul(pss[0:64, :], lhsT=C.ones_bf[0:64, 0:64], rhs=sq[:], start=True, stop=True), reads=[C.ones_bf.b, sq.b], writes=[pss.b])
                rs = rsring.next()
                S.op("act", lambda e, rs=rs, pss=pss: e.activation(out=rs[:], in_=pss[0:64, :], func=AF.Ln, scale=1.0 / 64, bias=EPS), reads=[pss.b], writes=[rs.b])
                S.op("act", lambda e, rs=rs: e.activation(out=rs[:], in_=rs[:], func=AF.Exp, scale=-0.5), reads=[rs.b], writes=[rs.b])
                yo = yoring.next()
                S.op("dve", lambda e, yo=yo, po=po, rs=rs, h=h: e.scalar_tensor_tensor(out=yo[:], in0=po[0:64, :], scalar=sbg[:, h:h + 1], in1=rs[:], op0=ALU.mult, op1=ALU.mult),
                     reads=[po.b, rs.b, sbg.b], writes=[yo.b])
                S.dma("sp", lambda e, yo=yo, h=h, qs=qs: e.dma_start(out=C.YMT[1024 + h * 64:1024 + (h + 1) * 64, qs], in_=yo[:]), reads=[yo.b], writes=[C.bYMT])


class _Stop(Exception):
    pass


def stage_rw(C, l):
    try:
        _stage_rw(C, l)
    except _Stop:
        pass


def _stage_rw(C, l):
    S = C.S

    import os as _os
    _bar = _os.environ.get("RWBAR", "") == "1"

    def chk(k):
        if getattr(C, "rwp", 99) == k:
            raise _Stop()
        if _bar:
            S.barrier()
    T = C.T
    I = C.I
    W = 256
    NBK = T // W
    NQ = W // 64
    L = 64
    R0 = O_RW
    with StageMem(C) as M:
        def cols(name, src, n):
            tl = M.tile([128, n], F32, name)
            load_cols(C, tl, tl[:, :], src)
            return tl
        mu = cols("mu", I["rw_mu"][l], 14)
        w0 = cols("w0", I["rw_w0"][l], 4)
        a0 = cols("a0", I["rw_a0"][l], 4)
        kkw = cols("kkw", I["rw_k_k"][l], 4)
        ka = cols("ka", I["rw_k_a"][l], 4)
        lnw = cols("lnw", I["rw_ln_w"][l], 4)
        lnb = cols("lnb", I["rw_ln_b"][l], 4)
        rkv = cols("rkv", I["rw_r_k"][l].rearrange("h d -> (h d)"), 4)
        wa = M.tile([128, 512], BF16, "wa")
        S.dma("pool", lambda e: e.dma_start(out=wa[0:64, :], in_=I["rw_w_up"][l]), writes=[wa.b])
        S.dma("pool", lambda e: e.dma_start(out=wa[64:128, :], in_=I["rw_a_up"][l]), writes=[wa.b])
        gu = M.tile([128, 512], BF16, "gu")
        S.dma("pool", lambda e: e.dma_start(out=gu[:], in_=I["rw_g_up"][l]), writes=[gu.b])
        if l > 0:
            v0 = cols("v0", I["rw_v0"][l - 1], 4)
            vdn = M.tile([128, 4, 32], BF16, "vdn")
            S.dma("pool", lambda e: e.dma_start(out=vdn[:], in_=I["rw_v_down"][l - 1].rearrange("(c p) j -> p c j", p=128)), writes=[vdn.b])
            vup = M.tile([32, 512], BF16, "vup")
            S.dma("pool", lambda e: e.dma_start(out=vup[:], in_=I["rw_v_up"][l - 1]), writes=[vup.b])
        blk = M.tile([128, 128], BF16, "blk")
        S.op("pool", lambda e: e.memset(blk[:], 0.0), writes=[blk.b])
        S.op("pool", lambda e: e.memset(blk[0:64, 0:64], 1.0), reads=[blk.b], writes=[blk.b])
        S.op("pool", lambda e: e.memset(blk[64:128, 64:128], 1.0), reads=[blk.b], writes=[blk.b])
        stackI = M.tile([128, 64], F32, "stackI")
        S.op("pool", lambda e: e.tensor_copy(out=stackI[0:64, :], in_=C.identf[0:64, 0:64]), reads=[C.identf.b], writes=[stackI.b])
        S.op("pool", lambda e: e.tensor_copy(out=stackI[64:128, :], in_=C.identf[64:128, 64:128]), reads=[C.identf.b, stackI.b], writes=[stackI.b])
        m0 = M.tile([128, W], F32, "m0")
        S.op("pool", lambda e: e.memset(m0[:], 1.0), writes=[m0.b])
        for q in range(NQ):
            S.op("pool", lambda e, q=q: e.memset(m0[:, q * L:q * L + 1], 0.0), reads=[m0.b], writes=[m0.b])
        def mask(name, op, cm, pj):
            mk = M.tile([64, 8, 64], F32, name)
            S.op("pool", lambda e: e.memset(mk[:], 1.0), writes=[mk.b])
            S.op("pool", lambda e: e.affine_select(out=mk[:], in_=mk[:], compare_op=op, fill=0.0, base=0, pattern=[[0, 8], [pj, 64]], channel_multiplier=cm),
                 reads=[mk.b], writes=[mk.b])
            return mk
        mUs = mask("mUs", ALU.is_gt, -1, 1)
        mUi = mask("mUi", ALU.is_ge, -1, 1)
        mLs = mask("mLs", ALU.is_gt, 1, -1)
        mId = mask("mId", ALU.is_ge, 1, -1)
        S.op("pool", lambda e: e.affine_select(out=mId[:], in_=mId[:], compare_op=ALU.is_ge, fill=0.0, base=0, pattern=[[0, 8], [1, 64]], channel_multiplier=-1),
             reads=[mId.b], writes=[mId.b])
        Sst = M.ring(2, [64, 8, 64], F32, "Sst")
        sst = Sst.next()
        S.op("pool", lambda e, sst=sst: e.memset(sst[:], 0.0), writes=[sst.b])

        ut = M.tile([128, 14, W + 1], F32, "ut")
        dsh = M.tile([128, W], F32, "dsh")
        us = M.tile([128, 14, W], F32, "us")
        twad = M.tile([128, W], BF16, "twad")
        sgd = M.tile([128, W], BF16, "sgd")
        lw = M.tile([128, 4, W], F32, "lw")
        cs = M.tile([128, 4, W], F32, "cs")
        at_ = M.tile([128, 4, W], F32, "a")
        gt_ = M.tile([128, 4, W], F32, "g")
        kkn = M.tile([128, 4, W], F32, "kkn")
        kp = M.tile([128, 4, W], F32, "kp")
        aT = M.tile([128, 4, W], F32, "aT")
        bT = M.tile([128, 4, W], F32, "bT")
        kT = M.tile([128, 4, W], F32, "kT")
        rT = M.tile([128, 4, W], F32, "rT")
        Bh = M.tile([128, 4, W], F32, "Bh")
        Kh = M.tile([128, 4, W], F32, "Kh")
        hl = {}
        for nm in ("aT", "bT", "kT", "rT"):
            hl[nm] = (M.tile([128, 4, W], BF16, nm + "h"), M.tile([128, 4, W], BF16, nm + "l"))
        rTl = M.tile([64, 8, W], F32, "rTl")
        gltm = M.tile([64, 8, NQ], F32, "gltm")
        dgt = M.ring(2, [64, 8, 64], F32, "dgt")
        bonus = M.tile([128, 4, W], F32, "bonus")
        yT = M.tile([128, 4, W], F32, "yT")
        ymt = M.tile([128, 4, W], BF16, "ymt")
        tA = M.ring(2, [128, W], F32, "tA")
        tB = M.ring(2, [128, W], F32, "tB")
        tbf = M.ring(2, [128, W], BF16, "tbf")
        glt = M.tile([128, 4, NQ], F32, "glt")
        if l > 0:
            vb = M.tile([128, 4, W], BF16, "vb")
            vd = M.tile([32, W], BF16, "vd")
            vf = M.tile([128, 4, W], F32, "vf")
        CAT = M.ring(1, [64, 8, 128], F32, "CAT")
        Btr = M.ring(1, [64, 8, 64], F32, "Bt")
        Ktr = M.ring(1, [64, 8, 64], F32, "Kt")
        Vtr = M.ring(1, [64, 8, 64], F32, "Vt")
        Pr = M.ring(2, [64, 8, 64], F32, "P")
        PTr = M.ring(2, [64, 8, 64], F32, "PT")
        TTr = M.ring(2, [64, 8, 64], F32, "TT")
        Akr = M.ring(1, [64, 8, 64], F32, "AkT")
        Rbr = M.ring(1, [64, 8, 64], F32, "RbT")
        Rkr = M.ring(1, [64, 8, 64], F32, "RkT")
        AUr = M.ring(1, [64, 8, 128], F32, "AU")
        RhTr = M.ring(1, [64, 8, 64], F32, "RhT")
        Olr = M.ring(1, [64, 8, 64], F32, "Ol")
        Phr = M.ring(1, [64, 8, 64], F32, "Ph")
        dSr = M.ring(1, [64, 8, 64], F32, "dS")
        o_r = M.ring(2, [64, 8, 64], F32, "o")
        osq = M.tile([64, 8, 64], F32, "osq")
        st1 = M.ring(2, [64, 8], F32, "s1")
        st2 = M.ring(2, [64, 8], F32, "s2")
        st3 = M.ring(2, [64, 8], F32, "s3")
        onr = M.ring(1, [64, 8, 64], F32, "on")
        psn = C.psum.next
        h8 = lambda ap: ap.rearrange("p (h d) -> p h d", h=8)

        def pcopy(dst, src, reads, writes):
            copy_op(C, evac_eng(C), dst, src, reads, writes)

        for bk in range(NBK):
            t0 = bk * W
            if t0 == 0:
                S.op("pool", lambda e: e.memset(ut[:, :, 0:1], 0.0), writes=[ut.b])
                S.dma("sp", lambda e: e.dma_start(out=ut[:, :, 1:W + 1], in_=C.UT[R0:R0 + 1792, 0:W].rearrange("(c p) t -> p c t", p=128)), reads=[C.bUT], writes=[ut.b])
            else:
                S.dma("sp", lambda e, t0=t0: e.dma_start(out=ut[:], in_=C.UT[R0:R0 + 1792, t0 - 1:t0 + W].rearrange("(c p) t -> p c t", p=128)), reads=[C.bUT], writes=[ut.b])
            for c in range(14):
                eng = "dve" if c % 2 == 0 else "pool"
                S.op("dve", lambda e, c=c: e.tensor_tensor(out=dsh[:], in0=ut[:, c, 0:W], in1=ut[:, c, 1:W + 1], op=ALU.subtract), reads=[ut.b], writes=[dsh.b])
                S.op("dve", lambda e, c=c: e.scalar_tensor_tensor(out=us[:, c, :], in0=dsh[:], scalar=mu[:, c:c + 1], in1=ut[:, c, 1:W + 1], op0=ALU.mult, op1=ALU.add),
                     reads=[dsh.b, mu.b, ut.b], writes=[us.b])
            dump(C, "us", us[:], us.b)
            chk(1)
            S.op("act", lambda e: e.activation(out=twad[0:64, :], in_=us[0:64, 12, :], func=AF.Tanh), reads=[us.b], writes=[twad.b])
            S.op("act", lambda e: e.copy(out=twad[64:128, :], in_=us[64:128, 12, :]), reads=[us.b], writes=[twad.b])
            S.op("act", lambda e: e.activation(out=sgd[:], in_=us[:, 13, :], func=AF.Sigmoid), reads=[us.b], writes=[sgd.b])
            for c in range(4):
                csl = slice(c * 128, (c + 1) * 128)
                p1 = psn()
                S.op("pe", lambda e, p1=p1, csl=csl: e.matmul(p1[:, 0:W], lhsT=wa[0:64, csl], rhs=twad[0:64, :], start=True, stop=True), reads=[wa.b, twad.b], writes=[p1.b])
                S.op("act", lambda e, p1=p1, c=c: e.activation(out=lw[:, c, :], in_=p1[:, 0:W], func=AF.Sigmoid, bias=w0[:, c:c + 1]), reads=[p1.b, w0.b], writes=[lw.b])
                p2 = psn()
                S.op("pe", lambda e, p2=p2, csl=csl: e.matmul(p2[:, 0:W], lhsT=wa[64:128, csl], rhs=twad[64:128, :], start=True, stop=True), reads=[wa.b, twad.b], writes=[p2.b])
                S.op("act", lambda e, p2=p2, c=c: e.activation(out=at_[:, c, :], in_=p2[:, 0:W], func=AF.Sigmoid, bias=a0[:, c:c + 1]), reads=[p2.b, a0.b], writes=[at_.b])
                p3 = psn()
                S.op("pe", lambda e, p3=p3, csl=csl: e.matmul(p3[:, 0:W], lhsT=gu[:, csl], rhs=sgd[:], start=True, stop=True), reads=[gu.b, sgd.b], writes=[p3.b])
                S.op("dve", lambda e, p3=p3, c=c: e.tensor_copy(out=gt_[:, c, :], in_=p3[:, 0:W]), reads=[p3.b], writes=[gt_.b])
            chk(2)
            if l == 0:
                S.dma("sp", lambda e, t0=t0: e.dma_start(out=C.VF[:, t0:t0 + W].rearrange("(c p) t -> p c t", p=128), in_=us[:, 8:12, :]), reads=[us.b], writes=[C.bVF])
            else:
                S.dma("sp", lambda e, t0=t0: e.dma_start(out=vf[:], in_=C.VF[:, t0:t0 + W].rearrange("(c p) t -> p c t", p=128)), reads=[C.bVF], writes=[vf.b])
                S.op("act", lambda e: e.copy(out=vb[:], in_=us[:, 8:12, :]), reads=[us.b], writes=[vb.b])
                pv = psn()
                for c in range(4):
                    S.op("pe", lambda e, pv=pv, c=c: e.matmul(pv[0:32, 0:W], lhsT=vdn[:, c, :], rhs=vb[:, c, :], start=(c == 0), stop=(c == 3)), reads=[vdn.b, vb.b], writes=[pv.b])
                S.op("dve", lambda e, pv=pv: e.tensor_copy(out=vd[:], in_=pv[0:32, 0:W]), reads=[pv.b], writes=[vd.b])
                for c in range(4):
                    csl = slice(c * 128, (c + 1) * 128)
                    p4 = psn()
                    S.op("pe", lambda e, p4=p4, csl=csl: e.matmul(p4[:, 0:W], lhsT=vup[0:32, csl], rhs=vd[0:32, :], start=True, stop=True), reads=[vup.b, vd.b], writes=[p4.b])
                    sv = tA.next()
                    S.op("act", lambda e, p4=p4, sv=sv, c=c: e.activation(out=sv[:], in_=p4[:, 0:W], func=AF.Sigmoid, bias=v0[:, c:c + 1]), reads=[p4.b, v0.b], writes=[sv.b])
                    dl = tB.next()
                    S.op("dve", lambda e, dl=dl, c=c: e.tensor_tensor(out=dl[:], in0=vf[:, c, :], in1=us[:, 8 + c, :], op=ALU.subtract), reads=[vf.b, us.b], writes=[dl.b])
                    S.op("dve", lambda e, dl=dl, sv=sv: e.tensor_tensor(out=dl[:], in0=dl[:], in1=sv[:], op=ALU.mult), reads=[dl.b, sv.b], writes=[dl.b])
                    S.op("dve", lambda e, dl=dl, c=c: e.tensor_tensor(out=us[:, 8 + c, :], in0=us[:, 8 + c, :], in1=dl[:], op=ALU.add), reads=[dl.b, us.b], writes=[us.b])
            chk(3)
            for c in range(4):
                r_c = us[:, c, :]
                k_c = us[:, 4 + c, :]
                v_c = us[:, 8 + c, :]
                S.op("dve", lambda e, c=c, k_c=k_c: e.tensor_scalar(out=kkn[:, c, :], in0=k_c, scalar1=kkw[:, c:c + 1], scalar2=None, op0=ALU.mult), reads=[us.b, kkw.b], writes=[kkn.b])
                sq = tbf.next()
                S.op("act", lambda e, sq=sq, c=c: e.activation(out=sq[:], in_=kkn[:, c, :], func=AF.Square), reads=[kkn.b], writes=[sq.b])
                p5 = psn()
                S.op("pe", lambda e, p5=p5, sq=sq: e.matmul(p5[:, 0:W], lhsT=blk[:], rhs=sq[:], start=True, stop=True), reads=[blk.b, sq.b], writes=[p5.b])
                rk = tA.next()
                S.op("dve", lambda e, rk=rk, p5=p5: e.tensor_scalar(out=rk[:], in0=p5[:, 0:W], scalar1=1e-24, scalar2=None, op0=ALU.max), reads=[p5.b], writes=[rk.b])
                S.op("act", lambda e, rk=rk: e.activation(out=rk[:], in_=rk[:], func=AF.Ln), reads=[rk.b], writes=[rk.b])
                S.op("act", lambda e, rk=rk: e.activation(out=rk[:], in_=rk[:], func=AF.Exp, scale=-0.5), reads=[rk.b], writes=[rk.b])
                S.op("dve", lambda e, rk=rk, c=c: e.tensor_tensor(out=kkn[:, c, :], in0=kkn[:, c, :], in1=rk[:], op=ALU.mult), reads=[rk.b, kkn.b], writes=[kkn.b])
                t1 = tB.next()
                S.op("dve", lambda e, t1=t1, c=c: e.tensor_scalar(out=t1[:], in0=at_[:, c, :], scalar1=-1.0, scalar2=ka[:, c:c + 1], op0=ALU.add, op1=ALU.mult), reads=[at_.b, ka.b], writes=[t1.b])
                S.op("dve", lambda e, t1=t1, c=c, k_c=k_c: e.scalar_tensor_tensor(out=kp[:, c, :], in0=t1[:], scalar=1.0, in1=k_c, op0=ALU.add, op1=ALU.mult), reads=[t1.b, us.b], writes=[kp.b])
                t2 = tA.next()
                S.op("dve", lambda e, t2=t2, c=c, r_c=r_c: e.tensor_tensor(out=t2[:], in0=r_c, in1=kp[:, c, :], op=ALU.mult), reads=[us.b, kp.b], writes=[t2.b])
                t2b = tbf.next()
                S.op("dve", lambda e, t2=t2, t2b=t2b, c=c: e.tensor_scalar(out=t2b[:], in0=t2[:], scalar1=rkv[:, c:c + 1], scalar2=None, op0=ALU.mult), reads=[t2.b, rkv.b], writes=[t2b.b])
                p6 = psn()
                S.op("pe", lambda e, p6=p6, t2b=t2b: e.matmul(p6[:, 0:W], lhsT=blk[:], rhs=t2b[:], start=True, stop=True), reads=[blk.b, t2b.b], writes=[p6.b])
                S.op("dve", lambda e, p6=p6, c=c, v_c=v_c: e.tensor_tensor(out=bonus[:, c, :], in0=p6[:, 0:W], in1=v_c, op=ALU.mult), reads=[p6.b, us.b], writes=[bonus.b])
                S.op("dve", lambda e, c=c: e.tensor_scalar(out=lw[:, c, :], in0=lw[:, c, :], scalar1=-0.6065306597126334, scalar2=None, op0=ALU.mult), reads=[lw.b], writes=[lw.b])
                S.op("dve", lambda e, c=c: e.tensor_tensor_scan(out=cs[:, c, :], data0=m0[:], data1=lw[:, c, :], initial=0.0, op0=ALU.mult, op1=ALU.add), reads=[lw.b, m0.b], writes=[cs.b])
                gx = tA.next()
                S.op("act", lambda e, gx=gx, c=c: e.activation(out=gx[:], in_=cs[:, c, :], func=AF.Exp), reads=[cs.b], writes=[gx.b])
                S.op("dve", lambda e, gx=gx, c=c, r_c=r_c: e.tensor_tensor(out=rT[:, c, :], in0=r_c, in1=gx[:], op=ALU.mult), reads=[gx.b, us.b], writes=[rT.b])
                ge = tB.next()
                S.op("dve", lambda e, ge=ge, c=c: e.tensor_tensor(out=ge[:], in0=cs[:, c, :], in1=lw[:, c, :], op=ALU.subtract), reads=[cs.b, lw.b], writes=[ge.b])
                S.op("act", lambda e, ge=ge: e.activation(out=ge[:], in_=ge[:], func=AF.Exp), reads=[ge.b], writes=[ge.b])
                S.op("dve", lambda e, ge=ge, c=c: e.scalar_tensor_tensor(out=aT[:, c, :], in0=kkn[:, c, :], scalar=-1.0, in1=ge[:], op0=ALU.mult, op1=ALU.mult), reads=[ge.b, kkn.b], writes=[aT.b])
                gi = tA.next()
                S.op("act", lambda e, gi=gi, c=c: e.activation(out=gi[:], in_=cs[:, c, :], func=AF.Exp, scale=-1.0), reads=[cs.b], writes=[gi.b])
                kb = tB.next()
                S.op("dve", lambda e, kb=kb, c=c: e.tensor_tensor(out=kb[:], in0=kkn[:, c, :], in1=at_[:, c, :], op=ALU.mult), reads=[kkn.b, at_.b], writes=[kb.b])
                S.op("dve", lambda e, kb=kb, gi=gi, c=c: e.tensor_tensor(out=bT[:, c, :], in0=kb[:], in1=gi[:], op=ALU.mult), reads=[kb.b, gi.b], writes=[bT.b])
                S.op("dve", lambda e, gi=gi, c=c: e.tensor_tensor(out=kT[:, c, :], in0=kp[:, c, :], in1=gi[:], op=ALU.mult), reads=[kp.b, gi.b], writes=[kT.b])
                e2 = tA.next()
                for q in range(NQ):
                    qs = slice(q * L, (q + 1) * L)
                    qe = (q + 1) * L - 1
                    S.op("act", lambda e, e2=e2, c=c, qs=qs, qe=qe: e.activation(out=e2[:, qs], in_=cs[:, c, qs], func=AF.Exp, scale=-1.0, bias=cs[:, c, qe:qe + 1]), reads=[cs.b], writes=[e2.b])
                S.op("dve", lambda e, e2=e2, kb=kb, c=c: e.tensor_tensor(out=Bh[:, c, :], in0=kb[:], in1=e2[:], op=ALU.mult), reads=[kb.b, e2.b], writes=[Bh.b])
                S.op("dve", lambda e, e2=e2, c=c: e.tensor_tensor(out=Kh[:, c, :], in0=kp[:, c, :], in1=e2[:], op=ALU.mult), reads=[kp.b, e2.b], writes=[Kh.b])
                S.op("act", lambda e, c=c: e.activation(out=glt[:, c, :], in_=cs[:, c, :].rearrange("p (q t) -> p q t", t=L)[:, :, L - 1], func=AF.Exp), reads=[cs.b], writes=[glt.b])
                for (nm, src) in (("aT", aT), ("bT", bT), ("kT", kT), ("rT", rT)):
                    hi, lo = hl[nm]
                    S.op("act", lambda e, hi=hi, src=src, c=c: e.copy(out=hi[:, c, :], in_=src[:, c, :]), reads=[src.b], writes=[hi.b])
                    S.op("pool", lambda e, hi=hi, lo=lo, src=src, c=c: e.tensor_tensor(out=lo[:, c, :], in0=src[:, c, :], in1=hi[:, c, :], op=ALU.subtract), reads=[src.b, hi.b], writes=[lo.b])
                for j in range(2):
                    pr_ = psn()
                    S.op("pe", lambda e, pr_=pr_, c=c, j=j: e.matmul(pr_[0:64, 0:W], lhsT=C.identf[:, j * 64:(j + 1) * 64], rhs=rT[:, c, :], start=True, stop=True), reads=[C.identf.b, rT.b], writes=[pr_.b])
                    pcopy(rTl[:, 2 * c + j, :], pr_[0:64, 0:W], [pr_.b], [rTl.b])
            for j in range(2):
                pg_ = psn()
                S.op("pe", lambda e, pg_=pg_, j=j: e.matmul(pg_[0:64, 0:4 * NQ], lhsT=C.identf[:, j * 64:(j + 1) * 64], rhs=glt[:].rearrange("p c q -> p (c q)"), start=True, stop=True), reads=[C.identf.b, glt.b], writes=[pg_.b])
                for c in range(4):
                    pcopy(gltm[:, 2 * c + j, :], pg_[0:64, c * NQ:(c + 1) * NQ], [pg_.b], [gltm.b])
            for nm, tl in (("lw", lw), ("a", at_), ("g", gt_), ("kkn", kkn), ("kp", kp), ("bonus", bonus), ("cs", cs), ("aT", aT), ("bT", bT), ("kT", kT), ("rT", rT), ("Bh", Bh), ("Kh", Kh), ("rTl", rTl), ("gltm", gltm)):
                dump(C, nm, tl[:], tl.b)
            for nm in ("aT", "bT"):
                dump(C, nm + "h", hl[nm][0][:], hl[nm][0].b)
                dump(C, nm + "l", hl[nm][1][:], hl[nm][1].b)
            chk(4)
            for q in range(NQ):
                sl = slice(q * L, (q + 1) * L)
                cat = CAT.next()
                Bt = Btr.next()
                Kt = Ktr.next()
                Vt = Vtr.next()
                for (src, sbuf_, dst, dtl) in ((aT, aT.b, cat[:, :, 0:64], cat), (Bh, Bh.b, Bt[:], Bt), (Kh, Kh.b, Kt[:], Kt), (None, us.b, Vt[:], Vt)):
                    pt = psn()
                    for c in range(4):
                        inp = us[:, 8 + c, sl] if src is None else src[:, c, sl]
                        S.op("pe", lambda e, pt=pt, c=c, inp=inp: e.matmul(pt[0:64, c * 128:(c + 1) * 128], lhsT=inp, rhs=C.identf[:], start=True, stop=True), reads=[sbuf_, C.identf.b], writes=[pt.b])
                    pcopy(dst, h8(pt[0:64, :]), [pt.b], [dtl.b])
                chk(5)
                P = Pr.next()
                PT = PTr.next()
                TT = TTr.next()
                AkT = Akr.next()
                RbT = Rbr.next()
                RkT = Rkr.next()
                for (ln_, rn_, msk, dst) in (("bT", "aT", mUs, PT), ("aT", "bT", mLs, P), ("kT", "aT", mUs, AkT), ("bT", "rT", mUi, RbT), ("kT", "rT", mUi, RkT)):
                    lh, ll = hl[ln_]
                    rh, rl = hl[rn_]
                    dv = dst[:].rearrange("p (c j) d -> p c j d", j=2)
                    for j in range(2):
                        pa = psn()
                        hp = slice(j * 64, j * 64 + 64)
                        for c in range(4):
                            for ii, (x_, y_) in enumerate(((lh, rh), (lh, rl), (ll, rh))):
                                S.op("pe", lambda e, pa=pa, c=c, hp=hp, x_=x_, y_=y_, ii=ii, sl=sl: e.matmul(pa[0:64, c * 64:(c + 1) * 64], lhsT=x_[hp, c, sl], rhs=y_[hp, c, sl], start=(ii == 0), stop=(ii == 2)),
                                     reads=[x_.b, y_.b], writes=[pa.b])
                        S.op("dve", lambda e, pa=pa, msk=msk, dv=dv, j=j: e.tensor_tensor(out=dv[:, :, j, :], in0=pa[0:64, 0:256].rearrange("p (h d) -> p h d", h=4), in1=msk[:, 0:4, :], op=ALU.mult),
                             reads=[pa.b, msk.b], writes=[dst.b])
                S.op("pool", lambda e, TT=TT, PT=PT: e.tensor_tensor(out=TT[:], in0=PT[:], in1=mId[:], op=ALU.add), reads=[PT.b, mId.b], writes=[TT.b])
                for nm, tl in (("P0", P), ("PT0", PT), ("TT0", TT), ("AkT", AkT), ("RbT", RbT), ("RkT", RkT), ("cat0", cat), ("Bt", Bt), ("Kt", Kt), ("Vt", Vt)):
                    dump(C, nm, tl[:], tl.b)
                chk(6)
                for i in range(1, 6):
                    pP = psn()
                    for h in range(8):
                        S.op("pe", lambda e, pP=pP, h=h, P=P, PT=PT: e.matmul(pP[0:64, h * 64:(h + 1) * 64], lhsT=PT[:, h, :], rhs=P[:, h, :], start=True, stop=True), reads=[P.b, PT.b], writes=[pP.b])
                    Pn = Pr.next()
                    pcopy(Pn[:], h8(pP[0:64, :]), [pP.b], [Pn.b])
                    if i < 5:
                        pPT = psn()
                        for h in range(8):
                            S.op("pe", lambda e, pPT=pPT, h=h, P=P, PT=PT: e.matmul(pPT[0:64, h * 64:(h + 1) * 64], lhsT=P[:, h, :], rhs=PT[:, h, :], start=True, stop=True), reads=[P.b, PT.b], writes=[pPT.b])
                        PTn = PTr.next()
                        pcopy(PTn[:], h8(pPT[0:64, :]), [pPT.b], [PTn.b])
                    else:
                        PTn = PT
                    pTT = psn()
                    for h in range(8):
                        S.op("pe", lambda e, pTT=pTT, h=h, Pn=Pn, TT=TT: e.matmul(pTT[0:64, h * 64:(h + 1) * 64], lhsT=Pn[:, h, :], rhs=TT[:, h, :], start=True, stop=True), reads=[Pn.b, TT.b], writes=[pTT.b])
                    TTn = TTr.next()
                    S.op("dve", lambda e, pTT=pTT, TT=TT, TTn=TTn: e.tensor_tensor(out=TTn[:], in0=h8(pTT[0:64, :]), in1=TT[:], op=ALU.add), reads=[pTT.b, TT.b], writes=[TTn.b])
                    P, PT, TT = Pn, PTn, TTn
                chk(7)
                pW = psn()
                for h in range(8):
                    S.op("pe", lambda e, pW=pW, h=h, AkT=AkT, Vt=Vt: e.matmul(pW[0:64, h * 64:(h + 1) * 64], lhsT=AkT[:, h, :], rhs=Vt[:, h, :], start=True, stop=True), reads=[AkT.b, Vt.b], writes=[pW.b])
                pcopy(cat[:, :, 64:128], h8(pW[0:64, :]), [pW.b], [cat.b])
                AU = AUr.next()
                for hb in range(2):
                    pAU = psn()
                    for hh in range(4):
                        h = hb * 4 + hh
                        S.op("pe", lambda e, pAU=pAU, hh=hh, h=h, TT=TT, cat=cat: e.matmul(pAU[0:64, hh * 128:(hh + 1) * 128], lhsT=TT[:, h, :], rhs=cat[:, h, :], start=True, stop=True), reads=[TT.b, cat.b], writes=[pAU.b])
                    pcopy(AU[:, hb * 4:(hb + 1) * 4, :], pAU[0:64, :].rearrange("p (h d) -> p h d", h=4), [pAU.b], [AU.b])
                RhT = RhTr.next()
                Ol = Olr.next()
                Ph = Phr.next()
                dS = dSr.next()
                pR = psn()
                for h in range(8):
                    c = h // 2
                    hp = slice((h % 2) * 64, (h % 2) * 64 + 64)
                    S.op("pe", lambda e, pR=pR, h=h, AU=AU, RbT=RbT: e.matmul(pR[0:64, h * 64:(h + 1) * 64], lhsT=AU[:, h, 0:64], rhs=RbT[:, h, :], start=True, stop=True), reads=[AU.b, RbT.b], writes=[pR.b])
                S.op("dve", lambda e, pR=pR, RhT=RhT, sl=sl: e.tensor_tensor(out=RhT[:], in0=h8(pR[0:64, :]), in1=rTl[:, :, sl], op=ALU.add), reads=[pR.b, rTl.b], writes=[RhT.b])
                pO = psn()
                for h in range(8):
                    S.op("pe", lambda e, pO=pO, h=h, AU=AU, RbT=RbT: e.matmul(pO[0:64, h * 64:(h + 1) * 64], lhsT=RbT[:, h, :], rhs=AU[:, h, 64:128], start=True, stop=False), reads=[AU.b, RbT.b], writes=[pO.b])
                    S.op("pe", lambda e, pO=pO, h=h, RkT=RkT, Vt=Vt: e.matmul(pO[0:64, h * 64:(h + 1) * 64], lhsT=RkT[:, h, :], rhs=Vt[:, h, :], start=False, stop=True), reads=[RkT.b, Vt.b], writes=[pO.b])
                pcopy(Ol[:], h8(pO[0:64, :]), [pO.b], [Ol.b])
                pF = psn()
                for h in range(8):
                    c = h // 2
                    hp = slice((h % 2) * 64, (h % 2) * 64 + 64)
                    S.op("pe", lambda e, pF=pF, h=h, AU=AU, Bt=Bt: e.matmul(pF[0:64, h * 64:(h + 1) * 64], lhsT=AU[:, h, 0:64], rhs=Bt[:, h, :], start=True, stop=True), reads=[AU.b, Bt.b], writes=[pF.b])
                dgq = dgt.next()
                S.op("pool", lambda e, dgq=dgq, q=q: e.tensor_tensor(out=dgq[:], in0=mId[:], in1=bc_last(gltm[:, :, q], [64, 8, 64]), op=ALU.mult), reads=[mId.b, gltm.b], writes=[dgq.b])
                S.op("dve", lambda e, pF=pF, Ph=Ph, dgq=dgq: e.tensor_tensor(out=Ph[:], in0=h8(pF[0:64, :]), in1=dgq[:], op=ALU.add), reads=[pF.b, dgq.b], writes=[Ph.b])
                pD = psn()
                for h in range(8):
                    S.op("pe", lambda e, pD=pD, h=h, AU=AU, Bt=Bt: e.matmul(pD[0:64, h * 64:(h + 1) * 64], lhsT=Bt[:, h, :], rhs=AU[:, h, 64:128], start=True, stop=False), reads=[AU.b, Bt.b], writes=[pD.b])
                    S.op("pe", lambda e, pD=pD, h=h, Kt=Kt, Vt=Vt: e.matmul(pD[0:64, h * 64:(h + 1) * 64], lhsT=Kt[:, h, :], rhs=Vt[:, h, :], start=False, stop=True), reads=[Kt.b, Vt.b], writes=[pD.b])
                pcopy(dS[:], h8(pD[0:64, :]), [pD.b], [dS.b])
                for nm, tl in (("TT", TT), ("AU", AU), ("RhT", RhT), ("Ol", Ol), ("Ph", Ph), ("dS", dS)):
                    dump(C, nm, tl[:], tl.b)
                chk(8)
                pY = psn()
                for h in range(8):
                    S.op("pe", lambda e, pY=pY, h=h, RhT=RhT, sst=sst: e.matmul(pY[0:64, h * 64:(h + 1) * 64], lhsT=RhT[:, h, :], rhs=sst[:, h, :], start=True, stop=True), reads=[RhT.b, sst.b], writes=[pY.b])
                o = o_r.next()
                S.op("dve", lambda e, pY=pY, o=o, Ol=Ol: e.tensor_tensor(out=o[:], in0=h8(pY[0:64, :]), in1=Ol[:], op=ALU.add), reads=[pY.b, Ol.b], writes=[o.b])
                pS = psn()
                for h in range(8):
                    S.op("pe", lambda e, pS=pS, h=h, Ph=Ph, sst=sst: e.matmul(pS[0:64, h * 64:(h + 1) * 64], lhsT=Ph[:, h, :], rhs=sst[:, h, :], start=True, stop=True), reads=[Ph.b, sst.b], writes=[pS.b])
                nsst = Sst.next()
                S.op("dve", lambda e, pS=pS, nsst=nsst, dS=dS: e.tensor_tensor(out=nsst[:], in0=h8(pS[0:64, :]), in1=dS[:], op=ALU.add), reads=[pS.b, dS.b], writes=[nsst.b])
                sst = nsst
                dump(C, "o", o[:], o.b)
                dump(C, "S1", sst[:], sst.b)
                chk(9)
                s1 = st1.next()
                s2 = st2.next()
                s3 = st3.next()
                S.op("dve", lambda e, s1=s1, o=o: e.tensor_reduce(out=s1[:], in_=o[:], axis=AX.X, op=ALU.add), reads=[o.b], writes=[s1.b])
                S.op("act", lambda e, o=o: e.activation(out=osq[:], in_=o[:], func=AF.Square), reads=[o.b], writes=[osq.b])
                S.op("dve", lambda e, s2=s2: e.tensor_reduce(out=s2[:], in_=osq[:], axis=AX.X, op=ALU.add), reads=[osq.b], writes=[s2.b])
                S.op("dve", lambda e, s1=s1: e.tensor_scalar(out=s1[:], in0=s1[:], scalar1=1.0 / 64, scalar2=None, op0=ALU.mult), reads=[s1.b], writes=[s1.b])
                S.op("dve", lambda e, s1=s1, s3=s3: e.tensor_tensor(out=s3[:], in0=s1[:], in1=s1[:], op=ALU.mult), reads=[s1.b], writes=[s3.b])
                S.op("dve", lambda e, s2=s2, s3=s3: e.scalar_tensor_tensor(out=s2[:], in0=s2[:], scalar=1.0 / 64, in1=s3[:], op0=ALU.mult, op1=ALU.subtract), reads=[s2.b, s3.b], writes=[s2.b])
                S.op("act", lambda e, s2=s2: e.activation(out=s2[:], in_=s2[:], func=AF.Ln, bias=64e-5), reads=[s2.b], writes=[s2.b])
                S.op("act", lambda e, s2=s2: e.activation(out=s2[:], in_=s2[:], func=AF.Exp, scale=-0.5), reads=[s2.b], writes=[s2.b])
                on = onr.next()
                S.op("dve", lambda e, on=on, o=o, s1=s1: e.tensor_tensor(out=on[:], in0=o[:], in1=bc_last(s1[:, 0:8], [64, 8, 64]), op=ALU.subtract), reads=[o.b, s1.b], writes=[on.b])
                S.op("dve", lambda e, on=on, s2=s2: e.tensor_tensor(out=on[:], in0=on[:], in1=bc_last(s2[:, 0:8], [64, 8, 64]), op=ALU.mult), reads=[on.b, s2.b], writes=[on.b])
                chk(10)
                pyt = psn()
                for c in range(4):
                    S.op("pe", lambda e, pyt=pyt, c=c, on=on: e.matmul(pyt[:, c * 64:(c + 1) * 64], lhsT=on[:, 2 * c:2 * c + 2, :].rearrange("p h d -> p (h d)"), rhs=C.identf[0:64, 0:64], start=True, stop=True),
                         reads=[on.b, C.identf.b], writes=[pyt.b])
                pcopy(yT[:, :, sl], pyt[:, 0:256].rearrange("p (c t) -> p c t", c=4), [pyt.b], [yT.b])
            for c in range(4):
                S.op("dve", lambda e, c=c: e.tensor_scalar(out=yT[:, c, :], in0=yT[:, c, :], scalar1=lnw[:, c:c + 1], scalar2=lnb[:, c:c + 1], op0=ALU.mult, op1=ALU.add),
                     reads=[yT.b, lnw.b, lnb.b], writes=[yT.b])
            S.op("pool", lambda e: e.tensor_tensor(out=yT[:], in0=yT[:], in1=bonus[:], op=ALU.add), reads=[yT.b, bonus.b], writes=[yT.b])
            S.op("dve", lambda e: e.tensor_tensor(out=ymt[:], in0=yT[:], in1=gt_[:], op=ALU.mult), reads=[yT.b, gt_.b], writes=[ymt.b])
            S.dma("sp", lambda e, t0=t0: e.dma_start(out=C.YMT[1536:2048, t0:t0 + W].rearrange("(c p) t -> p c t", p=128), in_=ymt[:]), reads=[ymt.b], writes=[C.bYMT])


_NC_CACHE = {}


def kernel(**inputs):
    x = np.asarray(inputs["x"], dtype=np.float32)
    mem = np.asarray(inputs["mem"], dtype=np.float32)
    B = x.shape[0]
    n = 8
    if "nc" not in _NC_CACHE:
        _NC_CACHE["nc"] = build_program()
    nc = _NC_CACHE["nc"]
    in_maps = []
    for c in range(n):
        b = c % B
        m = {"x": np.ascontiguousarray(x[b]), "mem": np.ascontiguousarray(mem[b])}
        for k in WEIGHT_NAMES:
            m[k] = np.asarray(inputs[k], dtype=np.float32)
        in_maps.append(m)
    res = run_bass_kernel_spmd(nc, in_maps, core_ids=list(range(n)))
    out = np.stack([res.results[b]["out"] for b in range(B)], axis=0)
    return out.astype(np.float32)
```

```python
import contextlib
import bisect
import numpy as np
import concourse.bass as bass
import concourse.mybir as mybir
from concourse.bass_utils import run_bass_kernel_spmd

AF = mybir.ActivationFunctionType
ALU = mybir.AluOpType
AX = mybir.AxisListType
F32 = mybir.dt.float32
BF16 = mybir.dt.bfloat16

D = 2048
KC = 16
DFF = 5632
NFC = 44
NIN = 5904
MEM = 256
EPS = 1e-5
DEPTH_FULL = 4
T_FULL = 4096
O_Z, O_X, O_B, O_C, O_DT = 0, 1024, 2048, 2304, 2560
O_SB = 2576
O_RW = 4112

WEIGHT_NAMES = [
    "ffn1_norm", "ffn1_w_gate", "ffn1_w_up", "ffn1_w_down", "mix_norm", "w_in", "w_out",
    "ssd_conv_w", "ssd_conv_b", "ssd_dt_bias", "ssd_a_log", "ssd_d", "ssd_norm", "sb_norm",
    "rw_mu", "rw_w0", "rw_w_up", "rw_a0", "rw_a_up", "rw_g_up", "rw_v0", "rw_v_down", "rw_v_up",
    "rw_k_k", "rw_k_a", "rw_r_k", "rw_ln_w", "rw_ln_b",
    "cross_norm", "cross_wq", "cross_wk", "cross_wv", "cross_wo",
    "ffn2_norm", "ffn2_w_gate", "ffn2_w_up", "ffn2_w_down", "mem_norm", "final_norm"]


class Buf:
    __slots__ = ("w", "r")

    def __init__(self):
        self.w = None
        self.r = {}


class Sched:
    ENGS = ("pe", "act", "dve", "pool", "sp")
    NDMA = 8

    def __init__(self, nc, stack):
        self.nc = nc
        self.q = {e: [] for e in self.ENGS}
        self.n = {e: 0 for e in self.ENGS}
        self.need = {e: set() for e in self.ENGS}
        self.sems = {}
        self.cnt = {}
        for e in self.ENGS:
            self.sems[e] = stack.enter_context(nc.semaphore("s_" + e))
        self.dma_pool = {}
        self.dma_rr = {}
        for e in ("sp", "pool", "act"):
            keys = []
            for i in range(self.NDMA):
                k = "d_%s%d" % (e, i)
                self.sems[k] = stack.enter_context(nc.semaphore(k))
                self.cnt[k] = 0
                keys.append(k)
            self.dma_pool[e] = keys
            self.dma_rr[e] = 0
        self.seen = {e: {} for e in self.ENGS}
        self.out_tokens = []

    def _waits(self, eng, reads, writes):
        need = {}

        def add(tok):
            if tok is None:
                return
            k, v = tok
            if k == "pe" and eng == "pe":
                return
            if need.get(k, 0) < v:
                need[k] = v
        for b in reads:
            add(b.w)
        for b in writes:
            add(b.w)
            for t in b.r.items():
                add(t)
        out = []
        seen = self.seen[eng]
        for k, v in need.items():
            if seen.get(k, 0) < v:
                seen[k] = v
                out.append((k, v))
                if k in self.need:
                    self.need[k].add(v)
        return out

    def _mark(self, tok, reads, writes):
        k, v = tok
        for b in reads:
            if b.r.get(k, 0) < v:
                b.r[k] = v
        for b in writes:
            b.w = tok
            b.r = {}

    def op(self, eng, fn, reads=(), writes=()):
        waits = self._waits(eng, reads, writes)
        self.n[eng] += 1
        tok = (eng, self.n[eng])
        self.q[eng].append((waits, fn, None, self.n[eng]))
        self._mark(tok, reads, writes)
        return tok

    def dma(self, eng, fn, reads=(), writes=(), is_output=False):
        k = self.dma_pool[eng][self.dma_rr[eng] % self.NDMA]
        self.dma_rr[eng] += 1
        waits = self._waits(eng, reads, writes)
        if self.cnt[k] > 0 and self.seen[eng].get(k, 0) < self.cnt[k]:
            self.seen[eng][k] = self.cnt[k]
            waits.append((k, self.cnt[k]))
        self.cnt[k] += 16
        tok = (k, self.cnt[k])
        self.n[eng] += 1
        self.q[eng].append((waits, fn, k, self.n[eng]))
        self._mark(tok, reads, writes)
        if is_output:
            self.out_tokens.append(tok)
        return tok

    def barrier(self):
        last = []
        for e in self.ENGS:
            if self.n[e] > 0 and self.q[e] and self.q[e][-1][2] is None:
                last.append((e, self.n[e]))
            elif self.n[e] > 0:
                for ent in reversed(self.q[e]):
                    if ent[2] is None:
                        last.append((e, ent[3]))
                        break
        for k, v in self.cnt.items():
            if v > 0:
                last.append((k, v))
        for e in self.ENGS:
            waits = []
            for k, v in last:
                if k == e:
                    continue
                if self.seen[e].get(k, 0) < v:
                    self.seen[e][k] = v
                    waits.append((k, v))
                    if k in self.need:
                        self.need[k].add(v)
            self.n[e] += 1
            self.q[e].append((waits, lambda en: en.nop(), None, self.n[e]))

    def finish(self):
        fin = {}
        for k, v in self.out_tokens:
            fin[k] = max(fin.get(k, 0), v)
        sems = self.sems
        q = self.q
        nc = self.nc
        ranks = {e: sorted(self.need[e]) for e in self.ENGS}

        def val(k, v):
            if k in ranks:
                return bisect.bisect_right(ranks[k], v)
            return v

        def replay(engname, engobj):
            needset = self.need[engname]
            for waits, fn, dk, idx in q[engname]:
                for wk, wv in waits:
                    engobj.wait_ge(sems[wk], val(wk, wv))
                ins = fn(engobj)
                if dk is not None:
                    ins.then_inc(sems[dk], 16)
                elif idx in needset:
                    ins.then_inc(sems[engname], 1)

        with nc.Block() as block:
            @block.tensor
            def _(e):
                replay("pe", e)

            @block.scalar
            def _(e):
                replay("act", e)

            @block.vector
            def _(e):
                replay("dve", e)

            @block.gpsimd
            def _(e):
                replay("pool", e)

            @block.sync
            def _(e):
                replay("sp", e)
                for k, v in fin.items():
                    e.wait_ge(sems[k], v)


class TL:
    __slots__ = ("t", "b")

    def __init__(self, t):
        self.t = t
        self.b = Buf()

    def __getitem__(self, k):
        return self.t[k]


class Ring:
    def __init__(self, tiles):
        self.tiles = tiles
        self.i = 0

    def next(self):
        t = self.tiles[self.i % len(self.tiles)]
        self.i += 1
        return t


class Ctx:
    pass


def build_program(T=T_FULL, depth=DEPTH_FULL, dbg=(), stages=None, LW=4, rwp=99, dbg_in=(), dumps=()):
    nc = bass.Bass("TRN2", target_bir_lowering=False)
    C = Ctx()
    C.nc = nc
    C.T = T
    C.TB = min(1024, T)
    C.NTB = T // C.TB
    C.depth = depth
    C.stages = stages
    C.rwp = rwp
    C.dumps = dumps
    C.dumped = set()
    shapes = {
        "x": [T, D], "mem": [MEM, D],
        "ffn1_norm": [4, D], "ffn1_w_gate": [4, D, DFF], "ffn1_w_up": [4, D, DFF], "ffn1_w_down": [4, DFF, D],
        "mix_norm": [4, D], "w_in": [4, D, NIN], "w_out": [4, D, D],
        "ssd_conv_w": [4, 4, 1536], "ssd_conv_b": [4, 1536], "ssd_dt_bias": [4, 16], "ssd_a_log": [4, 16],
        "ssd_d": [4, 16], "ssd_norm": [4, 1024], "sb_norm": [4, 512],
        "rw_mu": [4, 1792], "rw_w0": [4, 512], "rw_w_up": [4, 64, 512], "rw_a0": [4, 512], "rw_a_up": [4, 64, 512],
        "rw_g_up": [4, 128, 512], "rw_v0": [3, 512], "rw_v_down": [3, 512, 32], "rw_v_up": [3, 32, 512],
        "rw_k_k": [4, 512], "rw_k_a": [4, 512], "rw_r_k": [4, 8, 64], "rw_ln_w": [4, 512], "rw_ln_b": [4, 512],
        "cross_norm": [4, D], "cross_wq": [4, D, 512], "cross_wk": [4, D, 512], "cross_wv": [4, D, 512],
        "cross_wo": [4, 512, D],
        "ffn2_norm": [4, D], "ffn2_w_gate": [4, D, DFF], "ffn2_w_up": [4, D, DFF], "ffn2_w_down": [4, DFF, D],
        "mem_norm": [D], "final_norm": [D],
    }
    I = {}
    for k, shp in shapes.items():
        if len(shp) >= 2 and shp[0] in (3, 4) and k not in ("ssd_conv_w",):
            shp = [LW if shp[0] == 4 else max(LW - 1, 1)] + shp[1:]
        elif k == "ssd_conv_w":
            shp = [LW] + shp[1:]
        I[k] = nc.dram_tensor(k, shp, F32, kind="ExternalInput").ap()
    C.I = I
    C.out = nc.dram_tensor("out", [T, D], F32, kind="ExternalOutput").ap()

    def scratch(name, shape, dt):
        kind = "ExternalOutput" if name in dbg else ("ExternalInput" if name in dbg_in else "Internal")
        return nc.dram_tensor(name, shape, dt, kind=kind).ap()
    C.X = scratch("X", [T, D], F32)
    C.UT = scratch("UT", [NIN, T], F32)
    C.SZ = scratch("SZ", [T, 1024], BF16)
    C.VSB = scratch("VSB", [T, 512], BF16)
    C.YMT = scratch("YMT", [D, T], BF16)
    C.VF = scratch("VF", [512, T], F32)
    C.bX = [Buf() for _ in range(C.NTB)]
    C.bUT = Buf()
    C.bSZ = Buf()
    C.bVSB = Buf()
    C.bYMT = Buf()
    C.bVF = Buf()

    with contextlib.ExitStack() as st:
        S = Sched(nc, st)
        C.S = S
        C.st = st
        sbp = lambda name, shape, dt: TL(st.enter_context(nc.sbuf_tensor(name, shape, dt)))
        C.ident = sbp("ident", [128, 128], BF16)
        C.identf = sbp("identf", [128, 128], F32)
        C.ones_bf = sbp("ones_bf", [128, 128], BF16)
        C.memnT = sbp("memnT", [128, KC, MEM], BF16)
        C.psum = Ring([TL(st.enter_context(nc.psum_tensor("ps%d" % i, [128, 512], F32))) for i in range(8)])
        C.evac_i = 0
        for tl, dt in ((C.ident, BF16), (C.identf, F32)):
            S.op("pool", lambda e, tl=tl: e.memset(tl[:], 0.0), writes=[tl.b])
            S.op("pool", lambda e, tl=tl: e.affine_select(out=tl[:], in_=tl[:], compare_op=ALU.not_equal, fill=1.0,
                                                          base=0, pattern=[[-1, 128]], channel_multiplier=1),
                 reads=[tl.b], writes=[tl.b])
        S.op("pool", lambda e: e.memset(C.ones_bf[:], 1.0), writes=[C.ones_bf.b])

        run = lambda name: (stages is None) or (name in stages)
        src = C.I["x"]
        if run("mem"):
            stage_mem(C)
        for l in range(depth):
            if run("ffn1"):
                stage_ffn(C, l, 1, src)
                src = C.X
            if run("inproj"):
                stage_inproj(C, l, src)
            if run("ssd"):
                stage_ssd(C, l)
            if run("sb"):
                stage_sb(C, l)
            if run("rw"):
                stage_rw(C, l)
            if run("outproj"):
                stage_outproj(C, l)
            if run("cross"):
                stage_cross(C, l)
            if run("ffn2"):
                stage_ffn(C, l, 2, src)
        if run("final"):
            stage_final(C, src)
        S.barrier()
        S.finish()
    return nc


class StageMem:
    def __init__(self, C):
        self.C = C
        self.st = contextlib.ExitStack()
        self.k = 0

    def __enter__(self):
        self.st.__enter__()
        return self

    def __exit__(self, *a):
        self.C.S.barrier()
        self.st.__exit__(None, None, None)
        return False

    def tile(self, shape, dt, name=None):
        self.C.tcount = getattr(self.C, "tcount", 0) + 1
        return TL(self.st.enter_context(self.C.nc.sbuf_tensor("t%d_%s" % (self.C.tcount, name or ""), shape, dt)))

    def ring(self, n, shape, dt, name=None):
        return Ring([self.tile(shape, dt, name) for _ in range(n)])


def evac_eng(C):
    C.evac_i += 1
    return "act" if C.evac_i % 2 else "dve"


def copy_op(C, eng, out_ap, in_ap, reads, writes):
    S = C.S
    if eng == "act":
        S.op("act", lambda e: e.copy(out=out_ap, in_=in_ap), reads=reads, writes=writes)
    else:
        S.op(eng, lambda e: e.tensor_copy(out=out_ap, in_=in_ap), reads=reads, writes=writes)


def rstd_from_sumsq(C, ss, n, eps, tmp=None):
    S = C.S
    S.op("act", lambda e: e.activation(out=ss[:], in_=ss[:], func=AF.Ln, scale=1.0 / n, bias=eps), reads=[ss.b], writes=[ss.b])
    S.op("act", lambda e: e.activation(out=ss[:], in_=ss[:], func=AF.Exp, scale=-0.5), reads=[ss.b], writes=[ss.b])


def norm_transpose(C, M, src_ap, t0, ntok, gb, xnT, xnT_bufs, src_bufs, col0=0):
    S = C.S
    for m in range(ntok // 128):
        xt = M.xring.next()
        S.dma("sp", lambda e, xt=xt, m=m: e.dma_start(out=xt[:], in_=src_ap[t0 + m * 128:t0 + (m + 1) * 128, :]),
              reads=src_bufs, writes=[xt.b])
        ss = M.ssring.next()
        junk = M.junk
        S.op("act", lambda e, xt=xt, ss=ss: e.activation(out=junk[:], in_=xt[:], func=AF.Square, accum_out=ss[:]),
             reads=[xt.b], writes=[junk.b, ss.b])
        rstd_from_sumsq(C, ss, D, EPS)
        xn = M.xnring.next()
        S.op("dve", lambda e, xt=xt, ss=ss, xn=xn: e.scalar_tensor_tensor(out=xn[:], in0=xt[:], scalar=ss[:, 0:1], in1=gb[:],
                                                                        op0=ALU.mult, op1=ALU.mult),
             reads=[xt.b, ss.b, gb.b], writes=[xn.b])
        for half in range(2):
            pt = C.psum.next()
            ptv = pt.t[:].bitcast(BF16)
            for j in range(8):
                kc = half * 8 + j
                S.op("pe", lambda e, ptv=ptv, xn=xn, j=j, kc=kc: e.transpose(ptv[:, j * 128:(j + 1) * 128], xn[:, kc * 128:(kc + 1) * 128], C.ident[:]),
                     reads=[xn.b, C.ident.b], writes=[pt.b])
            dst = xnT[:, half * 8:(half + 1) * 8, col0 + m * 128:col0 + (m + 1) * 128]
            copy_op(C, evac_eng(C), dst, ptv.rearrange("p (k t) -> p k t", k=8), [pt.b], [xnT_bufs[m]])


def load_gb(C, M, row_ap):
    gb = M.gb
    C.S.dma("sp", lambda e: e.dma_start(out=gb[:], in_=row_ap.partition_broadcast(128)), writes=[gb.b])
    return gb


def load_wblk(C, M, w_ap, c0, ncols):
    wt = M.wring.next()
    C.S.dma("pool", lambda e: e.dma_start(out=wt[:, :, 0:ncols], in_=w_ap[:, c0:c0 + ncols].rearrange("(k p) f -> p k f", p=128)),
            writes=[wt.b])
    return wt


def mm_fm(C, wt, cs, ncols, xnT, xnT_bufs, tsl):
    S = C.S
    ps = C.psum.next()
    n = tsl.stop - tsl.start
    for kc in range(KC):
        S.op("pe", lambda e, kc=kc, ps=ps: e.matmul(ps[0:ncols, 0:n], lhsT=wt[:, kc, cs:cs + ncols], rhs=xnT[:, kc, tsl],
                                                   start=(kc == 0), stop=(kc == KC - 1)),
             reads=[wt.b] + list(xnT_bufs), writes=[ps.b])
    return ps


def proj_tok(C, M, actT, act_bufs, nch, w_ap, scale, tb, res_src):
    S = C.S
    TB = C.TB
    t0 = tb * TB
    nm = TB // 128
    G = 4
    for n in range(4):
        xrs = []
        for m in range(nm):
            xr = M.rring.next()
            S.dma("sp", lambda e, xr=xr, m=m, n=n: e.dma_start(out=xr[:], in_=res_src[t0 + m * 128:t0 + (m + 1) * 128, n * 512:(n + 1) * 512]),
                  reads=[C.bX[tb]] if res_src is C.X else [], writes=[xr.b])
            xrs.append(xr)
        accs = [C.psum.next() for _ in range(nm)]
        for g0 in range(0, nch, G):
            gn = min(G, nch - g0)
            w2 = M.w2ring.next()
            S.dma("pool", lambda e, w2=w2, g0=g0, gn=gn, n=n: e.dma_start(
                out=w2[:, 0:gn, :], in_=w_ap[g0 * 128:(g0 + gn) * 128, n * 512:(n + 1) * 512].rearrange("(g p) f -> p g f", p=128)),
                writes=[w2.b])
            for gi in range(gn):
                f = g0 + gi
                for m in range(nm):
                    S.op("pe", lambda e, f=f, gi=gi, m=m, w2=w2, acc=accs[m]: e.matmul(
                        acc[:, :], lhsT=actT[:, f, m * 128:(m + 1) * 128], rhs=w2[:, gi, :], start=(f == 0), stop=(f == nch - 1)),
                        reads=[w2.b] + list(act_bufs), writes=[accs[m].b])
        for m in range(nm):
            xr = xrs[m]
            S.op("dve", lambda e, xr=xr, acc=accs[m]: e.scalar_tensor_tensor(out=xr[:], in0=acc[:, :], scalar=scale, in1=xr[:],
                                                                            op0=ALU.mult, op1=ALU.add),
                 reads=[accs[m].b, xr.b], writes=[xr.b])
        for m in range(nm):
            xr = xrs[m]
            S.dma("act", lambda e, xr=xr, m=m, n=n: e.dma_start(out=C.X[t0 + m * 128:t0 + (m + 1) * 128, n * 512:(n + 1) * 512], in_=xr[:]),
                  reads=[xr.b], writes=[C.bX[tb]])


def lin_mem(C, M):
    TB = C.TB
    M.xring = M.ring(2, [128, D], F32, "x")
    M.xnring = M.ring(2, [128, D], BF16, "xn")
    M.ssring = M.ring(4, [128, 1], F32, "ss")
    M.junk = M.tile([128, D], BF16, "junk")
    M.gb = M.tile([128, D], F32, "gb")
    M.xnT = M.tile([128, KC, TB], BF16, "xnT")
    M.xnT_bufs = [Buf() for _ in range(TB // 128)]
    M.wring = M.ring(4, [128, KC, 256], BF16, "w")
    M.w2ring = M.ring(4, [128, 4, 512], BF16, "w2")
    M.rring = M.ring(8, [128, 512], F32, "res")


def stage_mem(C):
    with StageMem(C) as M:
        lin_mem(C, M)
        gb = load_gb(C, M, C.I["mem_norm"])
        bufs = [Buf(), Buf()]
        norm_transpose(C, M, C.I["mem"], 0, MEM, gb, C.memnT, bufs, [])
        C.S.op("pool", lambda e: e.nop(), reads=bufs, writes=[C.memnT.b])


def stage_ffn(C, l, which, src):
    S = C.S
    TB = C.TB
    pre = "ffn%d_" % which
    wg, wu, wd = C.I[pre + "w_gate"][l], C.I[pre + "w_up"][l], C.I[pre + "w_down"][l]
    NH = NFC // 2
    with StageMem(C) as M:
        lin_mem(C, M)
        hT = M.tile([128, NH, TB], BF16, "hT")
        hbufs = [Buf() for _ in range(NH)]
        sgring = M.ring(3, [128, 512], F32, "sg")
        gb = load_gb(C, M, C.I[pre + "norm"][l])
        for tb in range(C.NTB):
            t0 = tb * TB
            norm_transpose(C, M, src, t0, TB, gb, M.xnT, M.xnT_bufs, [C.bX[tb]] if src is C.X else [])
            for half in range(2):
                for fb in range(NH // 2):
                    c0 = (half * NH + fb * 2) * 128
                    wgt = load_wblk(C, M, wg, c0, 256)
                    wut = load_wblk(C, M, wu, c0, 256)
                    for fc in range(2):
                        f = fb * 2 + fc
                        for th in range(TB // 512):
                            tsl = slice(th * 512, (th + 1) * 512)
                            pg = mm_fm(C, wgt, fc * 128, 128, M.xnT, M.xnT_bufs, tsl)
                            pu = mm_fm(C, wut, fc * 128, 128, M.xnT, M.xnT_bufs, tsl)
                            sg = sgring.next()
                            S.op("act", lambda e, sg=sg, pg=pg: e.activation(out=sg[:], in_=pg[:, :], func=AF.Silu),
                                 reads=[pg.b], writes=[sg.b])
                            S.op("dve", lambda e, sg=sg, pu=pu, f=f, tsl=tsl: e.tensor_tensor(out=hT[:, f, tsl], in0=pu[:, :], in1=sg[:], op=ALU.mult),
                                 reads=[pu.b, sg.b], writes=[hbufs[f]])
                wrows = wd[half * NH * 128:(half + 1) * NH * 128, :]
                proj_tok(C, M, hT, hbufs, NH, wrows, 0.5, tb, src if half == 0 else C.X)


def stage_inproj(C, l, src):
    S = C.S
    TB = C.TB
    w = C.I["w_in"][l]
    with StageMem(C) as M:
        lin_mem(C, M)
        stg = M.ring(4, [128, 512], F32, "stg")
        stb = M.ring(4, [128, 256], BF16, "stb")
        gb = load_gb(C, M, C.I["mix_norm"][l])
        segs = [(O_Z, 1024, "z"), (O_X, O_SB - O_X, "fm"), (O_SB, 1024, "fm"), (O_SB + 1024, 512, "v"), (O_RW, NIN - O_RW, "fm")]
        for tb in range(C.NTB):
            t0 = tb * TB
            norm_transpose(C, M, src, t0, TB, gb, M.xnT, M.xnT_bufs, [C.bX[tb]] if src is C.X else [])
            for (s0, sn, kind) in segs:
                for c0 in range(s0, s0 + sn, 256):
                    ncol = min(256, s0 + sn - c0)
                    wt = load_wblk(C, M, w, c0, ncol)
                    if kind == "fm":
                        for cs in range(0, ncol, 128):
                            nn = min(128, ncol - cs)
                            for th in range(TB // 512):
                                tsl = slice(th * 512, (th + 1) * 512)
                                ps = mm_fm(C, wt, cs, nn, M.xnT, M.xnT_bufs, tsl)
                                sg = stg.next()
                                copy_op(C, evac_eng(C), sg[0:nn, :], ps[0:nn, :], [ps.b], [sg.b])
                                S.dma("sp", lambda e, sg=sg, r0=c0 + cs, nn=nn, th=th, t0=t0: e.dma_start(
                                    out=C.UT[r0:r0 + nn, t0 + th * 512:t0 + (th + 1) * 512], in_=sg[0:nn, :]),
                                    reads=[sg.b], writes=[C.bUT])
                    else:
                        for m in range(TB // 128):
                            ps = C.psum.next()
                            for kc in range(KC):
                                S.op("pe", lambda e, kc=kc, ps=ps, m=m, wt=wt: e.matmul(
                                    ps[:, 0:256], lhsT=M.xnT[:, kc, m * 128:(m + 1) * 128], rhs=wt[:, kc, 0:256],
                                    start=(kc == 0), stop=(kc == KC - 1)),
                                    reads=[wt.b] + M.xnT_bufs, writes=[ps.b])
                            sb_ = stb.next()
                            if kind == "z":
                                S.op("act", lambda e, sb_=sb_, ps=ps: e.activation(out=sb_[:], in_=ps[:, 0:256], func=AF.Silu),
                                     reads=[ps.b], writes=[sb_.b])
                                dst, db, cc = C.SZ, C.bSZ, c0 - O_Z
                            else:
                                copy_op(C, "dve", sb_[:], ps[:, 0:256], [ps.b], [sb_.b])
                                dst, db, cc = C.VSB, C.bVSB, c0 - (O_SB + 1024)
                            S.dma("sp", lambda e, sb_=sb_, dst=dst, cc=cc, m=m, t0=t0: e.dma_start(
                                out=dst[t0 + m * 128:t0 + (m + 1) * 128, cc:cc + 256], in_=sb_[:]),
                                reads=[sb_.b], writes=[db])


def stage_outproj(C, l):
    S = C.S
    TB = C.TB
    with StageMem(C) as M:
        lin_mem(C, M)
        for tb in range(C.NTB):
            t0 = tb * TB
            S.dma("sp", lambda e, t0=t0: e.dma_start(out=M.xnT[:], in_=C.YMT[:, t0:t0 + TB].rearrange("(k p) t -> p k t", p=128)),
                  reads=[C.bYMT], writes=M.xnT_bufs)
            proj_tok(C, M, M.xnT, M.xnT_bufs, KC, C.I["w_out"][l], 1.0, tb, C.X)


def stage_cross(C, l):
    S = C.S
    TB = C.TB
    scale = 128 ** -0.5
    with StageMem(C) as M:
        lin_mem(C, M)
        KT = M.tile([128, 4, MEM], BF16, "KT")
        V = M.tile([128, 2, 512], BF16, "V")
        qT = M.tile([128, 4, TB], BF16, "qT")
        oT = M.tile([128, 4, TB], BF16, "oT")
        obufs = [Buf() for _ in range(4)]
        pring = M.ring(4, [128, 512], BF16, "pT")
        rdring = M.ring(2, [128, 512], F32, "rden")
        for cb in range(2):
            wt = load_wblk(C, M, C.I["cross_wk"][l], cb * 256, 256)
            for hh in range(2):
                h = cb * 2 + hh
                ps = mm_fm(C, wt, hh * 128, 128, C.memnT, [C.memnT.b], slice(0, MEM))
                copy_op(C, evac_eng(C), KT[:, h, :], ps[:, 0:MEM], [ps.b], [KT.b])
            wt = load_wblk(C, M, C.I["cross_wv"][l], cb * 256, 256)
            for mc in range(2):
                ps = C.psum.next()
                for kc in range(KC):
                    S.op("pe", lambda e, kc=kc, ps=ps, mc=mc, wt=wt: e.matmul(
                        ps[:, 0:256], lhsT=C.memnT[:, kc, mc * 128:(mc + 1) * 128], rhs=wt[:, kc, 0:256],
                        start=(kc == 0), stop=(kc == KC - 1)), reads=[wt.b, C.memnT.b], writes=[ps.b])
                copy_op(C, evac_eng(C), V[:, mc, cb * 256:(cb + 1) * 256], ps[:, 0:256], [ps.b], [V.b])
        gb = load_gb(C, M, C.I["cross_norm"][l])
        for tb in range(C.NTB):
            t0 = tb * TB
            norm_transpose(C, M, C.X, t0, TB, gb, M.xnT, M.xnT_bufs, [C.bX[tb]])
            for cb in range(2):
                wt = load_wblk(C, M, C.I["cross_wq"][l], cb * 256, 256)
                for hh in range(2):
                    h = cb * 2 + hh
                    for th in range(TB // 512):
                        tsl = slice(th * 512, (th + 1) * 512)
                        ps = mm_fm(C, wt, hh * 128, 128, M.xnT, M.xnT_bufs, tsl)
                        S.op("act", lambda e, ps=ps, h=h, tsl=tsl: e.activation(out=qT[:, h, tsl], in_=ps[:, :], func=AF.Copy, scale=scale),
                             reads=[ps.b], writes=[qT.b])
            for h in range(4):
                for th in range(TB // 512):
                    tsl = slice(th * 512, (th + 1) * 512)
                    pts = []
                    for mc in range(2):
                        ps = C.psum.next()
                        S.op("pe", lambda e, ps=ps, h=h, mc=mc, tsl=tsl: e.matmul(ps[:, :], lhsT=KT[:, h, mc * 128:(mc + 1) * 128], rhs=qT[:, h, tsl],
                                                                              start=True, stop=True),
                             reads=[KT.b, qT.b], writes=[ps.b])
                        pT = pring.next()
                        S.op("act", lambda e, ps=ps, pT=pT: e.activation(out=pT[:], in_=ps[:, :], func=AF.Exp), reads=[ps.b], writes=[pT.b])
                        pts.append(pT)
                    po = C.psum.next()
                    pd = C.psum.next()
                    for mc in range(2):
                        S.op("pe", lambda e, po=po, mc=mc, h=h, pT=pts[mc]: e.matmul(po[:, :], lhsT=V[:, mc, h * 128:(h + 1) * 128], rhs=pT[:],
                                                                                  start=(mc == 0), stop=(mc == 1)),
                             reads=[V.b, pts[mc].b], writes=[po.b])
                    for mc in range(2):
                        S.op("pe", lambda e, pd=pd, mc=mc, pT=pts[mc]: e.matmul(pd[:, :], lhsT=C.ones_bf[:], rhs=pT[:], start=(mc == 0), stop=(mc == 1)),
                             reads=[C.ones_bf.b, pts[mc].b], writes=[pd.b])
                    rd = rdring.next()
                    S.op("dve", lambda e, rd=rd, pd=pd: e.reciprocal(out=rd[:], in_=pd[:, :]), reads=[pd.b], writes=[rd.b])
                    S.op("dve", lambda e, rd=rd, po=po, h=h, tsl=tsl: e.tensor_tensor(out=oT[:, h, tsl], in0=po[:, :], in1=rd[:], op=ALU.mult),
                         reads=[po.b, rd.b], writes=[obufs[h]])
            proj_tok(C, M, oT, obufs, 4, C.I["cross_wo"][l], 1.0, tb, C.X)


def stage_final(C, src):
    S = C.S
    with StageMem(C) as M:
        lin_mem(C, M)
        gb = load_gb(C, M, C.I["final_norm"])
        oring = M.ring(2, [128, D], F32, "o")
        for m in range(C.T // 128):
            tb = (m * 128) // C.TB
            xt = M.xring.next()
            S.dma("sp", lambda e, xt=xt, m=m: e.dma_start(out=xt[:], in_=src[m * 128:(m + 1) * 128, :]),
                  reads=[C.bX[tb]] if src is C.X else [], writes=[xt.b])
            ss = M.ssring.next()
            S.op("act", lambda e, xt=xt, ss=ss: e.activation(out=M.junk[:], in_=xt[:], func=AF.Square, accum_out=ss[:]),
                 reads=[xt.b], writes=[M.junk.b, ss.b])
            rstd_from_sumsq(C, ss, D, EPS)
            ot = oring.next()
            S.op("dve", lambda e, xt=xt, ss=ss, ot=ot: e.scalar_tensor_tensor(out=ot[:], in0=xt[:], scalar=ss[:, 0:1], in1=gb[:],
                                                                            op0=ALU.mult, op1=ALU.mult),
                 reads=[xt.b, ss.b, gb.b], writes=[ot.b])
            S.dma("sp", lambda e, ot=ot, m=m: e.dma_start(out=C.out[m * 128:(m + 1) * 128, :], in_=ot[:]),
                  reads=[ot.b], is_output=True)


def dump(C, name, ap, buf):
    if name not in getattr(C, "dumps", ()) or name in C.dumped:
        return
    C.dumped.add(name)
    d = C.nc.dram_tensor("D_" + name, list(ap.shape), ap.dtype, kind="ExternalOutput").ap()
    C.S.dma("sp", lambda e: e.dma_start(out=d, in_=ap), reads=[buf], is_output=True)


def bc_last(ap, shape):
    return ap.unsqueeze(2).broadcast_to(shape)


def load_cols(C, tl, dst_ap, src_1d):
    C.S.dma("sp", lambda e: e.dma_start(out=dst_ap, in_=src_1d.rearrange("(c p) -> p c", p=128), allow_slow_non_contiguous=True),
            writes=[tl.b])


def stage_ssd(C, l):
    S = C.S
    T = C.T
    I = C.I
    W = min(512, T)
    NSB = T // W
    NCH = W // 128
    with StageMem(C) as M:
        cw = M.tile([128, 12, 4], F32, "cw")
        for k in range(4):
            load_cols(C, cw, cw[:, :, k], I["ssd_conv_w"][l, k])
        cb = M.tile([128, 12], F32, "cb")
        load_cols(C, cb, cb[:, :], I["ssd_conv_b"][l])
        ng = M.tile([128, 1024], F32, "ng")
        S.dma("sp", lambda e: e.dma_start(out=ng[:], in_=I["ssd_norm"][l].partition_broadcast(128)), writes=[ng.b])
        dfull = M.tile([128, 16], F32, "dfull")
        S.dma("sp", lambda e: e.dma_start(out=dfull[:], in_=I["ssd_d"][l].partition_broadcast(128)), writes=[dfull.b])
        dtb = M.tile([128, 1], F32, "dtb")
        av = M.tile([128, 1], F32, "av")
        S.op("pool", lambda e: e.memset(dtb[:], 0.0), writes=[dtb.b])
        S.op("pool", lambda e: e.memset(av[:], 0.0), writes=[av.b])
        for r0 in (0, 32, 64):
            S.dma("sp", lambda e, r0=r0: e.dma_start(out=dtb[r0:r0 + 16, :], in_=I["ssd_dt_bias"][l].rearrange("(h o) -> h o", o=1)), writes=[dtb.b])
            S.dma("sp", lambda e, r0=r0: e.dma_start(out=av[r0:r0 + 16, :], in_=I["ssd_a_log"][l].rearrange("(h o) -> h o", o=1)), writes=[av.b])
        S.op("act", lambda e: e.activation(out=av[:], in_=av[:], func=AF.Exp), reads=[av.b], writes=[av.b])
        S.op("dve", lambda e: e.tensor_scalar(out=av[:], in0=av[:], scalar1=-1.0, scalar2=None, op0=ALU.mult), reads=[av.b], writes=[av.b])
        tri = M.tile([128, 2, 128], F32, "tri")
        S.op("pool", lambda e: e.memset(tri[:], 1.0), writes=[tri.b])
        S.op("pool", lambda e: e.affine_select(out=tri[:], in_=tri[:], compare_op=ALU.is_ge, fill=0.0, base=0,
                                               pattern=[[0, 2], [1, 128]], channel_multiplier=-1), reads=[tri.b], writes=[tri.b])
        sel = M.tile([16, 16, 128], F32, "sel")
        S.op("pool", lambda e: e.memset(sel[:], 0.0), writes=[sel.b])
        S.op("pool", lambda e: e.affine_select(out=sel[:], in_=sel[:], compare_op=ALU.not_equal, fill=1.0, base=0,
                                               pattern=[[-1, 16], [0, 128]], channel_multiplier=1), reads=[sel.b], writes=[sel.b])
        ones16 = M.tile([16, 128], F32, "ones16")
        S.op("pool", lambda e: e.memset(ones16[:], 1.0), writes=[ones16.b])
        onesW = M.tile([128, 128], F32, "onesW")
        S.op("pool", lambda e: e.memset(onesW[:], 1.0), writes=[onesW.b])
        ST = M.tile([128, 16, 64], F32, "ST")
        S.op("pool", lambda e: e.memset(ST[:], 0.0), writes=[ST.b])
        stbring = M.ring(2, [128, 16, 64], BF16, "STb")
        stb = stbring.next()
        S.op("pool", lambda e, stb=stb: e.memset(stb[:], 0.0), writes=[stb.b])
        cinring = M.ring(2, [128, 4, W + 3], F32, "cin")
        accring = M.ring(2, [128, W], F32, "acc")
        xbcring = M.ring(2, [128, 12, W], BF16, "xbc")
        rawring = M.ring(2, [128, W], F32, "raw")
        e1t = M.tile([128, W], F32, "e1")
        stkring = M.ring(2, [128, W], F32, "stk")
        dat = M.tile([128, W], F32, "da")
        acsring = M.ring(2, [128, W], F32, "acs")
        for t_ in rawring.tiles + stkring.tiles + acsring.tiles + [e1t, dat]:
            S.op("pool", lambda e, t_=t_: e.memset(t_[:], 0.0), writes=[t_.b])
        tmring = M.ring(2, [128, 128], F32, "tm")
        earing = M.ring(2, [128, 16], F32, "ea")
        xtmring = M.ring(2, [128, 1024], BF16, "xtm")
        btmring = M.ring(2, [128, 256], BF16, "btm")
        cbmring = M.ring(2, [128, 2, 128], F32, "cbm")
        xdring = M.ring(2, [128, 16, 64], BF16, "xd")
        xddring = M.ring(2, [128, 16, 64], BF16, "xdd")
        xDring = M.ring(2, [128, 16, 64], F32, "xD")
        difring = M.ring(2, [128, 4, 128], F32, "dif")
        decring = M.ring(2, [128, 4, 128], F32, "dec")
        wTring = M.ring(8, [128, 4, 128], BF16, "wT")
        t1ring = M.ring(2, [128, 512], F32, "t1")
        t2ring = M.ring(2, [128, 512], F32, "t2")
        yzring = M.ring(2, [128, 1024], F32, "yz")
        szring = M.ring(2, [128, 1024], BF16, "sz")
        ynring = M.ring(2, [128, 1024], BF16, "yn")
        ymtring = M.ring(2, [128, 8, W], BF16, "ymt")
        ss2ring = M.ring(2, [128, 2], F32, "ss2")
        dglring = M.ring(2, [16, 16], F32, "dgl")
        edlring = M.ring(2, [128, 16], F32, "edl")
        junk = M.tile([128, 512], BF16, "junk")

        for sb in range(NSB):
            t0 = sb * W
            xbc = xbcring.next()
            for cg in range(3):
                cin = cinring.next()
                r0 = O_X + cg * 512
                if t0 == 0:
                    S.op("pool", lambda e, cin=cin: e.memset(cin[:, :, 0:3], 0.0), writes=[cin.b])
                    S.dma("sp", lambda e, cin=cin, r0=r0: e.dma_start(out=cin[:, :, 3:3 + W], in_=C.UT[r0:r0 + 512, 0:W].rearrange("(c p) t -> p c t", p=128)),
                          reads=[C.bUT], writes=[cin.b])
                else:
                    S.dma("sp", lambda e, cin=cin, r0=r0, t0=t0: e.dma_start(out=cin[:], in_=C.UT[r0:r0 + 512, t0 - 3:t0 + W].rearrange("(c p) t -> p c t", p=128)),
                          reads=[C.bUT], writes=[cin.b])
                for ci in range(4):
                    c = cg * 4 + ci
                    acc = accring.next()
                    S.op("dve", lambda e, acc=acc, cin=cin, ci=ci, c=c: e.tensor_scalar(out=acc[:], in0=cin[:, ci, 3:3 + W], scalar1=cw[:, c, 3:4], scalar2=None, op0=ALU.mult),
                         reads=[cin.b, cw.b], writes=[acc.b])
                    for k in (2, 1, 0):
                        S.op("dve", lambda e, acc=acc, cin=cin, ci=ci, c=c, k=k: e.scalar_tensor_tensor(out=acc[:], in0=cin[:, ci, k:k + W], scalar=cw[:, c, k:k + 1], in1=acc[:],
                                                                                                  op0=ALU.mult, op1=ALU.add),
                             reads=[cin.b, cw.b, acc.b], writes=[acc.b])
                    S.op("act", lambda e, acc=acc, c=c, xbc=xbc: e.activation(out=xbc[:, c, :], in_=acc[:], func=AF.Silu, bias=cb[:, c:c + 1]),
                         reads=[acc.b, cb.b], writes=[xbc.b])
            raw = rawring.next()
            for r0 in (0, 32, 64):
                S.dma("sp", lambda e, raw=raw, r0=r0, t0=t0: e.dma_start(out=raw[r0:r0 + 16, :], in_=C.UT[O_DT:O_DT + 16, t0:t0 + W]),
                      reads=[C.bUT], writes=[raw.b])
            stk = stkring.next()
            acs = acsring.next()
            S.op("act", lambda e, raw=raw: e.activation(out=e1t[0:80, :], in_=raw[0:80, :], func=AF.Exp, bias=dtb[0:80, :]),
                 reads=[raw.b, dtb.b], writes=[e1t.b])
            S.op("act", lambda e, stk=stk: e.activation(out=stk[0:80, :], in_=e1t[0:80, :], func=AF.Ln, bias=1.0), reads=[e1t.b], writes=[stk.b])
            S.op("dve", lambda e, stk=stk: e.tensor_scalar(out=dat[0:80, :], in0=stk[0:80, :], scalar1=av[0:80, :], scalar2=None, op0=ALU.mult),
                 reads=[stk.b, av.b], writes=[dat.b])
            for ch in range(NCH):
                cs = slice(ch * 128, (ch + 1) * 128)
                S.op("dve", lambda e, acs=acs, cs=cs: e.tensor_tensor_scan(out=acs[0:80, cs], data0=onesW[0:80, :], data1=dat[0:80, cs], initial=0.0,
                                                                        op0=ALU.mult, op1=ALU.add), reads=[dat.b, onesW.b], writes=[acs.b])
            S.op("pool", lambda e, stk=stk, acs=acs: e.tensor_copy(out=stk[32:48, :], in_=acs[32:48, :]), reads=[acs.b], writes=[stk.b])
            for ch in range(NCH):
                cs = slice(ch * 128, (ch + 1) * 128)
                last = (ch + 1) * 128 - 1
                S.op("act", lambda e, stk=stk, acs=acs, cs=cs, last=last: e.activation(out=stk[64:80, cs], in_=acs[64:80, cs], func=AF.Exp, scale=-1.0,
                                                                                   bias=acs[64:80, last:last + 1]), reads=[acs.b], writes=[stk.b])
            ymt = ymtring.next()
            for ch in range(NCH):
                cs = slice(ch * 128, (ch + 1) * 128)
                last = (ch + 1) * 128 - 1
                tc = t0 + ch * 128
                sz = szring.next()
                S.dma("sp", lambda e, sz=sz, tc=tc: e.dma_start(out=sz[:], in_=C.SZ[tc:tc + 128, :]), reads=[C.bSZ], writes=[sz.b])
                ptm = C.psum.next()
                S.op("pe", lambda e, ptm=ptm, stk=stk, cs=cs: e.transpose(ptm[:, 0:128], stk[:, cs], C.identf[:]), reads=[stk.b, C.identf.b], writes=[ptm.b])
                tm = tmring.next()
                S.op("act", lambda e, tm=tm, ptm=ptm: e.copy(out=tm[:], in_=ptm[:, 0:128]), reads=[ptm.b], writes=[tm.b])
                ea = earing.next()
                S.op("act", lambda e, ea=ea, tm=tm: e.activation(out=ea[:], in_=tm[:, 32:48], func=AF.Exp), reads=[tm.b], writes=[ea.b])
                px = C.psum.next()
                pxv = px.t[:].bitcast(BF16)
                for c in range(8):
                    S.op("pe", lambda e, pxv=pxv, xbc=xbc, c=c, cs=cs: e.transpose(pxv[:, c * 128:(c + 1) * 128], xbc[:, c, cs], C.ident[:]),
                         reads=[xbc.b, C.ident.b], writes=[px.b])
                xtm = xtmring.next()
                S.op("dve", lambda e, xtm=xtm, pxv=pxv: e.tensor_copy(out=xtm[:], in_=pxv), reads=[px.b], writes=[xtm.b])
                pb = C.psum.next()
                pbv = pb.t[:].bitcast(BF16)
                for g in range(2):
                    S.op("pe", lambda e, pbv=pbv, xbc=xbc, g=g, cs=cs: e.transpose(pbv[:, g * 128:(g + 1) * 128], xbc[:, 8 + g, cs], C.ident[:]),
                         reads=[xbc.b, C.ident.b], writes=[pb.b])
                btm = btmring.next()
                S.op("act", lambda e, btm=btm, pbv=pbv: e.copy(out=btm[:], in_=pbv[:, 0:256]), reads=[pb.b], writes=[btm.b])
                pcb = C.psum.next()
                for g in range(2):
                    S.op("pe", lambda e, pcb=pcb, xbc=xbc, g=g, cs=cs: e.matmul(pcb[:, g * 128:(g + 1) * 128], lhsT=xbc[:, 8 + g, cs], rhs=xbc[:, 10 + g, cs], start=True, stop=True),
                         reads=[xbc.b], writes=[pcb.b])
                cbm = cbmring.next()
                S.op("dve", lambda e, cbm=cbm, pcb=pcb: e.tensor_tensor(out=cbm[:], in0=pcb[:, 0:256].rearrange("p (g l) -> p g l", g=2), in1=tri[:], op=ALU.mult),
                     reads=[pcb.b, tri.b], writes=[cbm.b])
                xd = xdring.next()
                xdd = xddring.next()
                xD = xDring.next()
                xv = xtm[:].rearrange("p (h d) -> p h d", h=16)
                S.op("dve", lambda e, xd=xd, xv=xv, tm=tm: e.tensor_tensor(out=xd[:], in0=xv, in1=bc_last(tm[:, 0:16], [128, 16, 64]), op=ALU.mult),
                     reads=[xtm.b, tm.b], writes=[xd.b])
                S.op("dve", lambda e, xd=xd, xdd=xdd, tm=tm: e.tensor_tensor(out=xdd[:], in0=xd[:], in1=bc_last(tm[:, 64:80], [128, 16, 64]), op=ALU.mult),
                     reads=[xd.b, tm.b], writes=[xdd.b])
                S.op("pool", lambda e, xD=xD, xv=xv: e.tensor_tensor(out=xD[:], in0=xv, in1=bc_last(dfull[:, 0:16], [128, 16, 64]), op=ALU.mult),
                     reads=[xtm.b, dfull.b], writes=[xD.b])
                wTs = []
                for q in range(4):
                    g = q // 2
                    pbc = C.psum.next()
                    for j in range(4):
                        h = 4 * q + j
                        S.op("pe", lambda e, pbc=pbc, j=j, h=h, acs=acs, cs=cs: e.matmul(pbc[:, j * 128:(j + 1) * 128], lhsT=sel[0:16, h, :], rhs=acs[0:16, cs], start=True, stop=True),
                             reads=[sel.b, acs.b], writes=[pbc.b])
                    dif = difring.next()
                    for j in range(4):
                        h = 4 * q + j
                        S.op("dve", lambda e, dif=dif, pbc=pbc, j=j, h=h, tm=tm: e.tensor_scalar(out=dif[:, j, :], in0=pbc[:, j * 128:(j + 1) * 128], scalar1=tm[:, 32 + h:33 + h], scalar2=0.0,
                                                                                         op0=ALU.subtract, op1=ALU.min), reads=[pbc.b, tm.b], writes=[dif.b])
                    dec = decring.next()
                    S.op("act", lambda e, dec=dec, dif=dif: e.activation(out=dec[:], in_=dif[:], func=AF.Exp), reads=[dif.b], writes=[dec.b])
                    wT = wTring.next()
                    S.op("dve", lambda e, wT=wT, dec=dec, cbm=cbm, g=g: e.tensor_tensor(out=wT[:], in0=dec[:], in1=cbm[:, g, :].unsqueeze(1).broadcast_to([128, 4, 128]), op=ALU.mult),
                         reads=[dec.b, cbm.b], writes=[wT.b])
                    wTs.append(wT)
                yz = yzring.next()
                ss2 = ss2ring.next()
                for g in range(2):
                    pyd = C.psum.next()
                    for hh in range(8):
                        h = g * 8 + hh
                        S.op("pe", lambda e, pyd=pyd, hh=hh, h=h, wT=wTs[h // 4], xd=xd: e.matmul(pyd[:, hh * 64:(hh + 1) * 64], lhsT=wT[:, h % 4, :], rhs=xd[:, h, :], start=True, stop=True),
                             reads=[wTs[h // 4].b, xd.b], writes=[pyd.b])
                    pyo = C.psum.next()
                    S.op("pe", lambda e, pyo=pyo, g=g, xbc=xbc, cs=cs, stb=stb: e.matmul(pyo[:, :], lhsT=xbc[:, 10 + g, cs], rhs=stb[:, g * 8:(g + 1) * 8, :], start=True, stop=True),
                         reads=[xbc.b, stb.b], writes=[pyo.b])
                    t1 = t1ring.next()
                    S.op("dve", lambda e, t1=t1, pyo=pyo, ea=ea, g=g: e.tensor_tensor(out=t1[:].rearrange("p (h d) -> p h d", h=8), in0=pyo[:, :].rearrange("p (h d) -> p h d", h=8),
                                                                                  in1=bc_last(ea[:, g * 8:(g + 1) * 8], [128, 8, 64]), op=ALU.mult),
                         reads=[pyo.b, ea.b], writes=[t1.b])
                    t2 = t2ring.next()
                    S.op("dve", lambda e, t1=t1, t2=t2, pyd=pyd: e.tensor_tensor(out=t2[:], in0=pyd[:, :], in1=t1[:], op=ALU.add), reads=[pyd.b, t1.b], writes=[t2.b])
                    S.op("pool", lambda e, t2=t2, xD=xD, g=g: e.tensor_tensor(out=t2[:].rearrange("p (h d) -> p h d", h=8), in0=t2[:].rearrange("p (h d) -> p h d", h=8),
                                                                           in1=xD[:, g * 8:(g + 1) * 8, :], op=ALU.add), reads=[t2.b, xD.b], writes=[t2.b])
                    S.op("pool", lambda e, t2=t2, yz=yz, sz=sz, g=g: e.tensor_tensor(out=yz[:, g * 512:(g + 1) * 512], in0=t2[:], in1=sz[:, g * 512:(g + 1) * 512], op=ALU.mult),
                         reads=[t2.b, sz.b], writes=[yz.b])
                    S.op("act", lambda e, yz=yz, ss2=ss2, g=g: e.activation(out=junk[:], in_=yz[:, g * 512:(g + 1) * 512], func=AF.Square, accum_out=ss2[:, g:g + 1]),
                         reads=[yz.b], writes=[junk.b, ss2.b])
                psts = []
                for g in range(2):
                    pst = C.psum.next()
                    S.op("pe", lambda e, pst=pst, g=g, btm=btm, xdd=xdd: e.matmul(pst[:, :], lhsT=btm[:, g * 128:(g + 1) * 128], rhs=xdd[:, g * 8:(g + 1) * 8, :], start=True, stop=True),
                         reads=[btm.b, xdd.b], writes=[pst.b])
                    psts.append(pst)
                dgl = dglring.next()
                S.op("dve", lambda e, dgl=dgl, acs=acs, last=last: e.tensor_scalar(out=dgl[:], in0=C.identf[0:16, 0:16], scalar1=acs[0:16, last:last + 1], scalar2=None, op0=ALU.mult),
                     reads=[acs.b, C.identf.b], writes=[dgl.b])
                pedl = C.psum.next()
                S.op("pe", lambda e, pedl=pedl, dgl=dgl: e.matmul(pedl[:, 0:16], lhsT=ones16[:], rhs=dgl[:], start=True, stop=True), reads=[ones16.b, dgl.b], writes=[pedl.b])
                edl = edlring.next()
                S.op("act", lambda e, edl=edl, pedl=pedl: e.activation(out=edl[:], in_=pedl[:, 0:16], func=AF.Exp), reads=[pedl.b], writes=[edl.b])
                S.op("dve", lambda e, edl=edl: e.tensor_tensor(out=ST[:], in0=ST[:], in1=bc_last(edl[:, 0:16], [128, 16, 64]), op=ALU.mult), reads=[ST.b, edl.b], writes=[ST.b])
                for g in range(2):
                    S.op("dve", lambda e, g=g, pst=psts[g]: e.tensor_tensor(out=ST[:, g * 8:(g + 1) * 8, :], in0=ST[:, g * 8:(g + 1) * 8, :],
                                                                          in1=pst[:, :].rearrange("p (h d) -> p h d", h=8), op=ALU.add),
                         reads=[ST.b, psts[g].b], writes=[ST.b])
                stb = stbring.next()
                S.op("act", lambda e, stb=stb: e.copy(out=stb[:], in_=ST[:]), reads=[ST.b], writes=[stb.b])
                rstd_from_sumsq(C, ss2, 512, EPS)
                yn = ynring.next()
                for g in range(2):
                    S.op("dve", lambda e, yn=yn, yz=yz, ss2=ss2, g=g: e.scalar_tensor_tensor(out=yn[:, g * 512:(g + 1) * 512], in0=yz[:, g * 512:(g + 1) * 512], scalar=ss2[:, g:g + 1],
                                                                                        in1=ng[:, g * 512:(g + 1) * 512], op0=ALU.mult, op1=ALU.mult),
                         reads=[yz.b, ss2.b, ng.b], writes=[yn.b])
                pym = C.psum.next()
                pymv = pym.t[:].bitcast(BF16)
                for c in range(8):
                    S.op("pe", lambda e, pymv=pymv, yn=yn, c=c: e.transpose(pymv[:, c * 128:(c + 1) * 128], yn[:, c * 128:(c + 1) * 128], C.ident[:]),
                         reads=[yn.b, C.ident.b], writes=[pym.b])
                S.op("act", lambda e, ymt=ymt, pymv=pymv, cs=cs: e.copy(out=ymt[:, :, cs], in_=pymv.rearrange("p (k t) -> p k t", k=8)), reads=[pym.b], writes=[ymt.b])
            S.dma("sp", lambda e, ymt=ymt, t0=t0: e.dma_start(out=C.YMT[0:1024, t0:t0 + W].rearrange("(k p) t -> p k t", p=128), in_=ymt[:]),
                  reads=[ymt.b], writes=[C.bYMT])


def stage_sb(C, l):
    S = C.S
    T = C.T
    I = C.I
    NG = T // 512
    NB = T // 128
    with StageMem(C) as M:
        ring6 = Ring(C.psum.tiles[:6])
        accr = Ring(C.psum.tiles[6:])
        qT = M.tile([128, 4, T], BF16, "qT")
        kT = M.tile([128, 4, T], BF16, "kT")
        vt = M.tile([128, NB, 512], BF16, "vt")
        for c in range(4):
            S.dma("pool", lambda e, c=c: e.dma_start(out=qT[:, c, :], in_=C.UT[O_SB + c * 128:O_SB + (c + 1) * 128, :]), reads=[C.bUT], writes=[qT.b])
            S.dma("pool", lambda e, c=c: e.dma_start(out=kT[:, c, :], in_=C.UT[O_SB + 512 + c * 128:O_SB + 512 + (c + 1) * 128, :]), reads=[C.bUT], writes=[kT.b])
        for j0 in range(0, NB, 8):
            j1 = min(NB, j0 + 8)
            S.dma("sp", lambda e, j0=j0, j1=j1: e.dma_start(out=vt[:, j0:j1, :], in_=C.VSB[j0 * 128:j1 * 128, :].rearrange("(j p) c -> p j c", p=128)), reads=[C.bVSB], writes=[vt.b])
        sbg = M.tile([64, 8], F32, "sbg")
        S.dma("sp", lambda e: e.dma_start(out=sbg[:], in_=I["sb_norm"][l].rearrange("(h d) -> d h", d=64), allow_slow_non_contiguous=True), writes=[sbg.b])
        masks = []
        for r in range(4):
            mk = M.tile([128, 4, 128], F32, "mask%d" % r)
            S.op("pool", lambda e, mk=mk: e.memset(mk[:], 1.0), writes=[mk.b])
            S.op("pool", lambda e, mk=mk, r=r: e.affine_select(out=mk[:], in_=mk[:], compare_op=ALU.is_gt, fill=0.0, base=-128 * r,
                                                              pattern=[[128, 4], [1, 128]], channel_multiplier=-1), reads=[mk.b], writes=[mk.b])
            masks.append(mk)
        trib = M.tile([128, 128], BF16, "trib")
        S.op("pool", lambda e: e.memset(trib[:], 1.0), writes=[trib.b])
        S.op("pool", lambda e: e.affine_select(out=trib[:], in_=trib[:], compare_op=ALU.is_ge, fill=0.0, base=0,
                                               pattern=[[-1, 128]], channel_multiplier=1), reads=[trib.b], writes=[trib.b])
        ering = M.ring(3, [128, 512], F32, "e")
        spring = M.ring(5, [128, 512], BF16, "sp")
        tring = M.ring(3, [128, 512], F32, "t")
        wring = M.ring(4, [128, 512], BF16, "wT")
        aring = M.ring(3, [128, 512], BF16, "acc")
        sqring = M.ring(2, [64, 512], BF16, "sq")
        rsring = M.ring(2, [64, 512], F32, "rs")
        yoring = M.ring(2, [64, 512], BF16, "yo")
        ering = M.ring(5, [128, 512], F32, "e2")
        its = []
        for g in range(NG):
            nj = 4 * g + 4
            for h in range(8):
                for j in range(nj - 1, -1, -1):
                    its.append(dict(g=g, h=h, j=j, nj=nj))
        state = dict(acc=None, po=None)

        def stA(it):
            g, h, j, nj = it["g"], it["h"], it["j"], it["nj"]
            qs = slice(g * 512, (g + 1) * 512)
            ks = slice(j * 128, (j + 1) * 128)
            hc = h // 2
            hp = slice((h % 2) * 64, (h % 2) * 64 + 64)
            pz = ring6.next()
            S.op("pe", lambda e: e.matmul(pz[:, :], lhsT=kT[hp, hc, ks], rhs=qT[hp, hc, qs], start=True, stop=True), reads=[kT.b, qT.b], writes=[pz.b])
            et = ering.next()
            S.op("act", lambda e: e.activation(out=et[:], in_=pz[:, :], func=AF.Exp, scale=0.125), reads=[pz.b], writes=[et.b])
            if j >= 4 * g:
                mk = masks[j - 4 * g]
                S.op("dve", lambda e: e.tensor_tensor(out=et[:], in0=et[:], in1=mk[:].rearrange("p a b -> p (a b)"), op=ALU.mult), reads=[et.b, mk.b], writes=[et.b])
            sp = spring.next()
            S.op("act", lambda e: e.activation(out=sp[:], in_=et[:], func=AF.Ln, bias=1.0), reads=[et.b], writes=[sp.b])
            it["et"] = et
            it["sp"] = sp

        def stB(it):
            g, h, j, nj = it["g"], it["h"], it["j"], it["nj"]
            et, sp = it["et"], it["sp"]
            if j == nj - 1:
                state["acc"] = None
            acc = state["acc"]
            ps = ring6.next()
            S.op("pe", lambda e: e.matmul(ps[:, :], lhsT=trib[:], rhs=sp[:], start=True, stop=(acc is None)), reads=[trib.b, sp.b], writes=[ps.b])
            if acc is not None:
                S.op("pe", lambda e: e.matmul(ps[:, :], lhsT=C.ones_bf[:], rhs=acc[:], start=False, stop=True), reads=[C.ones_bf.b, acc.b], writes=[ps.b])
            tt = tring.next()
            S.op("act", lambda e: e.activation(out=tt[:], in_=ps[:, :], func=AF.Exp, scale=-1.0), reads=[ps.b], writes=[tt.b])
            wT = wring.next()
            S.op("dve", lambda e: e.tensor_tensor(out=wT[:], in0=et[:], in1=tt[:], op=ALU.mult), reads=[et.b, tt.b], writes=[wT.b])
            it["wT"] = wT
            if j > 0:
                nacc = aring.next()
                if acc is None:
                    S.op("pool", lambda e: e.tensor_copy(out=nacc[:], in_=sp[:]), reads=[sp.b], writes=[nacc.b])
                else:
                    S.op("pool", lambda e: e.tensor_tensor(out=nacc[:], in0=acc[:], in1=sp[:], op=ALU.add), reads=[sp.b, acc.b], writes=[nacc.b])
                state["acc"] = nacc

        def stC(it):
            g, h, j, nj = it["g"], it["h"], it["j"], it["nj"]
            qs = slice(g * 512, (g + 1) * 512)
            wT = it["wT"]
            if j == nj - 1:
                state["po"] = accr.next()
            po = state["po"]
            S.op("pe", lambda e: e.matmul(po[0:64, :], lhsT=vt[:, j, h * 64:(h + 1) * 64], rhs=wT[:], start=(j == nj - 1), stop=(j == 0)), reads=[vt.b, wT.b], writes=[po.b])
            if j != 0:
                return
            sq = sqring.next()
            S.op("act", lambda e: e.activation(out=sq[:], in_=po[0:64, :], func=AF.Square), reads=[po.b], writes=[sq.b])
            pss = ring6.next()
            S.op("pe", lambda e: e.matmul(pss[0:64, :], lhsT=C.ones_bf[0:64, 0:64], rhs=sq[:], start=True, stop=True), reads=[C.ones_bf.b, sq.b], writes=[pss.b])
            rs = rsring.next()
            S.op("act", lambda e: e.activation(out=rs[:], in_=pss[0:64, :], func=AF.Ln, scale=1.0 / 64, bias=EPS), reads=[pss.b], writes=[rs.b])
            S.op("act", lambda e: e.activation(out=rs[:], in_=rs[:], func=AF.Exp, scale=-0.5), reads=[rs.b], writes=[rs.b])
            yo = yoring.next()
            S.op("dve", lambda e: e.scalar_tensor_tensor(out=yo[:], in0=po[0:64, :], scalar=sbg[:, h:h + 1], in1=rs[:], op0=ALU.mult, op1=ALU.mult), reads=[po.b, rs.b, sbg.b], writes=[yo.b])
            S.dma("sp", lambda e: e.dma_start(out=C.YMT[1024 + h * 64:1024 + (h + 1) * 64, qs], in_=yo[:]), reads=[yo.b], writes=[C.bYMT])

        N = len(its)
        for s_ in range(N + 2):
            if s_ < N:
                stA(its[s_])
            if 0 <= s_ - 1 < N:
                stB(its[s_ - 1])
            if 0 <= s_ - 2 < N:
                stC(its[s_ - 2])
                its[s_ - 2].clear()


class _Stop(Exception):
    pass


def stage_rw(C, l):
    try:
        _stage_rw(C, l)
    except _Stop:
        pass


def _stage_rw(C, l):
    S = C.S

    import os as _os
    _bar = _os.environ.get("RWBAR", "") == "1"

    def chk(k):
        if getattr(C, "rwp", 99) == k:
            raise _Stop()
        if _bar:
            S.barrier()
    T = C.T
    I = C.I
    W = 256
    NBK = T // W
    NQ = W // 64
    L = 64
    R0 = O_RW
    with StageMem(C) as M:
        def cols(name, src, n):
            tl = M.tile([128, n], F32, name)
            load_cols(C, tl, tl[:, :], src)
            return tl
        mu = cols("mu", I["rw_mu"][l], 14)
        w0 = cols("w0", I["rw_w0"][l], 4)
        a0 = cols("a0", I["rw_a0"][l], 4)
        kkw = cols("kkw", I["rw_k_k"][l], 4)
        ka = cols("ka", I["rw_k_a"][l], 4)
        lnw = cols("lnw", I["rw_ln_w"][l], 4)
        lnb = cols("lnb", I["rw_ln_b"][l], 4)
        rkv = cols("rkv", I["rw_r_k"][l].rearrange("h d -> (h d)"), 4)
        wa = M.tile([128, 512], BF16, "wa")
        S.dma("pool", lambda e: e.dma_start(out=wa[0:64, :], in_=I["rw_w_up"][l]), writes=[wa.b])
        S.dma("pool", lambda e: e.dma_start(out=wa[64:128, :], in_=I["rw_a_up"][l]), writes=[wa.b])
        gu = M.tile([128, 512], BF16, "gu")
        S.dma("pool", lambda e: e.dma_start(out=gu[:], in_=I["rw_g_up"][l]), writes=[gu.b])
        if l > 0:
            v0 = cols("v0", I["rw_v0"][l - 1], 4)
            vdn = M.tile([128, 4, 32], BF16, "vdn")
            S.dma("pool", lambda e: e.dma_start(out=vdn[:], in_=I["rw_v_down"][l - 1].rearrange("(c p) j -> p c j", p=128)), writes=[vdn.b])
            vup = M.tile([32, 512], BF16, "vup")
            S.dma("pool", lambda e: e.dma_start(out=vup[:], in_=I["rw_v_up"][l - 1]), writes=[vup.b])
        blk = M.tile([128, 128], BF16, "blk")
        S.op("pool", lambda e: e.memset(blk[:], 0.0), writes=[blk.b])
        S.op("pool", lambda e: e.memset(blk[0:64, 0:64], 1.0), reads=[blk.b], writes=[blk.b])
        S.op("pool", lambda e: e.memset(blk[64:128, 64:128], 1.0), reads=[blk.b], writes=[blk.b])
        stackI = M.tile([128, 64], F32, "stackI")
        S.op("pool", lambda e: e.tensor_copy(out=stackI[0:64, :], in_=C.identf[0:64, 0:64]), reads=[C.identf.b], writes=[stackI.b])
        S.op("pool", lambda e: e.tensor_copy(out=stackI[64:128, :], in_=C.identf[64:128, 64:128]), reads=[C.identf.b, stackI.b], writes=[stackI.b])
        m0 = M.tile([128, W], F32, "m0")
        S.op("pool", lambda e: e.memset(m0[:], 1.0), writes=[m0.b])
        for q in range(NQ):
            S.op("pool", lambda e, q=q: e.memset(m0[:, q * L:q * L + 1], 0.0), reads=[m0.b], writes=[m0.b])
        def mask(name, op, cm, pj):
            mk = M.tile([64, 8, 64], F32, name)
            S.op("pool", lambda e: e.memset(mk[:], 1.0), writes=[mk.b])
            S.op("pool", lambda e: e.affine_select(out=mk[:], in_=mk[:], compare_op=op, fill=0.0, base=0, pattern=[[0, 8], [pj, 64]], channel_multiplier=cm),
                 reads=[mk.b], writes=[mk.b])
            return mk
        mUs = mask("mUs", ALU.is_gt, -1, 1)
        mUi = mask("mUi", ALU.is_ge, -1, 1)
        mLs = mask("mLs", ALU.is_gt, 1, -1)
        mId = mask("mId", ALU.is_ge, 1, -1)
        S.op("pool", lambda e: e.affine_select(out=mId[:], in_=mId[:], compare_op=ALU.is_ge, fill=0.0, base=0, pattern=[[0, 8], [1, 64]], channel_multiplier=-1),
             reads=[mId.b], writes=[mId.b])
        Sst = M.ring(2, [64, 8, 64], F32, "Sst")
        sst = Sst.next()
        S.op("pool", lambda e, sst=sst: e.memset(sst[:], 0.0), writes=[sst.b])

        ut = M.tile([128, 14, W + 1], F32, "ut")
        dsh = M.tile([128, W], F32, "dsh")
        us = M.tile([128, 14, W], F32, "us")
        twad = M.tile([128, W], BF16, "twad")
        sgd = M.tile([128, W], BF16, "sgd")
        lw = M.tile([128, 4, W], F32, "lw")
        cs = M.tile([128, 4, W], F32, "cs")
        at_ = M.tile([128, 4, W], F32, "a")
        gt_ = M.tile([128, 4, W], F32, "g")
        kkn = M.tile([128, 4, W], F32, "kkn")
        kp = M.tile([128, 4, W], F32, "kp")
        aT = M.tile([128, 4, W], F32, "aT")
        bT = M.tile([128, 4, W], F32, "bT")
        kT = M.tile([128, 4, W], F32, "kT")
        rT = M.tile([128, 4, W], F32, "rT")
        Bh = M.tile([128, 4, W], F32, "Bh")
        Kh = M.tile([128, 4, W], F32, "Kh")
        hl = {}
        for nm in ("aT", "bT", "kT", "rT"):
            hl[nm] = (M.tile([128, 4, W], BF16, nm + "h"), M.tile([128, 4, W], BF16, nm + "l"))
        rTl = M.tile([64, 8, W], F32, "rTl")
        gltm = M.tile([64, 8, NQ], F32, "gltm")
        dgt = M.ring(2, [64, 8, 64], F32, "dgt")
        bonus = M.tile([128, 4, W], F32, "bonus")
        yT = M.tile([128, 4, W], F32, "yT")
        ymt = M.tile([128, 4, W], BF16, "ymt")
        tA = M.ring(2, [128, W], F32, "tA")
        tB = M.ring(2, [128, W], F32, "tB")
        tbf = M.ring(2, [128, W], BF16, "tbf")
        glt = M.tile([128, 4, NQ], F32, "glt")
        if l > 0:
            vb = M.tile([128, 4, W], BF16, "vb")
            vd = M.tile([32, W], BF16, "vd")
            vf = M.tile([128, 4, W], F32, "vf")
        CAT = M.ring(1, [64, 8, 128], F32, "CAT")
        Btr = M.ring(1, [64, 8, 64], F32, "Bt")
        Ktr = M.ring(1, [64, 8, 64], F32, "Kt")
        Vtr = M.ring(1, [64, 8, 64], F32, "Vt")
        Pr = M.ring(2, [64, 8, 64], F32, "P")
        PTr = M.ring(2, [64, 8, 64], F32, "PT")
        TTr = M.ring(2, [64, 8, 64], F32, "TT")
        Akr = M.ring(1, [64, 8, 64], F32, "AkT")
        Rbr = M.ring(1, [64, 8, 64], F32, "RbT")
        Rkr = M.ring(1, [64, 8, 64], F32, "RkT")
        AUr = M.ring(1, [64, 8, 128], F32, "AU")
        RhTr = M.ring(1, [64, 8, 64], F32, "RhT")
        Olr = M.ring(1, [64, 8, 64], F32, "Ol")
        Phr = M.ring(1, [64, 8, 64], F32, "Ph")
        dSr = M.ring(1, [64, 8, 64], F32, "dS")
        o_r = M.ring(2, [64, 8, 64], F32, "o")
        osq = M.tile([64, 8, 64], F32, "osq")
        st1 = M.ring(2, [64, 8], F32, "s1")
        st2 = M.ring(2, [64, 8], F32, "s2")
        st3 = M.ring(2, [64, 8], F32, "s3")
        onr = M.ring(1, [64, 8, 64], F32, "on")
        psn = C.psum.next
        h8 = lambda ap: ap.rearrange("p (h d) -> p h d", h=8)

        def pcopy(dst, src, reads, writes):
            copy_op(C, evac_eng(C), dst, src, reads, writes)

        for bk in range(NBK):
            t0 = bk * W
            if t0 == 0:
                S.op("pool", lambda e: e.memset(ut[:, :, 0:1], 0.0), writes=[ut.b])
                S.dma("sp", lambda e: e.dma_start(out=ut[:, :, 1:W + 1], in_=C.UT[R0:R0 + 1792, 0:W].rearrange("(c p) t -> p c t", p=128)), reads=[C.bUT], writes=[ut.b])
            else:
                S.dma("sp", lambda e, t0=t0: e.dma_start(out=ut[:], in_=C.UT[R0:R0 + 1792, t0 - 1:t0 + W].rearrange("(c p) t -> p c t", p=128)), reads=[C.bUT], writes=[ut.b])
            for c in range(14):
                eng = "dve" if c % 2 == 0 else "pool"
                S.op("dve", lambda e, c=c: e.tensor_tensor(out=dsh[:], in0=ut[:, c, 0:W], in1=ut[:, c, 1:W + 1], op=ALU.subtract), reads=[ut.b], writes=[dsh.b])
                S.op("dve", lambda e, c=c: e.scalar_tensor_tensor(out=us[:, c, :], in0=dsh[:], scalar=mu[:, c:c + 1], in1=ut[:, c, 1:W + 1], op0=ALU.mult, op1=ALU.add),
                     reads=[dsh.b, mu.b, ut.b], writes=[us.b])
            dump(C, "us", us[:], us.b)
            chk(1)
            S.op("act", lambda e: e.activation(out=twad[0:64, :], in_=us[0:64, 12, :], func=AF.Tanh), reads=[us.b], writes=[twad.b])
            S.op("act", lambda e: e.copy(out=twad[64:128, :], in_=us[64:128, 12, :]), reads=[us.b], writes=[twad.b])
            S.op("act", lambda e: e.activation(out=sgd[:], in_=us[:, 13, :], func=AF.Sigmoid), reads=[us.b], writes=[sgd.b])
            for c in range(4):
                csl = slice(c * 128, (c + 1) * 128)
                p1 = psn()
                S.op("pe", lambda e, p1=p1, csl=csl: e.matmul(p1[:, 0:W], lhsT=wa[0:64, csl], rhs=twad[0:64, :], start=True, stop=True), reads=[wa.b, twad.b], writes=[p1.b])
                S.op("act", lambda e, p1=p1, c=c: e.activation(out=lw[:, c, :], in_=p1[:, 0:W], func=AF.Sigmoid, bias=w0[:, c:c + 1]), reads=[p1.b, w0.b], writes=[lw.b])
                p2 = psn()
                S.op("pe", lambda e, p2=p2, csl=csl: e.matmul(p2[:, 0:W], lhsT=wa[64:128, csl], rhs=twad[64:128, :], start=True, stop=True), reads=[wa.b, twad.b], writes=[p2.b])
                S.op("act", lambda e, p2=p2, c=c: e.activation(out=at_[:, c, :], in_=p2[:, 0:W], func=AF.Sigmoid, bias=a0[:, c:c + 1]), reads=[p2.b, a0.b], writes=[at_.b])
                p3 = psn()
                S.op("pe", lambda e, p3=p3, csl=csl: e.matmul(p3[:, 0:W], lhsT=gu[:, csl], rhs=sgd[:], start=True, stop=True), reads=[gu.b, sgd.b], writes=[p3.b])
                S.op("dve", lambda e, p3=p3, c=c: e.tensor_copy(out=gt_[:, c, :], in_=p3[:, 0:W]), reads=[p3.b], writes=[gt_.b])
            chk(2)
            if l == 0:
                S.dma("sp", lambda e, t0=t0: e.dma_start(out=C.VF[:, t0:t0 + W].rearrange("(c p) t -> p c t", p=128), in_=us[:, 8:12, :]), reads=[us.b], writes=[C.bVF])
            else:
                S.dma("sp", lambda e, t0=t0: e.dma_start(out=vf[:], in_=C.VF[:, t0:t0 + W].rearrange("(c p) t -> p c t", p=128)), reads=[C.bVF], writes=[vf.b])
                S.op("act", lambda e: e.copy(out=vb[:], in_=us[:, 8:12, :]), reads=[us.b], writes=[vb.b])
                pv = psn()
                for c in range(4):
                    S.op("pe", lambda e, pv=pv, c=c: e.matmul(pv[0:32, 0:W], lhsT=vdn[:, c, :], rhs=vb[:, c, :], start=(c == 0), stop=(c == 3)), reads=[vdn.b, vb.b], writes=[pv.b])
                S.op("dve", lambda e, pv=pv: e.tensor_copy(out=vd[:], in_=pv[0:32, 0:W]), reads=[pv.b], writes=[vd.b])
                for c in range(4):
                    csl = slice(c * 128, (c + 1) * 128)
                    p4 = psn()
                    S.op("pe", lambda e, p4=p4, csl=csl: e.matmul(p4[:, 0:W], lhsT=vup[0:32, csl], rhs=vd[0:32, :], start=True, stop=True), reads=[vup.b, vd.b], writes=[p4.b])
                    sv = tA.next()
                    S.op("act", lambda e, p4=p4, sv=sv, c=c: e.activation(out=sv[:], in_=p4[:, 0:W], func=AF.Sigmoid, bias=v0[:, c:c + 1]), reads=[p4.b, v0.b], writes=[sv.b])
                    dl = tB.next()
                    S.op("dve", lambda e, dl=dl, c=c: e.tensor_tensor(out=dl[:], in0=vf[:, c, :], in1=us[:, 8 + c, :], op=ALU.subtract), reads=[vf.b, us.b], writes=[dl.b])
                    S.op("dve", lambda e, dl=dl, sv=sv: e.tensor_tensor(out=dl[:], in0=dl[:], in1=sv[:], op=ALU.mult), reads=[dl.b, sv.b], writes=[dl.b])
                    S.op("dve", lambda e, dl=dl, c=c: e.tensor_tensor(out=us[:, 8 + c, :], in0=us[:, 8 + c, :], in1=dl[:], op=ALU.add), reads=[dl.b, us.b], writes=[us.b])
            chk(3)
            for c in range(4):
                r_c = us[:, c, :]
                k_c = us[:, 4 + c, :]
                v_c = us[:, 8 + c, :]
                S.op("dve", lambda e, c=c, k_c=k_c: e.tensor_scalar(out=kkn[:, c, :], in0=k_c, scalar1=kkw[:, c:c + 1], scalar2=None, op0=ALU.mult), reads=[us.b, kkw.b], writes=[kkn.b])
                sq = tbf.next()
                S.op("act", lambda e, sq=sq, c=c: e.activation(out=sq[:], in_=kkn[:, c, :], func=AF.Square), reads=[kkn.b], writes=[sq.b])
                p5 = psn()
                S.op("pe", lambda e, p5=p5, sq=sq: e.matmul(p5[:, 0:W], lhsT=blk[:], rhs=sq[:], start=True, stop=True), reads=[blk.b, sq.b], writes=[p5.b])
                rk = tA.next()
                S.op("dve", lambda e, rk=rk, p5=p5: e.tensor_scalar(out=rk[:], in0=p5[:, 0:W], scalar1=1e-24, scalar2=None, op0=ALU.max), reads=[p5.b], writes=[rk.b])
                S.op("act", lambda e, rk=rk: e.activation(out=rk[:], in_=rk[:], func=AF.Ln), reads=[rk.b], writes=[rk.b])
                S.op("act", lambda e, rk=rk: e.activation(out=rk[:], in_=rk[:], func=AF.Exp, scale=-0.5), reads=[rk.b], writes=[rk.b])
                S.op("dve", lambda e, rk=rk, c=c: e.tensor_tensor(out=kkn[:, c, :], in0=kkn[:, c, :], in1=rk[:], op=ALU.mult), reads=[rk.b, kkn.b], writes=[kkn.b])
                t1 = tB.next()
                S.op("dve", lambda e, t1=t1, c=c: e.tensor_scalar(out=t1[:], in0=at_[:, c, :], scalar1=-1.0, scalar2=ka[:, c:c + 1], op0=ALU.add, op1=ALU.mult), reads=[at_.b, ka.b], writes=[t1.b])
                S.op("dve", lambda e, t1=t1, c=c, k_c=k_c: e.scalar_tensor_tensor(out=kp[:, c, :], in0=t1[:], scalar=1.0, in1=k_c, op0=ALU.add, op1=ALU.mult), reads=[t1.b, us.b], writes=[kp.b])
                t2 = tA.next()
                S.op("dve", lambda e, t2=t2, c=c, r_c=r_c: e.tensor_tensor(out=t2[:], in0=r_c, in1=kp[:, c, :], op=ALU.mult), reads=[us.b, kp.b], writes=[t2.b])
                t2b = tbf.next()
                S.op("dve", lambda e, t2=t2, t2b=t2b, c=c: e.tensor_scalar(out=t2b[:], in0=t2[:], scalar1=rkv[:, c:c + 1], scalar2=None, op0=ALU.mult), reads=[t2.b, rkv.b], writes=[t2b.b])
                p6 = psn()
                S.op("pe", lambda e, p6=p6, t2b=t2b: e.matmul(p6[:, 0:W], lhsT=blk[:], rhs=t2b[:], start=True, stop=True), reads=[blk.b, t2b.b], writes=[p6.b])
                S.op("dve", lambda e, p6=p6, c=c, v_c=v_c: e.tensor_tensor(out=bonus[:, c, :], in0=p6[:, 0:W], in1=v_c, op=ALU.mult), reads=[p6.b, us.b], writes=[bonus.b])
                S.op("dve", lambda e, c=c: e.tensor_scalar(out=lw[:, c, :], in0=lw[:, c, :], scalar1=-0.6065306597126334, scalar2=None, op0=ALU.mult), reads=[lw.b], writes=[lw.b])
                S.op("dve", lambda e, c=c: e.tensor_tensor_scan(out=cs[:, c, :], data0=m0[:], data1=lw[:, c, :], initial=0.0, op0=ALU.mult, op1=ALU.add), reads=[lw.b, m0.b], writes=[cs.b])
                gx = tA.next()
                S.op("act", lambda e, gx=gx, c=c: e.activation(out=gx[:], in_=cs[:, c, :], func=AF.Exp), reads=[cs.b], writes=[gx.b])
                S.op("dve", lambda e, gx=gx, c=c, r_c=r_c: e.tensor_tensor(out=rT[:, c, :], in0=r_c, in1=gx[:], op=ALU.mult), reads=[gx.b, us.b], writes=[rT.b])
                ge = tB.next()
                S.op("dve", lambda e, ge=ge, c=c: e.tensor_tensor(out=ge[:], in0=cs[:, c, :], in1=lw[:, c, :], op=ALU.subtract), reads=[cs.b, lw.b], writes=[ge.b])
                S.op("act", lambda e, ge=ge: e.activation(out=ge[:], in_=ge[:], func=AF.Exp), reads=[ge.b], writes=[ge.b])
                S.op("dve", lambda e, ge=ge, c=c: e.scalar_tensor_tensor(out=aT[:, c, :], in0=kkn[:, c, :], scalar=-1.0, in1=ge[:], op0=ALU.mult, op1=ALU.mult), reads=[ge.b, kkn.b], writes=[aT.b])
                gi = tA.next()
                S.op("act", lambda e, gi=gi, c=c: e.activation(out=gi[:], in_=cs[:, c, :], func=AF.Exp, scale=-1.0), reads=[cs.b], writes=[gi.b])
                kb = tB.next()
                S.op("dve", lambda e, kb=kb, c=c: e.tensor_tensor(out=kb[:], in0=kkn[:, c, :], in1=at_[:, c, :], op=ALU.mult), reads=[kkn.b, at_.b], writes=[kb.b])
                S.op("dve", lambda e, kb=kb, gi=gi, c=c: e.tensor_tensor(out=bT[:, c, :], in0=kb[:], in1=gi[:], op=ALU.mult), reads=[kb.b, gi.b], writes=[bT.b])
                S.op("dve", lambda e, gi=gi, c=c: e.tensor_tensor(out=kT[:, c, :], in0=kp[:, c, :], in1=gi[:], op=ALU.mult), reads=[kp.b, gi.b], writes=[kT.b])
                e2 = tA.next()
                for q in range(NQ):
                    qs = slice(q * L, (q + 1) * L)
                    qe = (q + 1) * L - 1
                    S.op("act", lambda e, e2=e2, c=c, qs=qs, qe=qe: e.activation(out=e2[:, qs], in_=cs[:, c, qs], func=AF.Exp, scale=-1.0, bias=cs[:, c, qe:qe + 1]), reads=[cs.b], writes=[e2.b])
                S.op("dve", lambda e, e2=e2, kb=kb, c=c: e.tensor_tensor(out=Bh[:, c, :], in0=kb[:], in1=e2[:], op=ALU.mult), reads=[kb.b, e2.b], writes=[Bh.b])
                S.op("dve", lambda e, e2=e2, c=c: e.tensor_tensor(out=Kh[:, c, :], in0=kp[:, c, :], in1=e2[:], op=ALU.mult), reads=[kp.b, e2.b], writes=[Kh.b])
                S.op("act", lambda e, c=c: e.activation(out=glt[:, c, :], in_=cs[:, c, :].rearrange("p (q t) -> p q t", t=L)[:, :, L - 1], func=AF.Exp), reads=[cs.b], writes=[glt.b])
                for (nm, src) in (("aT", aT), ("bT", bT), ("kT", kT), ("rT", rT)):
                    hi, lo = hl[nm]
                    S.op("act", lambda e, hi=hi, src=src, c=c: e.copy(out=hi[:, c, :], in_=src[:, c, :]), reads=[src.b], writes=[hi.b])
                    S.op("pool", lambda e, hi=hi, lo=lo, src=src, c=c: e.tensor_tensor(out=lo[:, c, :], in0=src[:, c, :], in1=hi[:, c, :], op=ALU.subtract), reads=[src.b, hi.b], writes=[lo.b])
                for j in range(2):
                    pr_ = psn()
                    S.op("pe", lambda e, pr_=pr_, c=c, j=j: e.matmul(pr_[0:64, 0:W], lhsT=C.identf[:, j * 64:(j + 1) * 64], rhs=rT[:, c, :], start=True, stop=True), reads=[C.identf.b, rT.b], writes=[pr_.b])
                    pcopy(rTl[:, 2 * c + j, :], pr_[0:64, 0:W], [pr_.b], [rTl.b])
            for j in range(2):
                pg_ = psn()
                S.op("pe", lambda e, pg_=pg_, j=j: e.matmul(pg_[0:64, 0:4 * NQ], lhsT=C.identf[:, j * 64:(j + 1) * 64], rhs=glt[:].rearrange("p c q -> p (c q)"), start=True, stop=True), reads=[C.identf.b, glt.b], writes=[pg_.b])
                for c in range(4):
                    pcopy(gltm[:, 2 * c + j, :], pg_[0:64, c * NQ:(c + 1) * NQ], [pg_.b], [gltm.b])
            for nm, tl in (("lw", lw), ("a", at_), ("g", gt_), ("kkn", kkn), ("kp", kp), ("bonus", bonus), ("cs", cs), ("aT", aT), ("bT", bT), ("kT", kT), ("rT", rT), ("Bh", Bh), ("Kh", Kh), ("rTl", rTl), ("gltm", gltm)):
                dump(C, nm, tl[:], tl.b)
            for nm in ("aT", "bT"):
                dump(C, nm + "h", hl[nm][0][:], hl[nm][0].b)
                dump(C, nm + "l", hl[nm][1][:], hl[nm][1].b)
            chk(4)
            for q in range(NQ):
                sl = slice(q * L, (q + 1) * L)
                cat = CAT.next()
                Bt = Btr.next()
                Kt = Ktr.next()
                Vt = Vtr.next()
                for (src, sbuf_, dst, dtl) in ((aT, aT.b, cat[:, :, 0:64], cat), (Bh, Bh.b, Bt[:], Bt), (Kh, Kh.b, Kt[:], Kt), (None, us.b, Vt[:], Vt)):
                    pt = psn()
                    for c in range(4):
                        inp = us[:, 8 + c, sl] if src is None else src[:, c, sl]
                        S.op("pe", lambda e, pt=pt, c=c, inp=inp: e.matmul(pt[0:64, c * 128:(c + 1) * 128], lhsT=inp, rhs=C.identf[:], start=True, stop=True), reads=[sbuf_, C.identf.b], writes=[pt.b])
                    pcopy(dst, h8(pt[0:64, :]), [pt.b], [dtl.b])
                chk(5)
                P = Pr.next()
                PT = PTr.next()
                TT = TTr.next()
                AkT = Akr.next()
                RbT = Rbr.next()
                RkT = Rkr.next()
                for (ln_, rn_, msk, dst) in (("bT", "aT", mUs, PT), ("aT", "bT", mLs, P), ("kT", "aT", mUs, AkT), ("bT", "rT", mUi, RbT), ("kT", "rT", mUi, RkT)):
                    lh, ll = hl[ln_]
                    rh, rl = hl[rn_]
                    dv = dst[:].rearrange("p (c j) d -> p c j d", j=2)
                    for j in range(2):
                        pa = psn()
                        hp = slice(j * 64, j * 64 + 64)
                        for c in range(4):
                            for ii, (x_, y_) in enumerate(((lh, rh), (lh, rl), (ll, rh))):
                                S.op("pe", lambda e, pa=pa, c=c, hp=hp, x_=x_, y_=y_, ii=ii, sl=sl: e.matmul(pa[0:64, c * 64:(c + 1) * 64], lhsT=x_[hp, c, sl], rhs=y_[hp, c, sl], start=(ii == 0), stop=(ii == 2)),
                                     reads=[x_.b, y_.b], writes=[pa.b])
                        S.op("dve", lambda e, pa=pa, msk=msk, dv=dv, j=j: e.tensor_tensor(out=dv[:, :, j, :], in0=pa[0:64, 0:256].rearrange("p (h d) -> p h d", h=4), in1=msk[:, 0:4, :], op=ALU.mult),
                             reads=[pa.b, msk.b], writes=[dst.b])
                S.op("pool", lambda e, TT=TT, PT=PT: e.tensor_tensor(out=TT[:], in0=PT[:], in1=mId[:], op=ALU.add), reads=[PT.b, mId.b], writes=[TT.b])
                for nm, tl in (("P0", P), ("PT0", PT), ("TT0", TT), ("AkT", AkT), ("RbT", RbT), ("RkT", RkT), ("cat0", cat), ("Bt", Bt), ("Kt", Kt), ("Vt", Vt)):
                    dump(C, nm, tl[:], tl.b)
                chk(6)
                for i in range(1, 6):
                    pP = psn()
                    for h in range(8):
                        S.op("pe", lambda e, pP=pP, h=h, P=P, PT=PT: e.matmul(pP[0:64, h * 64:(h + 1) * 64], lhsT=PT[:, h, :], rhs=P[:, h, :], start=True, stop=True), reads=[P.b, PT.b], writes=[pP.b])
                    Pn = Pr.next()
                    pcopy(Pn[:], h8(pP[0:64, :]), [pP.b], [Pn.b])
                    if i < 5:
                        pPT = psn()
                        for h in range(8):
                            S.op("pe", lambda e, pPT=pPT, h=h, P=P, PT=PT: e.matmul(pPT[0:64, h * 64:(h + 1) * 64], lhsT=P[:, h, :], rhs=PT[:, h, :], start=True, stop=True), reads=[P.b, PT.b], writes=[pPT.b])
                        PTn = PTr.next()
                        pcopy(PTn[:], h8(pPT[0:64, :]), [pPT.b], [PTn.b])
                    else:
                        PTn = PT
                    pTT = psn()
                    for h in range(8):
                        S.op("pe", lambda e, pTT=pTT, h=h, Pn=Pn, TT=TT: e.matmul(pTT[0:64, h * 64:(h + 1) * 64], lhsT=Pn[:, h, :], rhs=TT[:, h, :], start=True, stop=True), reads=[Pn.b, TT.b], writes=[pTT.b])
                    TTn = TTr.next()
                    S.op("dve", lambda e, pTT=pTT, TT=TT, TTn=TTn: e.tensor_tensor(out=TTn[:], in0=h8(pTT[0:64, :]), in1=TT[:], op=ALU.add), reads=[pTT.b, TT.b], writes=[TTn.b])
                    P, PT, TT = Pn, PTn, TTn
                chk(7)
                pW = psn()
                for h in range(8):
                    S.op("pe", lambda e, pW=pW, h=h, AkT=AkT, Vt=Vt: e.matmul(pW[0:64, h * 64:(h + 1) * 64], lhsT=AkT[:, h, :], rhs=Vt[:, h, :], start=True, stop=True), reads=[AkT.b, Vt.b], writes=[pW.b])
                pcopy(cat[:, :, 64:128], h8(pW[0:64, :]), [pW.b], [cat.b])
                AU = AUr.next()
                for hb in range(2):
                    pAU = psn()
                    for hh in range(4):
                        h = hb * 4 + hh
                        S.op("pe", lambda e, pAU=pAU, hh=hh, h=h, TT=TT, cat=cat: e.matmul(pAU[0:64, hh * 128:(hh + 1) * 128], lhsT=TT[:, h, :], rhs=cat[:, h, :], start=True, stop=True), reads=[TT.b, cat.b], writes=[pAU.b])
                    pcopy(AU[:, hb * 4:(hb + 1) * 4, :], pAU[0:64, :].rearrange("p (h d) -> p h d", h=4), [pAU.b], [AU.b])
                RhT = RhTr.next()
                Ol = Olr.next()
                Ph = Phr.next()
                dS = dSr.next()
                pR = psn()
                for h in range(8):
                    c = h // 2
                    hp = slice((h % 2) * 64, (h % 2) * 64 + 64)
                    S.op("pe", lambda e, pR=pR, h=h, AU=AU, RbT=RbT: e.matmul(pR[0:64, h * 64:(h + 1) * 64], lhsT=AU[:, h, 0:64], rhs=RbT[:, h, :], start=True, stop=True), reads=[AU.b, RbT.b], writes=[pR.b])
                S.op("dve", lambda e, pR=pR, RhT=RhT, sl=sl: e.tensor_tensor(out=RhT[:], in0=h8(pR[0:64, :]), in1=rTl[:, :, sl], op=ALU.add), reads=[pR.b, rTl.b], writes=[RhT.b])
                pO = psn()
                for h in range(8):
                    S.op("pe", lambda e, pO=pO, h=h, AU=AU, RbT=RbT: e.matmul(pO[0:64, h * 64:(h + 1) * 64], lhsT=RbT[:, h, :], rhs=AU[:, h, 64:128], start=True, stop=False), reads=[AU.b, RbT.b], writes=[pO.b])
                    S.op("pe", lambda e, pO=pO, h=h, RkT=RkT, Vt=Vt: e.matmul(pO[0:64, h * 64:(h + 1) * 64], lhsT=RkT[:, h, :], rhs=Vt[:, h, :], start=False, stop=True), reads=[RkT.b, Vt.b], writes=[pO.b])
                pcopy(Ol[:], h8(pO[0:64, :]), [pO.b], [Ol.b])
                pF = psn()
                for h in range(8):
                    c = h // 2
                    hp = slice((h % 2) * 64, (h % 2) * 64 + 64)
                    S.op("pe", lambda e, pF=pF, h=h, AU=AU, Bt=Bt: e.matmul(pF[0:64, h * 64:(h + 1) * 64], lhsT=AU[:, h, 0:64], rhs=Bt[:, h, :], start=True, stop=True), reads=[AU.b, Bt.b], writes=[pF.b])
                dgq = dgt.next()
                S.op("pool", lambda e, dgq=dgq, q=q: e.tensor_tensor(out=dgq[:], in0=mId[:], in1=bc_last(gltm[:, :, q], [64, 8, 64]), op=ALU.mult), reads=[mId.b, gltm.b], writes=[dgq.b])
                S.op("dve", lambda e, pF=pF, Ph=Ph, dgq=dgq: e.tensor_tensor(out=Ph[:], in0=h8(pF[0:64, :]), in1=dgq[:], op=ALU.add), reads=[pF.b, dgq.b], writes=[Ph.b])
                pD = psn()
                for h in range(8):
                    S.op("pe", lambda e, pD=pD, h=h, AU=AU, Bt=Bt: e.matmul(pD[0:64, h * 64:(h + 1) * 64], lhsT=Bt[:, h, :], rhs=AU[:, h, 64:128], start=True, stop=False), reads=[AU.b, Bt.b], writes=[pD.b])
                    S.op("pe", lambda e, pD=pD, h=h, Kt=Kt, Vt=Vt: e.matmul(pD[0:64, h * 64:(h + 1) * 64], lhsT=Kt[:, h, :], rhs=Vt[:, h, :], start=False, stop=True), reads=[Kt.b, Vt.b], writes=[pD.b])
                pcopy(dS[:], h8(pD[0:64, :]), [pD.b], [dS.b])
                for nm, tl in (("TT", TT), ("AU", AU), ("RhT", RhT), ("Ol", Ol), ("Ph", Ph), ("dS", dS)):
                    dump(C, nm, tl[:], tl.b)
                chk(8)
                pY = psn()
                for h in range(8):
                    S.op("pe", lambda e, pY=pY, h=h, RhT=RhT, sst=sst: e.matmul(pY[0:64, h * 64:(h + 1) * 64], lhsT=RhT[:, h, :], rhs=sst[:, h, :], start=True, stop=True), reads=[RhT.b, sst.b], writes=[pY.b])
                o = o_r.next()
                S.op("dve", lambda e, pY=pY, o=o, Ol=Ol: e.tensor_tensor(out=o[:], in0=h8(pY[0:64, :]), in1=Ol[:], op=ALU.add), reads=[pY.b, Ol.b], writes=[o.b])
                pS = psn()
                for h in range(8):
                    S.op("pe", lambda e, pS=pS, h=h, Ph=Ph, sst=sst: e.matmul(pS[0:64, h * 64:(h + 1) * 64], lhsT=Ph[:, h, :], rhs=sst[:, h, :], start=True, stop=True), reads=[Ph.b, sst.b], writes=[pS.b])
                nsst = Sst.next()
                S.op("dve", lambda e, pS=pS, nsst=nsst, dS=dS: e.tensor_tensor(out=nsst[:], in0=h8(pS[0:64, :]), in1=dS[:], op=ALU.add), reads=[pS.b, dS.b], writes=[nsst.b])
                sst = nsst
                dump(C, "o", o[:], o.b)
                dump(C, "S1", sst[:], sst.b)
                chk(9)
                s1 = st1.next()
                s2 = st2.next()
                s3 = st3.next()
                S.op("dve", lambda e, s1=s1, o=o: e.tensor_reduce(out=s1[:], in_=o[:], axis=AX.X, op=ALU.add), reads=[o.b], writes=[s1.b])
                S.op("act", lambda e, o=o: e.activation(out=osq[:], in_=o[:], func=AF.Square), reads=[o.b], writes=[osq.b])
                S.op("dve", lambda e, s2=s2: e.tensor_reduce(out=s2[:], in_=osq[:], axis=AX.X, op=ALU.add), reads=[osq.b], writes=[s2.b])
                S.op("dve", lambda e, s1=s1: e.tensor_scalar(out=s1[:], in0=s1[:], scalar1=1.0 / 64, scalar2=None, op0=ALU.mult), reads=[s1.b], writes=[s1.b])
                S.op("dve", lambda e, s1=s1, s3=s3: e.tensor_tensor(out=s3[:], in0=s1[:], in1=s1[:], op=ALU.mult), reads=[s1.b], writes=[s3.b])
                S.op("dve", lambda e, s2=s2, s3=s3: e.scalar_tensor_tensor(out=s2[:], in0=s2[:], scalar=1.0 / 64, in1=s3[:], op0=ALU.mult, op1=ALU.subtract), reads=[s2.b, s3.b], writes=[s2.b])
                S.op("act", lambda e, s2=s2: e.activation(out=s2[:], in_=s2[:], func=AF.Ln, bias=64e-5), reads=[s2.b], writes=[s2.b])
                S.op("act", lambda e, s2=s2: e.activation(out=s2[:], in_=s2[:], func=AF.Exp, scale=-0.5), reads=[s2.b], writes=[s2.b])
                on = onr.next()
                S.op("dve", lambda e, on=on, o=o, s1=s1: e.tensor_tensor(out=on[:], in0=o[:], in1=bc_last(s1[:, 0:8], [64, 8, 64]), op=ALU.subtract), reads=[o.b, s1.b], writes=[on.b])
                S.op("dve", lambda e, on=on, s2=s2: e.tensor_tensor(out=on[:], in0=on[:], in1=bc_last(s2[:, 0:8], [64, 8, 64]), op=ALU.mult), reads=[on.b, s2.b], writes=[on.b])
                chk(10)
                pyt = psn()
                for c in range(4):
                    S.op("pe", lambda e, pyt=pyt, c=c, on=on: e.matmul(pyt[:, c * 64:(c + 1) * 64], lhsT=on[:, 2 * c:2 * c + 2, :].rearrange("p h d -> p (h d)"), rhs=C.identf[0:64, 0:64], start=True, stop=True),
                         reads=[on.b, C.identf.b], writes=[pyt.b])
                pcopy(yT[:, :, sl], pyt[:, 0:256].rearrange("p (c t) -> p c t", c=4), [pyt.b], [yT.b])
            for c in range(4):
                S.op("dve", lambda e, c=c: e.tensor_scalar(out=yT[:, c, :], in0=yT[:, c, :], scalar1=lnw[:, c:c + 1], scalar2=lnb[:, c:c + 1], op0=ALU.mult, op1=ALU.add),
                     reads=[yT.b, lnw.b, lnb.b], writes=[yT.b])
            S.op("pool", lambda e: e.tensor_tensor(out=yT[:], in0=yT[:], in1=bonus[:], op=ALU.add), reads=[yT.b, bonus.b], writes=[yT.b])
            S.op("dve", lambda e: e.tensor_tensor(out=ymt[:], in0=yT[:], in1=gt_[:], op=ALU.mult), reads=[yT.b, gt_.b], writes=[ymt.b])
            S.dma("sp", lambda e, t0=t0: e.dma_start(out=C.YMT[1536:2048, t0:t0 + W].rearrange("(c p) t -> p c t", p=128), in_=ymt[:]), reads=[ymt.b], writes=[C.bYMT])


_NC_CACHE = {}


def kernel(**inputs):
    x = np.asarray(inputs["x"], dtype=np.float32)
    mem = np.asarray(inputs["mem"], dtype=np.float32)
    B = x.shape[0]
    n = 8
    if "nc" not in _NC_CACHE:
        _NC_CACHE["nc"] = build_program()
    nc = _NC_CACHE["nc"]
    in_maps = []
    for c in range(n):
        b = c % B
        m = {"x": np.ascontiguousarray(x[b]), "mem": np.ascontiguousarray(mem[b])}
        for k in WEIGHT_NAMES:
            m[k] = np.asarray(inputs[k], dtype=np.float32)
        in_maps.append(m)
    res = run_bass_kernel_spmd(nc, in_maps, core_ids=list(range(n)))
    out = np.stack([res.results[b]["out"] for b in range(B)], axis=0)
    return out.astype(np.float32)
```
